# Optimizing a Trainium2 kernel written in Bass

```python
import functools
import jax, jax.numpy as jnp
from jax import lax
import numpy as np

D_MODEL = 2048
BATCH = 1
SEQ = 16384
DEPTH = 1
DEC_BATCH = 32
DEC_SEQ = 1
PAST_LEN = 16384
PAGE_SIZE = 128

ATT_HEADS = 8
HEAD_DIM = 128
ATT_WIDTH = ATT_HEADS * HEAD_DIM
DILATED_PATTERNS = ((128, 1), (512, 4), (2048, 16))
WINDOW = max(w for w, _ in DILATED_PATTERNS)
Q_BLOCK = 128
HG_HEADS = 8
HG_DK = 128
HG_DV = (D_MODEL - ATT_WIDTH) // HG_HEADS
HG_KWIDTH = HG_HEADS * HG_DK
HG_VWIDTH = HG_HEADS * HG_DV
HG_CHUNK = 64
MIX_WIDTH = ATT_WIDTH + HG_VWIDTH
IN_WIDTH = 3 * ATT_WIDTH + 2 * HG_KWIDTH + 2 * HG_VWIDTH
IN_SPLITS = (ATT_WIDTH, 2 * ATT_WIDTH, 3 * ATT_WIDTH,
             3 * ATT_WIDTH + HG_KWIDTH, 3 * ATT_WIDTH + 2 * HG_KWIDTH,
             3 * ATT_WIDTH + 2 * HG_KWIDTH + HG_VWIDTH)
D_FF = -(-(8 * D_MODEL) // (3 * 256)) * 256
EPS = 1e-6

kernel_name = 'hymba_dilated_attn_hgrn2_decode_step'


def rmsnorm(x, g):
    xf = x.astype(jnp.float32)
    y = xf * lax.rsqrt(jnp.mean(xf * xf, axis=-1, keepdims=True) + EPS)
    return (y * g.astype(jnp.float32)).astype(x.dtype)


def dilated_attention(q, k_all, v_all, q_idx):
    scale = HEAD_DIM ** -0.5
    lses, outs = [], []
    for window, dil in DILATED_PATTERNS:
        n_keys = window // dil + 1
        idx = q_idx[:, None] - dil * jnp.arange(n_keys, dtype=jnp.int32)[None, :]
        valid = idx >= 0
        idx_c = jnp.maximum(idx, 0)
        kg = jnp.take(k_all, idx_c, axis=1)
        vg = jnp.take(v_all, idx_c, axis=1)
        s = jnp.einsum('bqhd,bqjhd->bhqj', q, kg).astype(jnp.float32) * scale
        s = jnp.where(valid[None, None], s, -jnp.inf)
        lse = jax.nn.logsumexp(s, axis=-1)
        p = jnp.exp(s - lse[..., None])
        o = jnp.einsum('bhqj,bqjhd->bqhd', p.astype(v_all.dtype), vg)
        lses.append(lse)
        outs.append(o.astype(jnp.float32))
    w = jax.nn.softmax(jnp.stack(lses, 0), axis=0)
    w = jnp.transpose(w, (0, 1, 3, 2))[..., None]
    return jnp.sum(w * jnp.stack(outs, 0), axis=0).astype(q.dtype)


def prompt_attend(q, k, v):
    b, t, h, dh = q.shape
    nb = t // Q_BLOCK
    qb = jnp.transpose(q.reshape(b, nb, Q_BLOCK, h, dh), (1, 0, 2, 3, 4))
    pos = jnp.arange(t, dtype=jnp.int32).reshape(nb, Q_BLOCK)
    o = lax.map(lambda a: dilated_attention(a[0], k, v, a[1]), (qb, pos))
    o = jnp.transpose(o, (1, 0, 2, 3, 4)).reshape(b, t, h, dh)
    keep = min(WINDOW, t)
    return o, (k[:, t - keep:], v[:, t - keep:])


def sample_attend(q, k, v, cache_k, cache_v):
    past = cache_k.shape[1]
    k_all = jnp.concatenate([cache_k.astype(k.dtype), k], axis=1)
    v_all = jnp.concatenate([cache_v.astype(v.dtype), v], axis=1)
    q_idx = past + jnp.arange(q.shape[1], dtype=jnp.int32)
    return dilated_attention(q, k_all, v_all, q_idx), (k, v)


def hgrn_chunked(q, k, v, logf, s0, chunk):
    b_, t = q.shape[:2]
    n = t // chunk

    def to_chunks(a):
        return jnp.moveaxis(a.astype(jnp.float32).reshape(b_, n, chunk, *a.shape[2:]), 1, 0)

    causal = jnp.tril(jnp.ones((chunk, chunk), dtype=bool))[None, :, :, None, None]

    def step(s, xs):
        qc, kc, vc, gc = xs
        cum = jnp.cumsum(gc, axis=1)
        o_inter = jnp.einsum('bthk,bhkv->bthv', qc * jnp.exp(cum), s)
        decay = jnp.exp(jnp.where(causal, cum[:, :, None] - cum[:, None, :], -jnp.inf))
        a = jnp.einsum('bthk,bshk,btshk->bhts', qc, kc, decay)
        o_intra = jnp.einsum('bhts,bshv->bthv', a, vc)
        last = cum[:, -1]
        k_dec = kc * jnp.exp(last[:, None] - cum)
        s_new = jnp.exp(last)[..., None] * s + jnp.einsum('bshk,bshv->bhkv', k_dec, vc)
        return s_new, o_inter + o_intra

    s_fin, o = lax.scan(step, s0.astype(jnp.float32), (to_chunks(q), to_chunks(k), to_chunks(v), to_chunks(logf)))
    o = jnp.moveaxis(o, 0, 1).reshape(b_, t, *v.shape[2:])
    return o, s_fin


def decoder_layer(x, lb, n_pre_mix, w_in, attn_gain, hg_gain, w_out, n_post_mix,
                  n_pre_ffn, w_gate, w_up, w_down, n_post_ffn, attend, recur):
    b, t, _ = x.shape
    h = rmsnorm(x, n_pre_mix)
    proj = jnp.einsum('btd,de->bte', h, w_in)
    q_a, k_a, v_a, q_h, f_h, i_h, g_h = jnp.split(proj, IN_SPLITS, axis=-1)
    heads = lambda a, nh: a.reshape(b, t, nh, -1)
    o_att, kv_new = attend(heads(q_a, ATT_HEADS), heads(k_a, ATT_HEADS), heads(v_a, ATT_HEADS))
    f = lb + (1.0 - lb) * jax.nn.sigmoid(heads(f_h, HG_HEADS).astype(jnp.float32))
    qh = jax.nn.silu(heads(q_h, HG_HEADS).astype(jnp.float32))
    o_hg, s_new = recur(qh, 1.0 - f, heads(i_h, HG_HEADS).astype(jnp.float32), jnp.log(f))
    o_att = rmsnorm(o_att.reshape(b, t, ATT_WIDTH), attn_gain)
    o_hg = rmsnorm(o_hg, hg_gain) * jax.nn.silu(heads(g_h, HG_HEADS).astype(jnp.float32))
    merged = jnp.concatenate([o_att.astype(x.dtype), o_hg.reshape(b, t, HG_VWIDTH).astype(x.dtype)], axis=-1)
    mix = jnp.einsum('bte,ed->btd', merged, w_out)
    x = x + rmsnorm(mix, n_post_mix)
    h = rmsnorm(x, n_pre_ffn)
    ff = jax.nn.silu(jnp.einsum('btd,df->btf', h, w_gate)) * jnp.einsum('btd,df->btf', h, w_up)
    ff = jnp.einsum('btf,fd->btd', ff, w_down)
    x = x + rmsnorm(ff, n_post_ffn)
    return x, kv_new, s_new


def setup_inputs(seed: int = 0) -> dict:
    key = jax.random.key(seed)
    ks = jax.random.split(key, 20)
    f32 = jnp.float32
    nrm = lambda k, shape, s: jax.random.normal(k, shape, f32) * s
    gain = lambda k, shape: 1.0 + 0.1 * jax.random.normal(k, shape, f32)
    past_rows = min(WINDOW, PAST_LEN)
    return {
        'x_prompt': nrm(ks[0], (BATCH, SEQ, D_MODEL), 1.0),
        'x_sample': nrm(ks[1], (DEC_BATCH, DEC_SEQ, D_MODEL), 1.0),
        'cache_win_k': nrm(ks[2], (DEPTH, DEC_BATCH, past_rows, ATT_HEADS, HEAD_DIM), 1.0),
        'cache_win_v': nrm(ks[3], (DEPTH, DEC_BATCH, past_rows, ATT_HEADS, HEAD_DIM), 1.0),
        'state_hgrn': nrm(ks[4], (DEPTH, DEC_BATCH, HG_HEADS, HG_DK, HG_DV), 0.3),
        'norm_pre_mix': gain(ks[5], (DEPTH, D_MODEL)),
        'w_in': nrm(ks[6], (DEPTH, D_MODEL, IN_WIDTH), D_MODEL ** -0.5),
        'hg_lb_logits': nrm(ks[7], (DEPTH + 1, HG_KWIDTH), 0.5),
        'attn_out_gain': gain(ks[8], (DEPTH, ATT_WIDTH)),
        'hg_norm_gain': gain(ks[9], (DEPTH, HG_DV)),
        'w_out': nrm(ks[10], (DEPTH, MIX_WIDTH, D_MODEL), MIX_WIDTH ** -0.5),
        'norm_post_mix': gain(ks[11], (DEPTH, D_MODEL)),
        'norm_pre_ffn': gain(ks[12], (DEPTH, D_MODEL)),
        'w_gate': nrm(ks[13], (DEPTH, D_MODEL, D_FF), D_MODEL ** -0.5),
        'w_up': nrm(ks[14], (DEPTH, D_MODEL, D_FF), D_MODEL ** -0.5),
        'w_down': nrm(ks[15], (DEPTH, D_FF, D_MODEL), D_FF ** -0.5),
        'norm_post_ffn': gain(ks[16], (DEPTH, D_MODEL)),
    }


def reference(x_prompt, x_sample, cache_win_k, cache_win_v, state_hgrn, norm_pre_mix, w_in,
              hg_lb_logits, attn_out_gain, hg_norm_gain, w_out, norm_post_mix, norm_pre_ffn,
              w_gate, w_up, w_down, norm_post_ffn):
    lb_all = jnp.cumsum(jax.nn.softmax(hg_lb_logits.astype(jnp.float32), axis=0), axis=0)
    yp, ys = x_prompt, x_sample
    kp, vp, sp, ksm, vsm, ssm = [], [], [], [], [], []
    for l in range(DEPTH):
        lb = lb_all[l].reshape(HG_HEADS, HG_DK)
        params = (norm_pre_mix[l], w_in[l], attn_out_gain[l], hg_norm_gain[l], w_out[l],
                  norm_post_mix[l], norm_pre_ffn[l], w_gate[l], w_up[l], w_down[l], norm_post_ffn[l])
        s0 = jnp.zeros((x_prompt.shape[0], HG_HEADS, HG_DK, HG_DV), jnp.float32)
        yp, (k1, v1), s1 = decoder_layer(
            yp, lb, *params, attend=prompt_attend,
            recur=functools.partial(hgrn_chunked, s0=s0, chunk=HG_CHUNK))
        ys, (k2, v2), s2 = decoder_layer(
            ys, lb, *params,
            attend=functools.partial(sample_attend, cache_k=cache_win_k[l], cache_v=cache_win_v[l]),
            recur=functools.partial(hgrn_chunked, s0=state_hgrn[l], chunk=x_sample.shape[1]))
        kp.append(k1); vp.append(v1); sp.append(s1)
        ksm.append(k2); vsm.append(v2); ssm.append(s2)
    return (yp, ys, jnp.stack(kp), jnp.stack(vp), jnp.stack(sp), jnp.stack(ksm), jnp.stack(vsm), jnp.stack(ssm))
```

```python
import contextlib
import numpy as np
import concourse.bass as bass
import concourse.mybir as mybir
from concourse.bass_utils import run_bass_kernel_spmd

F32 = mybir.dt.float32
BF16 = mybir.dt.bfloat16
AF = mybir.ActivationFunctionType
ALU = mybir.AluOpType
AX = mybir.AxisListType

D = 2048
OWN = 2048
HALO = 2048
G = 512
NCORE = 8
INW = 7168
DFF = 5632
NS = 4
EPS = 1e-6
SCALE = 128 ** -0.5
ENGS = ("pe", "act", "dve", "pool", "sp")


class Buf:
    __slots__ = ("name", "w", "r", "dkey")

    def __init__(self, name):
        self.name = name
        self.w = None
        self.r = []
        self.dkey = None


class Prog:
    def __init__(self, nc):
        self.nc = nc
        self.ops = {e: [] for e in ENGS}
        self.cnt = {}
        self.waited = {e: {} for e in ENGS}
        self.dma_keys = []
        self.nbuf = 0
        self.pending = []

    def buf(self, name=None):
        self.nbuf += 1
        return Buf(name or f"b{self.nbuf}")

    @staticmethod
    def _flat(seq):
        out = []
        for b in seq:
            if isinstance(b, (list, tuple)):
                out.extend(Prog._flat(b))
            else:
                out.append(b)
        return out

    def defer(self, thunk):
        self.pending.append(thunk)

    def flush(self):
        p, self.pending = self.pending, []
        for t in p:
            t()

    def _deps(self, eng, reads, writes, nowaw=False):
        need = {}

        def add(tok):
            if tok is None:
                return
            k, v = tok
            if k == eng and eng == "pe":
                return
            if need.get(k, 0) < v:
                need[k] = v
        for b in reads:
            add(b.w)
        for b in writes:
            if not nowaw:
                add(b.w)
            for t in b.r:
                add(t)
        waits = []
        wd = self.waited[eng]
        for k, v in need.items():
            if wd.get(k, 0) < v:
                wd[k] = v
                waits.append((k, v))
        return waits

    def _commit(self, tok, reads, writes):
        for b in reads:
            b.r.append(tok)
            if len(b.r) > 64:
                mx = {}
                for k, v in b.r:
                    if mx.get(k, 0) < v:
                        mx[k] = v
                b.r = list(mx.items())
        for b in writes:
            b.w = tok
            b.r = []

    def op(self, eng, fn, reads=(), writes=()):
        reads, writes = self._flat(reads), self._flat(writes)
        waits = self._deps(eng, reads, writes)
        self.cnt[eng] = self.cnt.get(eng, 0) + 1
        tok = (eng, self.cnt[eng])
        self.ops[eng].append((waits, fn, (eng, 1)))
        self._commit(tok, reads, writes)
        return tok

    def dma(self, queue, fn, reads=(), writes=(), sembuf=None, nowaw=False):
        reads, writes = self._flat(reads), self._flat(writes)
        sb = sembuf or (writes[0] if writes else reads[0])
        if sb.dkey is None:
            sb.dkey = f"d{len(self.dma_keys)}"
            self.dma_keys.append(sb.dkey)
        key = sb.dkey
        waits = self._deps(queue, reads, writes, nowaw=nowaw)
        self.cnt[key] = self.cnt.get(key, 0) + 16
        tok = (key, self.cnt[key])
        self.ops[queue].append((waits, fn, (key, 16)))
        self._commit(tok, reads, writes)
        return tok

    def handoff(self, old, new):
        toks = []
        old, new = self._flat(old), self._flat(new)
        for b in old:
            if b.w is not None:
                toks.append(b.w)
            toks.extend(b.r)
        for b in new:
            b.r.extend(toks)

    def barrier(self):
        self.flush()
        for e in ENGS:
            waits = []
            wd = self.waited[e]
            for k, v in self.cnt.items():
                if k == e:
                    continue
                if wd.get(k, 0) < v:
                    wd[k] = v
                    waits.append((k, v))
            if waits:
                self.ops[e].append((waits, None, None))

    def emit(self, stack):
        nc = self.nc
        sems = {}
        for k in list(ENGS) + self.dma_keys:
            if k in self.cnt:
                sems[k] = stack.enter_context(nc.semaphore(f"s_{k}"))
        block = stack.enter_context(nc.Block())

        def run(engname):
            def body(e):
                for waits, fn, inc in self.ops[engname]:
                    for k, v in waits:
                        e.wait_ge(sems[k], v)
                    if fn is not None:
                        ins = fn(e)
                        ins.then_inc(sems[inc[0]], inc[1])
            return body

        block.tensor(run("pe"))
        block.scalar(run("act"))
        block.vector(run("dve"))
        block.gpsimd(run("pool"))
        block.sync(run("sp"))


class WQ:
    def __init__(self, nslots_fn):
        self.jobs = []
        self.tiles = []
        self.issued = 0
        self.k = 0
        self.nslots = nslots_fn

    def add(self, thunk):
        self.jobs.append(thunk)

    def prime(self):
        S = self.nslots()
        while self.issued < len(self.jobs) and self.issued <= self.k + S - 1:
            self.tiles.append(self.jobs[self.issued]())
            self.issued += 1

    def get(self):
        self.prime()
        t = self.tiles[self.k]
        self.tiles[self.k] = None
        self.k += 1
        return t


def mms(specs):
    def f(e):
        ins = None
        for (o, l, r, s, t) in specs:
            ins = e.matmul(o, lhsT=l, rhs=r, start=s, stop=t)
        return ins
    return f


def trs(specs):
    def f(e):
        ins = None
        for (o, i, idt) in specs:
            ins = e.transpose(out=o, in_=i, identity=idt)
        return ins
    return f


def act(out, in_, func, scale=1.0, bias=0.0, accum=None):
    if accum is None:
        return lambda e: e.activation(out=out, in_=in_, func=func, bias=bias, scale=scale)
    return lambda e: e.activation(out=out, in_=in_, func=func, bias=bias, scale=scale, accum_out=accum)


def tt(out, a, b, op):
    return lambda e: e.tensor_tensor(out=out, in0=a, in1=b, op=op)


def ts(out, a, s1, op0, s2=None, op1=None):
    if op1 is None:
        return lambda e: e.tensor_scalar(out=out, in0=a, scalar1=s1, scalar2=None, op0=op0)
    return lambda e: e.tensor_scalar(out=out, in0=a, scalar1=s1, scalar2=s2, op0=op0, op1=op1)


def stt(out, a, s, b, op0, op1):
    return lambda e: e.scalar_tensor_tensor(out=out, in0=a, scalar=s, in1=b, op0=op0, op1=op1)


def cp(out, in_):
    return lambda e: e.tensor_copy(out=out, in_=in_)


def acp(out, in_):
    return lambda e: e.copy(out=out, in_=in_)


def dm(out, in_):
    return lambda e: e.dma_start(out=out, in_=in_)


def dmnc(out, in_):
    return lambda e: e.dma_start(out=out, in_=in_, allow_slow_non_contiguous=True)


def build():
    nc = bass.Bass("TRN2", target_bir_lowering=False)

    def din(name, shape, dt=F32):
        return nc.dram_tensor(name, list(shape), dt, kind="ExternalInput").ap()

    def dout(name, shape, dt=F32):
        return nc.dram_tensor(name, list(shape), dt, kind="ExternalOutput").ap()

    def dscr(name, shape, dt):
        return nc.dram_tensor(name, list(shape), dt, kind="Internal").ap()

    xh = din("xh", [HALO + OWN, D])
    xs = din("xs", [NS, D])
    ck = din("ck", [NS, 2048, 1024])
    cv = din("cv", [NS, 2048, 1024])
    st_in = din("st_in", [NS, 8, 128, 128])
    w_in = din("w_in", [D, INW])
    w_out = din("w_out", [D, D])
    w_gate = din("w_gate", [D, DFF])
    w_up = din("w_up", [D, DFF])
    w_down = din("w_down", [DFF, D])
    g1T_d = din("g1T", [128, 16])
    g3T_d = din("g3T", [128, 16])
    g2_d = din("g2", [1, D])
    g4_d = din("g4", [1, D])
    ag_d = din("ag", [1, 1024])
    hg_d = din("hg", [1, 128])
    lbl_d = din("lbl", [128, 2, 8])
    ident_d = din("ident", [128, 128])
    mcur_d = din("mcur", [128, 128])
    mprev_d = din("mprev", [128, 128])
    bdm_d = din("bdm", [128, 128])
    oneh_d = din("oneh", [128, NS, NS])
    hv_d = din("hv", [128, 1])

    y = dout("y", [OWN, D])
    ys = dout("ys", [NS, D])
    wk = dout("wk", [OWN, 1024])
    wv = dout("wv", [OWN, 1024])
    sp_out = dout("sp_out", [8, 128, 128])
    ks_o = dout("ks_o", [NS, 1024])
    vs_o = dout("vs_o", [NS, 1024])
    ss_o = dout("ss_o", [NS, 8, 128, 128])

    qs = dscr("qs", [OWN + 16, 1024], BF16)
    kscr = dscr("kscr", [HALO + OWN + 16, 1024], BF16)
    vscr = dscr("vscr", [HALO + OWN + 16, 1040], BF16)
    rs = dscr("rs", [3, OWN + 16, 1040], F32)
    mhg = dscr("mhg", [128, 8, OWN], BF16)
    sproj = dscr("sproj", [NS, INW], F32)
    x1scr = dscr("x1scr", [OWN, D], F32)

    st = contextlib.ExitStack()
    P = Prog(nc)

    def sb(name, shape, dt):
        t = st.enter_context(nc.sbuf_tensor("sb_" + name, list(shape), dt))
        return t, P.buf(name)

    ps_all = st.enter_context(nc.psum_tensor("ps_all", [128, 8, 512], F32))
    PB = [[P.buf(f"pb{i}")] * 4 for i in range(8)]

    def pbf(i):
        return ps_all[:, i, :]

    def pbb(i):
        return ps_all[:, i, :].bitcast(BF16)

    CQ = P.buf("constq")
    ident_f, Bidf = sb("ident_f", [128, 128], F32)
    ident, Bid = sb("ident", [128, 128], BF16)
    mcur, Bmc = sb("mcur", [128, 128], F32)
    mprev, Bmp = sb("mprev", [128, 128], F32)
    bdm, Bbdm = sb("bdm", [128, 128], F32)
    oneh, Boh = sb("oneh", [128, NS, NS], F32)
    hv, Bhv = sb("hv", [128, 1], F32)
    g1T, Bg1 = sb("g1T", [128, 16], F32)
    g3T, Bg3 = sb("g3T", [128, 16], F32)
    agrow, Bag = sb("agrow", [128, 1024], F32)
    hgrow, Bhg = sb("hgrow", [128, 128], F32)
    lbl, Blbl = sb("lbl", [128, 2, 8], F32)
    lb, Blb = sb("lb", [128, 8], F32)
    oml, Boml = sb("oml", [128, 8], F32)
    maskA, BmA = sb("maskA", [128, 4, 256], BF16)
    maskB, BmB = sb("maskB", [128, 4, 256], BF16)
    rmask, Brm = sb("rmask", [128, G], F32)

    for (t, B, src) in ((ident_f, Bidf, ident_d), (mcur, Bmc, mcur_d), (mprev, Bmp, mprev_d),
                        (bdm, Bbdm, bdm_d), (oneh, Boh, oneh_d), (hv, Bhv, hv_d), (g1T, Bg1, g1T_d),
                        (g3T, Bg3, g3T_d), (lbl, Blbl, lbl_d)):
        P.dma("sp", dm(t[:], src), writes=[B], sembuf=CQ)
    P.dma("sp", dm(agrow[:], ag_d[0].partition_broadcast(128)), writes=[Bag], sembuf=CQ)
    P.dma("sp", dm(hgrow[:], hg_d[0].partition_broadcast(128)), writes=[Bhg], sembuf=CQ)
    _tot = (CQ.dkey, P.cnt[CQ.dkey])
    for B in (Bidf, Bmc, Bmp, Bbdm, Boh, Bhv, Bg1, Bg3, Blbl, Bag, Bhg):
        B.w = _tot
    P.op("dve", cp(ident[:], ident_f[:]), reads=[Bidf], writes=[Bid])
    P.op("dve", cp(maskA[:, :, 0:128], mcur[:].unsqueeze(1).to_broadcast([128, 4, 128])), reads=[Bmc], writes=[BmA])
    P.op("dve", cp(maskA[:, :, 128:256], mprev[:].unsqueeze(1).to_broadcast([128, 4, 128])), reads=[Bmp, BmA], writes=[BmA])
    P.op("dve", cp(maskB[:, :, 0:128], mcur[:].unsqueeze(1).to_broadcast([128, 4, 128])), reads=[Bmc], writes=[BmB])
    P.op("dve", ts(maskB[:, :, 128:256], mprev[:].unsqueeze(1).to_broadcast([128, 4, 128]), hv[:, 0:1], ALU.mult),
         reads=[Bmp, Bhv, BmB], writes=[BmB])
    mbA, BmbA = sb("mbA", [128, 2, 256], BF16)
    mbB, BmbB = sb("mbB", [128, 2, 256], BF16)
    for (mb_, Bmb_, msrc, Bmsrc) in ((mbA, BmbA, maskA, BmA), (mbB, BmbB, maskB, BmB)):
        P.op("dve", ts(mb_[:], msrc[:, 0:2, :], -1.0, ALU.add, 30000.0, ALU.mult), reads=[Bmsrc], writes=[Bmb_])
    P.op("pool", lambda e: e.memset(rmask[:], 1.0), writes=[Brm])
    P.op("pool", lambda e: e.memset(rmask[:].rearrange("p (c t) -> p c t", t=64)[:, :, 0:1], 0.0), reads=[Brm], writes=[Brm])
    P.op("dve", tt(lb[:], lbl[:, 0, :], lbl[:, 1, :], ALU.subtract), reads=[Blbl], writes=[Blb])
    P.op("act", act(lb[:], lb[:], AF.Sigmoid), reads=[Blb], writes=[Blb])
    P.op("dve", ts(oml[:], lb[:], -1.0, ALU.mult, 1.0, ALU.add), reads=[Blb], writes=[Boml])

    wbuf = []
    for i in range(2):
        t, B = sb(f"wbuf{i}", [128, 16, 512], BF16)
        wbuf.append((t, B))
    wctr = [0]

    def wload(src_ap, nchunk=16):
        t, B = wbuf[wctr[0] % len(wbuf)]
        wctr[0] += 1
        P.dma("pool", dm(t[:, 0:nchunk, :], src_ap.rearrange("(c p) n -> p c n", p=128)), writes=[B])
        return t, B

    def wstream(srcs, nchunk=16):
        S = len(wbuf)
        issued = 0
        tiles = []
        for k in range(len(srcs)):
            while issued < len(srcs) and issued <= k + S - 1:
                tiles.append(wload(srcs[issued], nchunk))
                issued += 1
            yield k, tiles[k]

    R1, BR1 = sb("R1", [128, 11264], F32)
    R2, BR2 = sb("R2", [128, 8192], F32)
    R3, BR3 = sb("R3", [128, 6400], F32)
    R4, BR4 = sb("R4", [128, 7424], F32)

    def view(reg, off_b, shape, dt):
        n = int(np.prod(shape[1:]))
        esz = 4 if dt == F32 else 2
        assert off_b % 4 == 0
        nf = (n * esz + 3) // 4
        ap = reg[:, off_b // 4: off_b // 4 + nf]
        if dt != F32:
            ap = ap.bitcast(dt)
        if len(shape) == 3:
            ap = ap.rearrange("p (a b) -> p a b", b=shape[2])
        return ap

    stat, Bstat = sb("stat", [128, 16], F32)
    xt = view(R4, 0, [128, D], F32); Bxt = P.buf("xt")
    xtb_t, Bxtb = sb("xtb", [128, D], F32)
    xts = [(xt, Bxt), (xtb_t[:], Bxtb)]
    xtctr = [0]
    xn0, Bxn0 = sb("xn", [128, D], BF16)
    xn1 = view(R4, 24576, [128, D], BF16); Bxn1 = P.buf("xn1")
    xn, Bxn = xn0, Bxn0
    xnctr = [0]
    hT = view(R4, 8192, [128, 16, G], BF16); BhT = P.buf("hT")
    hTs, BhTs = sb("hTs", [128, 16, NS], BF16)
    stg32 = [sb(f"stg32_{i}", [128, 512], F32) for i in range(2)]
    stg16 = [sb(f"stg16_{i}", [128, 512], BF16) for i in range(3)]
    vstg = [sb(f"vstg_{i}", [128, 4, 130], BF16) for i in range(2)]
    sstg = [sb(f"sstg_{i}", [NS, 512], F32) for i in range(1)]
    for (t, B) in vstg:
        P.op("pool", (lambda t: lambda e: e.memset(t[:], 1.0))(t), writes=[B])
    ctr = {"s32": 0, "s16": 0, "v": 0, "ss": 0, "pb": 0}

    def nxt(lst, key):
        r = lst[ctr[key] % len(lst)]
        ctr[key] += 1
        return r

    p1flag = [False]
    Bstn = [P.buf("stn0"), P.buf("stn1")]

    def sigmoid_to(dst, Bdst, src, Bsrc):
        P.op("act", act(dst, src, AF.Exp, scale=-1.0), reads=[Bsrc], writes=[Bdst])
        P.op("pool", ts(dst, dst, 1.0, ALU.add, 1.0, ALU.mult), reads=[Bdst], writes=[Bdst])
        P.op("dve", (lambda d_: lambda e: e.reciprocal(out=d_, in_=d_))(dst), reads=[Bdst], writes=[Bdst])

    def silu_to(dst, Bdst, src, Bsrc, tmp, Btmp_):
        P.op("act", act(tmp, src, AF.Exp, scale=-1.0), reads=[Bsrc], writes=[Btmp_])
        P.op("act", acp(dst, src), reads=[Bsrc], writes=[Bdst])
        P.op("pool", ts(tmp, tmp, 1.0, ALU.add, 1.0, ALU.mult), reads=[Btmp_], writes=[Btmp_])
        P.op("dve", (lambda d_: lambda e: e.reciprocal(out=d_, in_=d_))(tmp), reads=[Btmp_], writes=[Btmp_])
        P.op("pool", tt(dst, dst, tmp, ALU.mult), reads=[Bdst, Btmp_], writes=[Bdst])

    xn_alt = [None]

    def norm_T_a(src, Bsrc, rows, gT, BgT, dst, Bdst, col0, scale_eng="pool"):
        use_alt = (xn_alt[0] is not None) and (xnctr[0] % 2 == 1)
        (xn, Bxn) = xn_alt[0] if use_alt else (xn0, Bxn0)
        sc_ = xnctr[0] % 2 if xn_alt[0] is not None else 0
        xnctr[0] += 1
        ssq = stat[0:rows, 6 + sc_:7 + sc_]
        Bst_ = Bstn[sc_]
        P.op("act", act(xn[0:rows, :], src, AF.Square, accum=ssq), reads=[Bsrc], writes=[Bxn, Bst_])
        P.op("act", act(ssq, ssq, AF.Ln, scale=1.0 / D, bias=EPS), reads=[Bst_], writes=[Bst_])
        P.op("act", act(ssq, ssq, AF.Exp, scale=-0.5), reads=[Bst_], writes=[Bst_])
        if scale_eng == "pool":
            P.op("pool", ts(xn[0:rows, :], src, ssq, ALU.mult, 1.0, ALU.mult), reads=[Bsrc, Bst_], writes=[Bxn])
        else:
            P.op("act", (lambda xn=xn, ssq=ssq: lambda e: e.activation(out=xn[0:rows, :], in_=src, func=AF.Copy, scale=ssq))(),
                 reads=[Bsrc, Bst_], writes=[Bxn])

        def stage_b():
            for half in range(2):
                pv = pbb(half).rearrange("p (c t) -> p c t", t=128)
                P.op("pe", trs([(pv[:, c, 0:rows], xn[0:rows, (half * 8 + c) * 128:(half * 8 + c + 1) * 128],
                                 ident[0:rows, 0:rows]) for c in range(8)]),
                     reads=[Bxn, Bid], writes=[PB[half]])
                P.op("dve", tt(dst[:, half * 8:half * 8 + 8, col0:col0 + rows], pv[:, :, 0:rows],
                               gT[:, half * 8:half * 8 + 8].unsqueeze(2).to_broadcast([128, 8, rows]), ALU.mult),
                     reads=[PB[half], BgT], writes=[Bdst])
        return stage_b

    def norm_T(*a, **k):
        norm_T_a(*a, **k)()

    def pipeline(stages, n):
        carry = {}
        K = len(stages)
        for step in range(n + K - 1):
            for k in range(K):
                t = step - k
                if 0 <= t < n:
                    carry[t] = stages[k](t, carry.get(t))

    vhg = view(R1, 0, [128, 4, 1024], BF16); Bvhg = P.buf("vhg")
    qsil = view(R1, 8192, [128, 4, G], F32); Bqsil = [P.buf(f"qsil{i}") for i in range(4)]
    qt = view(R1, 16384, [128, 8, G], BF16); Bqt = P.buf("qt")
    kt = view(R1, 24576, [128, 8, G], BF16); Bkt = P.buf("kt")
    kdec = view(R1, 32768, [128, 8, G], BF16); Bkdec = P.buf("kdec")
    gsil = view(R2, 0, [128, 8, G], BF16); Bgsil = P.buf("gsil")
    mst = view(R2, 8192, [128, 8, G], BF16); Bmst = P.buf("mst")
    tmpv = [view(R2, 16384 + i * 2048, [128, G], F32) for i in range(7)]
    Btmp = [P.buf(f"tmp{i}") for i in range(7)]
    Sst = view(R3, 0, [128, 8, 128], F32); BS = [P.buf(f"S{h}") for h in range(8)]
    Sbf = view(R3, 4096, [128, 8, 128], BF16); BSbf = [P.buf(f"Sbf{h}") for h in range(8)]
    elast = view(R3, 6144, [128, 8, 8], F32); Bel = P.buf("elast")
    kdT = view(R3, 6656, [128, 8, 128], BF16); BkdT = [P.buf(f"kdT{h}") for h in range(8)]
    aTm = view(R3, 8704, [128, 8, 128], BF16); BaTm = [P.buf(f"aTm{h}") for h in range(8)]
    ohg = view(R3, 10752, [128, 8, 128], F32); Bohg = P.buf("ohg")
    sq = view(R3, 14848, [128, 8, 128], F32); Bsq = P.buf("sq")
    onb = view(R3, 18944, [128, 8, 128], BF16); Bonb = P.buf("onb")
    onbs = [(onb, Bonb), (view(R2, 30720, [128, 8, 128], BF16), P.buf("onb2"))]
    pend_tail = []
    chunk_ctr = [0]
    st8 = view(R3, 20992, [128, 8], F32); Bst8 = P.buf("st8")
    sigs = [view(R3, 21504, [128, G], F32), tmpv[6], view(R1, 40960, [128, G], F32), view(R1, 43008, [128, G], F32)]
    Bsigs = [P.buf("sig0"), Btmp[6], P.buf("sig2"), P.buf("sig3")]
    qsil_s = view(R3, 23552, [128, 8, NS], F32); Bqss = P.buf("qsil_s")
    sig_s = view(R3, 23680, [128, 8, NS], F32); Bsgs = P.buf("sig_s")

    P.op("pool", lambda e: e.memset(Sst, 0.0), writes=BS)
    P.op("pool", lambda e: e.memset(Sbf, 0.0), writes=BSbf)

    own_blocks = [0, 1, 2, 3, 4, 5, 10, 11, 6, 8, 7, 9, 12, 13]
    halo_blocks = [2, 3, 4, 5, 8, 10, 9, 11]
    prep_queue = []
    xpre = []
    apre = []
    TOKM = (0, 1, 2, 3, 4, 5, 10, 11)

    def w_in_blk(b):
        return w_in[:, b * 512:(b + 1) * 512]

    def hgrn_prep(hd, sig_ap, Bsig, own):
        tf, tk, tg, tc, tE, tEi = tmpv[0:6]
        Bf, Bk, Bg_, Bc, BE, BEi = Btmp[0:6]
        P.op("dve", ts(tf, sig_ap, oml[:, hd:hd + 1], ALU.mult, lb[:, hd:hd + 1], ALU.add),
             reads=[Bsig, Boml, Blb], writes=[Bf])
        P.op("pool", ts(tk, tf, -1.0, ALU.mult, 1.0, ALU.add), reads=[Bf], writes=[Bk])
        P.op("act", act(tg, tf, AF.Ln), reads=[Bf], writes=[Bg_])
        P.op("dve", lambda e: e.tensor_tensor_scan(out=tc, data0=rmask[:], data1=tg, initial=0.0,
                                                    op0=ALU.mult, op1=ALU.add),
             reads=[Brm, Bg_], writes=[Bc])
        P.op("act", act(tE, tc, AF.Exp), reads=[Bc], writes=[BE])
        P.op("act", act(tEi, tc, AF.Exp, scale=-1.0), reads=[Bc], writes=[BEi])
        P.op("dve", cp(elast[:, hd, :], tE.rearrange("p (c t) -> p c t", t=64)[:, :, 63]),
             reads=[BE], writes=[Bel])
        P.op("pool", tt(kt[:, hd, :], tk, tEi, ALU.mult), reads=[Bk, BEi], writes=[Bkt])
        P.op("dve", tt(kdec[:, hd, :].rearrange("p (c t) -> p c t", t=64),
                       kt[:, hd, :].rearrange("p (c t) -> p c t", t=64),
                       elast[:, hd, :].unsqueeze(2).to_broadcast([128, 8, 64]), ALU.mult),
             reads=[Bkt, Bel], writes=[Bkdec])
        if own:
            P.op("pool", tt(qt[:, hd, :], qsil[:, hd % 4, :], tE, ALU.mult), reads=[Bqsil[hd % 4], BE], writes=[Bqt])

    def hgrn_prep_tile(gi, own, tile):
        tsl = slice(tile * 128, tile * 128 + 128)
        kdv = pbb(0).rearrange("p (c t) -> p c t", t=128)
        P.op("pe", trs([(kdv[:, hd, :], kdec[:, hd, tsl], ident[:]) for hd in range(8)]),
             reads=[Bkdec, Bid], writes=[PB[0]])
        P.op("act", acp(kdT[:], kdv), reads=[PB[0]], writes=BkdT)
        if own:
            for half in range(2):
                P.op("pe", mms([(pbf(1)[:, r4 * 128:(r4 + 1) * 128], kt[:, half * 4 + r4, tsl], qt[:, half * 4 + r4, tsl], True, True)
                                for r4 in range(4)]), reads=[Bkt, Bqt], writes=[PB[1]])
                P.op("dve", tt(aTm[:, half * 4:half * 4 + 4, :], pbf(1).rearrange("p (a b) -> p a b", b=128),
                               bdm[:].unsqueeze(1).to_broadcast([128, 4, 128]), ALU.mult),
                     reads=[PB[1], Bbdm], writes=BaTm[half * 4:half * 4 + 4])

    def hgrn_chunk(gi, own, tile, c2):
        hgrn_flush_tail(keep=1)
        pb_ = 64 * c2
        ch = tile * 2 + c2
        csl = slice(tile * 128 + pb_, tile * 128 + pb_ + 64)
        if own:
            for half in range(2):
                specs = []
                for r4 in range(4):
                    hd = half * 4 + r4
                    oreg = pbf(5)[0:64, r4 * 128:(r4 + 1) * 128]
                    specs.append((oreg, qt[:, hd, csl], Sbf[:, hd, :], True, False))
                    specs.append((oreg, aTm[pb_:pb_ + 64, hd, pb_:pb_ + 64],
                                  vhg[pb_:pb_ + 64, tile, hd * 128:(hd + 1) * 128], False, True))
                P.op("pe", mms(specs), reads=[Bqt, BSbf[half * 4:half * 4 + 4], BaTm[half * 4:half * 4 + 4], Bvhg], writes=[PB[5]])
                P.op("act", acp(ohg[0:64, half * 4:half * 4 + 4, :], pbf(5)[0:64, :].rearrange("p (a b) -> p a b", b=128)),
                     reads=[PB[5]], writes=[Bohg])
        for half in range(2):
            bank = 6 + half
            hs = slice(half * 4, half * 4 + 4)
            P.op("pe", mms([(pbf(bank)[:, r4 * 128:(r4 + 1) * 128], kdT[pb_:pb_ + 64, half * 4 + r4, :],
                             vhg[pb_:pb_ + 64, tile, (half * 4 + r4) * 128:(half * 4 + r4 + 1) * 128], True, True) for r4 in range(4)]),
                 reads=[BkdT[half * 4:half * 4 + 4], Bvhg], writes=[PB[bank]])
            P.op("dve", tt(Sst[:, hs, :], Sst[:, hs, :], elast[:, hs, ch:ch + 1].to_broadcast([128, 4, 128]), ALU.mult),
                 reads=[BS[half * 4:half * 4 + 4], Bel], writes=BS[half * 4:half * 4 + 4])
            P.op("dve", tt(Sst[:, hs, :], Sst[:, hs, :], pbf(bank).rearrange("p (a b) -> p a b", b=128), ALU.add),
                 reads=[BS[half * 4:half * 4 + 4], PB[bank]], writes=BS[half * 4:half * 4 + 4])
            P.op("pool", cp(Sbf[:, hs, :], Sst[:, hs, :]), reads=BS[half * 4:half * 4 + 4], writes=BSbf[half * 4:half * 4 + 4])
        if own:
            o64 = ohg[0:64]
            s64 = sq[0:64]
            P.op("act", act(s64, o64, AF.Square), reads=[Bohg], writes=[Bsq])
            P.op("dve", lambda e, s64=s64: e.tensor_reduce(out=st8[0:64, :], in_=s64, axis=AX.X, op=ALU.add),
                 reads=[Bsq], writes=[Bst8])
            P.op("act", act(st8[0:64, :], st8[0:64, :], AF.Ln, scale=1.0 / 128, bias=EPS), reads=[Bst8], writes=[Bst8])
            P.op("act", act(st8[0:64, :], st8[0:64, :], AF.Exp, scale=-0.5), reads=[Bst8], writes=[Bst8])
            P.op("dve", tt(s64, o64, st8[0:64, :].unsqueeze(2).to_broadcast([64, 8, 128]), ALU.mult),
                 reads=[Bohg, Bst8], writes=[Bsq])
            onb_, Bonb_ = onbs[chunk_ctr[0] % 2]
            chunk_ctr[0] += 1
            P.op("dve", tt(onb_[0:64], s64, hgrow[0:64, :].unsqueeze(1).to_broadcast([64, 8, 128]), ALU.mult),
                 reads=[Bsq, Bhg], writes=[Bonb_])

            def tail(onb_=onb_, Bonb_=Bonb_, csl=csl):
                tv = pbb(0).rearrange("p (c t) -> p c t", t=128)
                P.op("pe", trs([(tv[:, hd, 0:64], onb_[0:64, hd, :], ident[0:64, 0:64]) for hd in range(8)]),
                     reads=[Bonb_, Bid], writes=[PB[0]])
                P.op("dve", tt(mst[:, :, csl], tv[:, :, 0:64], gsil[:, :, csl], ALU.mult),
                     reads=[PB[0], Bgsil], writes=[Bmst])
            pend_tail.append(tail)

    def hgrn_flush_tail(keep=0):
        while len(pend_tail) > keep:
            pend_tail.pop(0)()

    def hgrn_items(gi):
        own = gi >= 4
        items = []
        for tile in range(4):
            items.append((lambda tile=tile: hgrn_prep_tile(gi, own, tile)))
            for c2 in range(2):
                items.append((lambda tile=tile, c2=c2: hgrn_chunk(gi, own, tile, c2)))
        if own:
            o0 = (gi - 4) * G
            items.append(hgrn_flush_tail)
            items.append((lambda: P.dma("sp", dm(mhg[:, :, o0:o0 + G], mst), reads=[Bmst])))
        return items

    p1flag[0] = True
    xn_alt[0] = (xn1, Bxn1)
    WQ1 = WQ(lambda: len(wbuf))
    for gi in range(8):
        for b_ in (own_blocks if gi >= 4 else halo_blocks):
            WQ1.add((lambda b_=b_: wload(w_in_blk(b_))))
    for gi in range(8):
        own = gi >= 4
        last = gi == 7
        r0 = gi * G
        o0 = (gi - 4) * G
        blocks = own_blocks if own else halo_blocks
        def A1(tile, _):
            if apre:
                return apre.pop(0)
            if xpre:
                xt_, Bxt_ = xpre.pop(0)
            else:
                xt_, Bxt_ = xts[xtctr[0] % 2]
                xtctr[0] += 1
                P.dma("act", dm(xt_, xh[r0 + tile * 128: r0 + tile * 128 + 128, :]), writes=[Bxt_])
            return norm_T_a(xt_, Bxt_, 128, g1T, Bg1, hT, BhT, tile * 128)

        def A2(tile, sb_):
            sb_()
        pipeline([A1, A2], 4)
        if last:
            P.dma("act", dm(xt[0:NS, :], xs), writes=[Bxt])
            norm_T(xt[0:NS, :], Bxt, NS, g1T, Bg1, hTs, BhTs, 0)
        citems = hgrn_items(gi - 1) if gi > 0 else []
        nsafe = 6 if own else 4
        n_items0 = len(citems)
        ss_state = [0, 0]

        def spread_items():
            ss_state[0] += 1
            if not citems:
                return
            tgt = int((ss_state[0] - 1) * n_items0 / max(1, nsafe * 4 - 5)) + (1 if ss_state[0] > 1 else 0)
            while citems and ss_state[1] < tgt:
                citems.pop(0)()
                ss_state[1] += 1
        for bi, blk in enumerate(blocks):
            if bi == len(blocks) - 2 and gi + 1 < 8:
                for t_ in range(2):
                    xt_, Bxt_ = xpre.pop(0)
                    apre.append(norm_T_a(xt_, Bxt_, 128, g1T, Bg1, hT, BhT, t_ * 128))
            if bi == 2 and gi + 1 < 8:
                for t_ in range(2):
                    xt_, Bxt_ = xts[xtctr[0] % 2]
                    xtctr[0] += 1
                    rr = (gi + 1) * G + t_ * 128
                    P.dma("act", dm(xt_, xh[rr: rr + 128, :]), writes=[Bxt_])
                    xpre.append((xt_, Bxt_))
            wt, Bw = WQ1.get()
            hgrn_flush_tail()
            if bi >= nsafe:
                while citems:
                    citems.pop(0)()
            if blk in TOKM:
                for tile in range(4 + (1 if last else 0)):
                    if prep_queue:
                        prep_queue.pop(0)()
                    spread_items()
                    pbi = (2 + (ctr["pb"] % 2)) if last else (2 + (ctr["pb"] % 3))
                    ctr["pb"] += 1
                    if tile < 4:
                        M = 128
                        lh = lambda c: hT[:, c, tile * 128:(tile + 1) * 128]
                        Bl = BhT
                    else:
                        M = NS
                        lh = lambda c: hTs[:, c, :]
                        Bl = BhTs
                    pt = pbf(pbi)[0:M, :]
                    P.op("pe", mms([(pt, lh(c), wt[:, c, :], c == 0, c == 15) for c in range(16)]),
                         reads=[Bl, Bw], writes=[PB[pbi]])
                    rows = slice(r0 + tile * 128, r0 + tile * 128 + 128)
                    orow = slice(o0 + tile * 128, o0 + tile * 128 + 128)
                    if tile == 4:
                        s_, Bs_ = nxt(sstg, "ss")
                        P.op("act", acp(s_[:], pt), reads=[PB[pbi]], writes=[Bs_])
                        P.dma("sp", dm(sproj[:, blk * 512:(blk + 1) * 512], s_[:]), reads=[Bs_])
                    elif blk in (0, 1):
                        s_, Bs_ = nxt(stg16, "s16")
                        P.op("act", acp(s_[:], pt), reads=[PB[pbi]], writes=[Bs_])
                        P.dma("sp", dm(qs[orow, blk * 512:(blk + 1) * 512], s_[:]), reads=[Bs_])
                    elif blk in (2, 3):
                        s_, Bs_ = nxt(stg16, "s16")
                        if own:
                            f_, Bf_ = nxt(stg32, "s32")
                            P.op("act", acp(f_[:], pt), reads=[PB[pbi]], writes=[Bf_])
                            P.dma("sp", dm(wk[orow, (blk - 2) * 512:(blk - 1) * 512], f_[:]), reads=[Bf_])
                            P.op("pool", cp(s_[:], f_[:]), reads=[Bf_], writes=[Bs_])
                        else:
                            P.op("act", acp(s_[:], pt), reads=[PB[pbi]], writes=[Bs_])
                        P.dma("sp", dm(kscr[rows, (blk - 2) * 512:(blk - 1) * 512], s_[:]), reads=[Bs_])
                    elif blk in (4, 5):
                        v_, Bv_ = nxt(vstg, "v")
                        if own:
                            f_, Bf_ = nxt(stg32, "s32")
                            P.op("act", acp(f_[:], pt), reads=[PB[pbi]], writes=[Bf_])
                            P.dma("sp", dm(wv[orow, (blk - 4) * 512:(blk - 3) * 512], f_[:]), reads=[Bf_])
                            P.op("pool", cp(v_[:, :, 0:128], f_[:].rearrange("p (h d) -> p h d", d=128)), reads=[Bf_], writes=[Bv_])
                        else:
                            P.op("act", acp(v_[:, :, 0:128], pt.rearrange("p (h d) -> p h d", d=128)), reads=[PB[pbi]], writes=[Bv_])
                        P.dma("sp", dm(vscr[rows, (blk - 4) * 520:(blk - 3) * 520], v_[:].rearrange("p h d -> p (h d)")), reads=[Bv_])
                    else:
                        P.op("dve", cp(vhg[:, tile, (blk - 10) * 512:(blk - 9) * 512], pt), reads=[PB[pbi]], writes=[Bvhg])
            else:
                fm_pending = []
                for j in range(4):
                    if prep_queue:
                        prep_queue.pop(0)()
                    spread_items()
                    pbi = (2 + (ctr["pb"] % 2)) if last else (2 + (ctr["pb"] % 3))
                    ctr["pb"] += 1
                    pt = pbf(pbi)
                    P.op("pe", mms([(pt, wt[:, c, j * 128:(j + 1) * 128], hT[:, c, :], c == 0, c == 15) for c in range(16)]),
                         reads=[BhT, Bw], writes=[PB[pbi]])
                    if blk in (6, 7):
                        P.op("act", act(qsil[:, j, :], pt, AF.Silu), reads=[PB[pbi]], writes=[Bqsil[j]])
                    elif blk in (8, 9):
                        hd = (blk - 8) * 4 + j
                        sg_ = sigs[j]
                        P.op("act", act(sg_, pt, AF.Sigmoid), reads=[PB[pbi]], writes=[Bsigs[j]])
                        fm_pending.append((hd, sg_, Bsigs[j]))
                    else:
                        hd = (blk - 12) * 4 + j
                        P.op("act", act(gsil[:, hd, :], pt, AF.Silu), reads=[PB[pbi]], writes=[Bgsil])
                    if last and blk in (6, 7, 8, 9):
                        hd = ((blk - 6) % 2) * 4 + j if blk in (6, 7) else (blk - 8) * 4 + j
                        if blk in (6, 7):
                            hd = (blk - 6) * 4 + j
                        pts = pbf(4)[:, 0:NS]
                        P.op("pe", mms([(pts, wt[:, c, j * 128:(j + 1) * 128], hTs[:, c, :], c == 0, c == 15) for c in range(16)]),
                             reads=[BhTs, Bw], writes=[PB[4]])
                        if blk in (6, 7):
                            P.op("act", act(qsil_s[:, hd, :], pts, AF.Silu), reads=[PB[4]], writes=[Bqss])
                        else:
                            P.op("act", act(sig_s[:, hd, :], pts, AF.Sigmoid), reads=[PB[4]], writes=[Bsgs])
                    if last and blk in (12, 13):
                        if j == 0:
                            pts = pbf(4)[0:NS, :]
                            P.op("pe", mms([(pts, hTs[:, c, :], wt[:, c, :], c == 0, c == 15) for c in range(16)]),
                                 reads=[BhTs, Bw], writes=[PB[4]])
                            s_, Bs_ = nxt(sstg, "ss")
                            P.op("act", acp(s_[:], pts), reads=[PB[4]], writes=[Bs_])
                            P.dma("sp", dm(sproj[:, blk * 512:(blk + 1) * 512], s_[:]), reads=[Bs_])
                for (hd_, sg_, Bsg_) in fm_pending:
                    prep_queue.append((lambda hd_=hd_, sg_=sg_, Bsg_=Bsg_, own=own: hgrn_prep(hd_, sg_, Bsg_, own)))
    while prep_queue:
        prep_queue.pop(0)()
    for it_ in hgrn_items(7):
        it_()

    p1flag[0] = False
    xn_alt[0] = None
    P.dma("sp", dm(sp_out.rearrange("h k v -> k h v"), Sst), reads=BS)

    P.barrier()

    def wload_gu(t):
        tl, B = wbuf[wctr[0] % len(wbuf)]
        wctr[0] += 1
        P.dma("pool", dm(tl[:, :, 0:256], w_gate[:, t * 256:(t + 1) * 256].rearrange("(c p) n -> p c n", p=128)), writes=[B])
        P.dma("pool", dm(tl[:, :, 256:512], w_up[:, t * 256:(t + 1) * 256].rearrange("(c p) n -> p c n", p=128)), writes=[B], nowaw=True)
        return tl, B

    def wd_blk(dmb, kq):
        return w_down[kq * 1408:(kq + 1) * 1408, dmb * 512:(dmb + 1) * 512]
    NGU = DFF // 256
    dseq = [(dmb, kq) for dmb in range(4) for kq in range(4)]
    WQ3 = WQ(lambda: len(wbuf))
    for og in range(4):
        for d_ in range(4):
            WQ3.add((lambda d_=d_: wload(w_out[:, d_ * 512:(d_ + 1) * 512])))
        for t_ in range(NGU):
            WQ3.add((lambda t_=t_: wload_gu(t_)))
        for sq_ in dseq:
            WQ3.add((lambda sq_=sq_: wload(wd_blk(*sq_), nchunk=11)))

    qbc = view(R1, 0, [128, 1024], F32); Bqbc = P.buf("qbc")
    kcs = [(view(R1, 4096, [128, 1024], F32), P.buf("kc0")), (view(R2, 16384, [128, 1024], F32), P.buf("kc1"))]
    vcs = [(view(R1, 8192, [128, 1024], F32), P.buf("vc0")), (view(R2, 20480, [128, 1024], F32), P.buf("vc1"))]
    prod = view(R1, 12288, [128, 8, 128], F32); Bprod = P.buf("prod")
    s8s = [(view(R1, 16384, [128, 8], F32), P.buf("s8a")), (view(R1, 16384 + 32, [128, 8], F32), P.buf("s8b"))]
    srow = view(R1, 16384 + 64, [128, 5 * 1024], F32); Bsrow = P.buf("srow")
    mrow = view(R1, 16384 + 64 + 20480, [128, 1056], F32); Bmrow = P.buf("mrow")
    Sj = view(R2, 0, [128, 8, 128], F32); BSj = P.buf("Sj")
    Sn = view(R2, 4096, [128, 8, 128], F32); BSn = P.buf("Sn")
    vb = view(R2, 8192, [128, 8, 128], F32); Bvb = P.buf("vb")
    fj = view(R2, 12288, [128, 8], F32); Bfj = P.buf("fj")
    kj = view(R2, 12288 + 64, [128, 8], F32); Bkj = P.buf("kj")
    mrgb = view(R2, 24576, [128, D], BF16); Bmrgb = P.buf("mrgb")
    mTs, BmTs = sb("mTs", [128, 16, NS], BF16)
    Bp15b = P.buf("p15b")
    A = [Bqbc, Bprod, Bsrow, Bmrow, BSj, BSn, Bvb, Bfj, Bkj, Bmrgb, Bp15b] + [b for (_, b) in kcs + vcs + s8s]

    P.dma("sp", dm(srow[0:NS, 0:3072], sproj[:, 0:3072]), writes=[Bsrow])
    P.dma("sp", dm(srow[0:NS, 3072:5120], sproj[:, 5120:7168]), writes=[Bsrow], nowaw=True)
    P.dma("sp", dm(ks_o, srow[0:NS, 1024:2048]), reads=[Bsrow])
    P.dma("sp", dm(vs_o, srow[0:NS, 2048:3072]), reads=[Bsrow])
    num_lo = pbf(0)[0:NS, :]
    num_hi = pbf(1)[0:NS, :]
    den_p = pbf(2)[0:NS, 0:8]
    pats = [(1920, 1), (1536, 4), (0, 16)]
    nacc = 0
    for j in range(NS):
        P.dma("sp", dm(qbc, sproj[j, 0:1024].partition_broadcast(128)), writes=[Bqbc])
        for (start_, dil) in pats:
            (kc, Bkc), (vc, Bvc), (s8, Bs8) = kcs[nacc % 2], vcs[nacc % 2], s8s[nacc % 2]
            krows = ck[j, start_:start_ + 128 * dil, :].rearrange("(i s) n -> i s n", s=dil)[:, 0, :]
            vrows = cv[j, start_:start_ + 128 * dil, :].rearrange("(i s) n -> i s n", s=dil)[:, 0, :]
            P.dma("sp", dm(kc, krows), writes=[Bkc])
            P.dma("act", dm(vc, vrows), writes=[Bvc])
            kc3 = kc.rearrange("p (h d) -> p h d", d=128)
            P.op("dve", tt(kc3, kc3, qbc.rearrange("p (h d) -> p h d", d=128), ALU.mult), reads=[Bkc, Bqbc], writes=[Bkc])
            P.op("dve", (lambda s8=s8, kc3=kc3: lambda e: e.tensor_reduce(out=s8, in_=kc3, axis=AX.X, op=ALU.add))(),
                 reads=[Bkc], writes=[Bs8])
            P.op("act", act(s8, s8, AF.Exp, scale=SCALE), reads=[Bs8], writes=[Bs8])
            vc3 = vc.rearrange("p (h d) -> p h d", d=128)
            P.op("pool", tt(vc3, vc3, s8.unsqueeze(2).to_broadcast([128, 8, 128]), ALU.mult), reads=[Bvc, Bs8], writes=[Bvc])
            first = nacc == 0
            lastm = nacc == NS * 3 - 1
            P.op("pe", mms([(num_lo, oneh[:, j, :], vc[:, 0:512], first, lastm),
                            (num_hi, oneh[:, j, :], vc[:, 512:1024], first, lastm),
                            (den_p, oneh[:, j, :], s8, first, lastm)]),
                 reads=[Bvc, Bs8, Boh], writes=[PB[0], PB[1], PB[2]])
            nacc += 1
    P.op("act", acp(mrow[0:NS, 0:512], num_lo), reads=[PB[0]], writes=[Bmrow])
    P.op("act", acp(mrow[0:NS, 512:1024], num_hi), reads=[PB[1], Bmrow], writes=[Bmrow])
    P.op("act", acp(mrow[0:NS, 1024:1032], den_p), reads=[PB[2], Bmrow], writes=[Bmrow])
    q4 = srow[0:NS, 0:1024].rearrange("p (h d) -> p h d", d=128)
    k4 = srow[0:NS, 1024:2048].rearrange("p (h d) -> p h d", d=128)
    v4 = srow[0:NS, 2048:3072].rearrange("p (h d) -> p h d", d=128)
    p4 = prod[0:NS]
    (s8, Bs8) = s8s[0]
    e4 = s8[0:NS, :]
    W4 = [Bprod, Bs8, Bmrow]
    P.op("dve", tt(p4, q4, k4, ALU.mult), reads=[Bsrow], writes=[Bprod])
    P.op("dve", lambda e: e.tensor_reduce(out=e4, in_=p4, axis=AX.X, op=ALU.add), reads=[Bprod], writes=[Bs8])
    P.op("act", act(e4, e4, AF.Exp, scale=SCALE), reads=[Bs8], writes=[Bs8])
    P.op("dve", ts(e4, e4, 3.0, ALU.mult), reads=[Bs8], writes=[Bs8])
    P.op("dve", tt(p4, v4, e4.unsqueeze(2).to_broadcast([NS, 8, 128]), ALU.mult), reads=[Bsrow, Bs8], writes=[Bprod])
    m4 = mrow[0:NS, 0:1024].rearrange("p (h d) -> p h d", d=128)
    d4 = mrow[0:NS, 1024:1032]
    P.op("dve", tt(m4, m4, p4, ALU.add), reads=W4, writes=[Bmrow])
    P.op("dve", tt(d4, d4, e4, ALU.add), reads=W4, writes=[Bmrow])
    P.op("dve", lambda e: e.reciprocal(out=d4, in_=d4), reads=[Bmrow], writes=[Bmrow])
    P.op("dve", tt(m4, m4, d4.unsqueeze(2).to_broadcast([NS, 8, 128]), ALU.mult), reads=[Bmrow], writes=[Bmrow])
    ssq = stat[0:NS, 1:2]
    P.op("act", act(p4.rearrange("p h d -> p (h d)"), mrow[0:NS, 0:1024], AF.Square, accum=ssq), reads=[Bmrow], writes=[Bprod, Bstat])
    P.op("act", act(ssq, ssq, AF.Ln, scale=1.0 / 1024, bias=EPS), reads=[Bstat], writes=[Bstat])
    P.op("act", act(ssq, ssq, AF.Exp, scale=-0.5), reads=[Bstat], writes=[Bstat])
    P.op("dve", stt(mrgb[0:NS, 0:1024], mrow[0:NS, 0:1024], ssq, agrow[0:NS, :], ALU.mult, ALU.mult),
         reads=[Bmrow, Bstat, Bag], writes=[Bmrgb])
    o_lo = pbf(3)[0:NS, :]
    o_hi = pbf(4)[0:NS, :]
    for j in range(NS):
        P.dma("sp", dm(Sj, st_in[j].rearrange("h k v -> k h v")), writes=[BSj])
        P.dma("act", dm(vb.rearrange("p h d -> p (h d)"), sproj[j, 5120:6144].partition_broadcast(128)), writes=[Bvb])
        P.op("dve", tt(fj, sig_s[:, :, j], oml[:], ALU.mult), reads=[Bsgs, Boml], writes=[Bfj])
        P.op("dve", tt(fj, fj, lb[:], ALU.add), reads=[Bfj, Blb], writes=[Bfj])
        P.op("dve", ts(kj, fj, -1.0, ALU.mult, 1.0, ALU.add), reads=[Bfj], writes=[Bkj])
        P.op("dve", tt(Sn, Sj, fj.unsqueeze(2).to_broadcast([128, 8, 128]), ALU.mult), reads=[BSj, Bfj], writes=[BSn])
        P.op("pool", tt(vb, vb, kj.unsqueeze(2).to_broadcast([128, 8, 128]), ALU.mult), reads=[Bvb, Bkj], writes=[Bvb])
        P.op("dve", tt(Sn, Sn, vb, ALU.add), reads=[BSn, Bvb], writes=[BSn])
        P.dma("sp", dm(ss_o[j].rearrange("h k v -> k h v"), Sn), reads=[BSn])
        P.op("pool", tt(vb, Sn, qsil_s[:, :, j].unsqueeze(2).to_broadcast([128, 8, 128]), ALU.mult), reads=[BSn, Bqss], writes=[Bvb])
        vf = vb.rearrange("p h d -> p (h d)")
        P.op("pe", mms([(o_lo, oneh[:, j, :], vf[:, 0:512], j == 0, j == NS - 1),
                        (o_hi, oneh[:, j, :], vf[:, 512:1024], j == 0, j == NS - 1)]),
             reads=[Bvb, Boh], writes=[PB[3], PB[4]])
    oh4 = mrow[0:NS, 0:1024]
    P.op("act", acp(oh4[:, 0:512], o_lo), reads=[PB[3]], writes=[Bmrow])
    P.op("act", acp(oh4[:, 512:1024], o_hi), reads=[PB[4], Bmrow], writes=[Bmrow])
    oh48 = oh4.rearrange("p (h d) -> p h d", d=128)
    P.op("act", act(p4, oh48, AF.Square), reads=[Bmrow], writes=[Bprod])
    P.op("dve", lambda e: e.tensor_reduce(out=e4, in_=p4, axis=AX.X, op=ALU.add), reads=[Bprod], writes=[Bs8])
    P.op("act", act(e4, e4, AF.Ln, scale=1.0 / 128, bias=EPS), reads=[Bs8], writes=[Bs8])
    P.op("act", act(e4, e4, AF.Exp, scale=-0.5), reads=[Bs8], writes=[Bs8])
    P.op("dve", tt(oh48, oh48, e4.unsqueeze(2).to_broadcast([NS, 8, 128]), ALU.mult), reads=[Bmrow, Bs8], writes=[Bmrow])
    P.op("dve", tt(oh48, oh48, hgrow[0:NS, :].unsqueeze(1).to_broadcast([NS, 8, 128]), ALU.mult), reads=[Bmrow, Bhg], writes=[Bmrow])
    gr = srow[0:NS, 4096:5120]
    gtmp = srow[0:NS, 3072:4096]
    P.op("act", act(gtmp, gr, AF.Exp, scale=-1.0), reads=[Bsrow], writes=[Bsrow])
    P.op("dve", ts(gtmp, gtmp, 1.0, ALU.add), reads=[Bsrow], writes=[Bsrow])
    P.op("dve", lambda e: e.reciprocal(out=gtmp, in_=gtmp), reads=[Bsrow], writes=[Bsrow])
    P.op("dve", tt(gr, gr, gtmp, ALU.mult), reads=[Bsrow], writes=[Bsrow])
    P.op("dve", tt(mrgb[0:NS, 1024:2048], oh4, gr, ALU.mult), reads=[Bmrow, Bsrow, Bmrgb], writes=[Bmrgb])
    tvs = pbb(5).rearrange("p (c t) -> p c t", t=64)
    P.op("pe", trs([(tvs[:, c, 0:NS], mrgb[0:NS, c * 128:(c + 1) * 128], ident[0:NS, 0:NS]) for c in range(16)]),
         reads=[Bmrgb, Bid], writes=[PB[5]])
    P.op("dve", cp(mTs[:], tvs[:, :, 0:NS]), reads=[PB[5]], writes=[BmTs])

    P.barrier()

    wbuf.append((view(R3, 8192, [128, 16, 512], BF16), P.buf("wbuf2")))
    WQ3.prime()
    Bp2 = P.buf("p2")
    P.handoff(A, [Bp2])
    qtm = [(view(R1, i * 2048, [128, 1024], BF16), P.buf(f"qtm{i}")) for i in range(2)]
    kcm = [(view(R1, 4096 + i * 2048, [128, 1024], BF16), P.buf(f"kcm{i}")) for i in range(2)]
    kpm = [(view(R1, 8192 + i * 2048, [128, 1024], BF16), P.buf(f"kpm{i}")) for i in range(2)]
    vcm = [(view(R2, 16512 + i * 2080, [128, 1040], BF16), P.buf(f"vcm{i}")) for i in range(4)]
    vpm = [(view(R1, 12288 + i * 2080, [128, 1040], BF16), P.buf(f"vpm{i}")) for i in range(3)]
    qT = [(view(R1, 22688 + i * 2048, [128, 8, 128], BF16), P.buf(f"qT{i}")) for i in range(2)]
    kcT = [(view(R1, 26784 + i * 2048, [128, 8, 128], BF16), P.buf(f"kcT{i}")) for i in range(3)]
    kpT = [(view(R1, 32928 + i * 2048, [128, 8, 128], BF16), P.buf(f"kpT{i}")) for i in range(2)]
    pT = [(view(R2, 8320 + i * 2048, [128, 4, 256], BF16), P.buf(f"pT{i}")) for i in range(4)]
    pTm = []
    resb = [(view(R2, i * 4160, [128, 8, 130], F32), P.buf(f"res{i}")) for i in range(2)]
    for lst in (qtm, kcm, kpm, vcm, vpm, qT, kcT, kpT, pT, pTm, resb):
        P.handoff([Bp2], [b for (_, b) in lst])
    P.handoff([Bp15b], [b for (_, b) in resb])

    Brs = P.buf("rs_acc")
    iters = [(pi, dil, r, b) for pi, dil in enumerate((1, 4, 16)) for r in range(dil) for b in range(16 // dil)]

    def S1(it, _):
        pi, dil, r, b = iters[it]
        s = it % 2
        s3 = it % 3
        s3p = (it - 1) % 3
        o_start = dil * 128 * b + r
        c_start = HALO + o_start
        p_start = HALO + dil * 128 * (b - 1) + r

        def rows(t, start):
            return t[start:start + 128 * dil, :].rearrange("(i s) n -> i s n", s=dil)[:, 0, :] if dil > 1 \
                else t[start:start + 128, :]
        (q_, Bq_), (kc_, Bkc_) = qtm[s], kcm[s]
        (vc_, Bvc_) = vcm[it % 4]
        P.dma("sp", dm(q_, rows(qs, o_start)), writes=[Bq_])
        P.dma("sp", dm(kc_, rows(kscr, c_start)), writes=[Bkc_])
        P.dma("sp", dm(vc_, rows(vscr, c_start)), writes=[Bvc_])
        (qT_, BqT_), (kcT_, BkcT_) = qT[s], kcT[s3]
        tlist = [(q_, Bq_, qT_, BqT_, 0), (kc_, Bkc_, kcT_, BkcT_, 1)]
        if b == 0:
            (kp_, Bkp_), (vp_, Bvp_) = kpm[s], vpm[it % 3]
            (kpT_, BkpT_) = kpT[s]
            P.dma("sp", dm(kp_, rows(kscr, p_start)), writes=[Bkp_])
            P.dma("sp", dm(vp_, rows(vscr, p_start)), writes=[Bvp_])
            tlist.append((kp_, Bkp_, kpT_, BkpT_, 0))
        else:
            (kpT_, BkpT_) = kcT[s3p]
            (vp_, Bvp_) = vcm[(it - 1) % 4]
        for (src, Bsrc, dstT, BdT, pbi) in tlist:
            tv = pbb(pbi).rearrange("p (c t) -> p c t", t=128)
            P.op("pe", trs([(tv[:, h, :], src[:, h * 128:(h + 1) * 128], ident[:]) for h in range(8)]),
                 reads=[Bsrc, Bid], writes=[PB[pbi]])
            P.op("dve", cp(dstT, tv), reads=[PB[pbi]], writes=[BdT])
        return (qT_, BqT_, kcT_, BkcT_, kpT_, BkpT_, vc_, Bvc_, vp_, Bvp_, o_start)

    def S2(it, c_):
        pi, dil, r, b = iters[it]
        (qT_, BqT_, kcT_, BkcT_, kpT_, BkpT_, vc_, Bvc_, vp_, Bvp_, o_start) = c_
        mb_, Bmb_ = (mbB, BmbB) if b == 0 else (mbA, BmbA)
        mbf = mb_[:].rearrange("p a b -> p (a b)")
        for half in range(2):
            (pT_, BpT_) = pT[(it % 2) * 2 + half]
            b0, b1 = (2, 3) if half == 0 else (6, 7)
            specs = []
            for bank in (b0, b1):
                specs.append((pbf(bank), ident[:], mbf, True, False))
                for e2 in range(2):
                    hh = (0 if bank == b0 else 2) + e2
                    h = half * 4 + hh
                    off = e2 * 256
                    specs.append((pbf(bank)[:, off:off + 128], kcT_[:, h, :], qT_[:, h, :], False, False))
                    specs.append((pbf(bank)[:, off + 128:off + 256], kpT_[:, h, :], qT_[:, h, :], False, e2 == 1))
            P.op("pe", mms(specs), reads=[BqT_, BkcT_, BkpT_, Bid, Bmb_], writes=[PB[b0], PB[b1]])
            P.op("act", act(pT_[:, 0:2, :].rearrange("p a b -> p (a b)"), pbf(b0), AF.Exp, scale=SCALE),
                 reads=[PB[b0]], writes=[BpT_])
            P.op("act", act(pT_[:, 2:4, :].rearrange("p a b -> p (a b)"), pbf(b1), AF.Exp, scale=SCALE),
                 reads=[PB[b1]], writes=[BpT_])
        return c_

    def S3(it, c_):
        P.flush()
        pi, dil, r, b = iters[it]
        s = it % 2
        (qT_, BqT_, kcT_, BkcT_, kpT_, BkpT_, vc_, Bvc_, vp_, Bvp_, o_start) = c_
        (res_, Bres_) = resb[s]
        for half in range(2):
            (pT_, BpT_) = pT[(it % 2) * 2 + half]
            v0, v1 = 4, 5
            specs = []
            for hh in range(4):
                h = half * 4 + hh
                bank = v0 if hh < 2 else v1
                off = (hh % 2) * 130
                oreg = pbf(bank)[:, off:off + 130]
                specs.append((oreg, pT_[:, hh, 0:128], vc_[:, h * 130:(h + 1) * 130], True, False))
                specs.append((oreg, pT_[:, hh, 128:256], vp_[:, h * 130:(h + 1) * 130], False, True))
            P.op("pe", mms(specs), reads=[BpT_, Bvc_, Bvp_], writes=[PB[v0], PB[v1]])
            P.op("dve", cp(res_[:, half * 4:half * 4 + 2, :], pbf(v0)[:, 0:260].rearrange("p (a b) -> p a b", b=130)),
                 reads=[PB[v0]], writes=[Bres_])
            P.op("act", acp(res_[:, half * 4 + 2:half * 4 + 4, :], pbf(v1)[:, 0:260].rearrange("p (a b) -> p a b", b=130)),
                 reads=[PB[v1], Bres_], writes=[Bres_])
        dst = rs[0, o_start:o_start + 128 * dil, :].rearrange("(i s) n -> i s n", s=dil)[:, 0, :] if dil > 1 \
            else rs[0, o_start:o_start + 128, :]
        if pi == 0:
            P.defer((lambda dst=dst, res_=res_, Bres_=Bres_:
                     P.dma("sp", dm(dst, res_.rearrange("p a b -> p (a b)")), reads=[Bres_, Brs])))
        else:
            P.defer((lambda dst=dst, res_=res_, Bres_=Bres_:
                     P.dma("pool", (lambda e: e.dma_start(out=dst, in_=res_.rearrange("p a b -> p (a b)"), accum_op=ALU.add)),
                           reads=[Bres_], writes=[Brs], sembuf=Brs)))
    pipeline([S1, S2, S3], len(iters))

    P.barrier()

    Bp3a = P.buf("p3a")
    P.handoff([Bp2] + [b for lst in (qtm, kcm, kpm, vcm, vpm, qT, kcT, kpT, pT, pTm) for (_, b) in lst], [Bp3a])
    mix = view(R1, 0, [128, 4, D], F32); Bmix = [P.buf(f"mix{i}") for i in range(4)]
    xt3 = view(R1, 32768, [128, D], F32); Bxt3 = P.buf("xt3")
    ffT = view(R1, 0, [128, 44, G], BF16); BffT = P.buf("ffT")
    x1r = view(R1, 0, [128, D], F32); Bx1r = P.buf("x1r")
    xnF = view(R1, 8192, [128, D], BF16); BxnF = P.buf("xnF")
    ars = [(view(R2, i * 4160, [128, 8, 130], F32), P.buf(f"ar{i}")) for i in range(2)]
    mTa = view(R2, 8320, [128, 8, G], BF16); BmTa = [P.buf(f"mTa{i}") for i in range(4)]
    onats = [(view(R2, 16512 + i * 2048, [128, 1024], BF16), P.buf(f"onat{i}")) for i in range(2)]
    Bar_all = [b for (_, b) in ars] + [b for (_, b) in onats]
    xnB = view(R1, 40960, [128, D], BF16); BxnB = P.buf("xnB")
    x3pair = [(xt3, None), (xtb_t[:], Bxtb)]
    Bx1s = [P.buf(f"x1s{i}") for i in range(4)]
    Bsta = [P.buf("sta0"), P.buf("sta1")]
    Bstc = [P.buf("stc0"), P.buf("stc1")]
    Bstf = [P.buf("stf0"), P.buf("stf1")]
    h2T = view(R2, 0, [128, 16, G], BF16); Bh2T = P.buf("h2T")
    ffo = view(R2, 0, [128, 4, D], F32); Bffo = [P.buf(f"ffo{i}") for i in range(4)]
    mTh = view(R4, 0, [128, 8, G], BF16); BmTh = P.buf("mTh")
    sgv = [(view(R4, 8192 + i * 2048, [128, G], F32), P.buf(f"sg{i}")) for i in range(2)]
    h2Ts = view(R4, 12288, [128, 16, NS], BF16); Bh2Ts = P.buf("h2Ts")
    ffTs = view(R4, 12416, [128, 44, NS], BF16); BffTs = P.buf("ffTs")
    smix = view(R4, 12800, [128, D], F32); Bsmix = P.buf("smix")
    sffo = view(R4, 20992, [128, D], F32); Bsffo = P.buf("sffo")
    sgs = view(R4, 29184, [128, NS], F32); Bsgs2 = P.buf("sgs")
    grow = view(R3, 0, [128, D], F32); Bgrow = P.buf("grow")
    sxs, Bsxs = xt3, Bxt3
    Bp3c = P.buf("p3c")
    P.handoff(BS + BSbf + [Bel, Bohg, Bsq, Bonb, Bst8, Bqss, Bsgs] + BkdT + BaTm + Bsigs, [Bp3c])
    for b_ in [BmTh, Bh2Ts, BffTs, Bsmix, Bsffo, Bsxs, Bsgs2] + [b for (_, b) in sgv]:
        P.handoff([Bp3c], [b_])
    for b_ in Bmix + [Bxt3]:
        P.handoff([Bp3a], [b_])
    for b_ in [BmTa] + Bar_all:
        P.handoff([b for (_, b) in resb] + [Bp15b], [b_])
    x3pair[0] = (xt3, Bxt3)
    xn_alt[0] = (xnB, BxnB)

    def rms_rows(src, rows, width, col):
        ssq_ = stat[0:rows, col:col + 1]
        P.op("act", act(xn[0:rows, 0:width], src, AF.Square, accum=ssq_), reads=[], writes=[Bxn, Bstat])
        P.op("act", act(ssq_, ssq_, AF.Ln, scale=1.0 / width, bias=EPS), reads=[Bstat], writes=[Bstat])
        P.op("act", act(ssq_, ssq_, AF.Exp, scale=-0.5), reads=[Bstat], writes=[Bstat])
        return ssq_

    for og in range(4):
        last = og == 3
        o0 = og * G
        ntile = 4 + (1 if last else 0)
        P.dma("sp", dm(mTh, mhg[:, :, o0:o0 + G]), writes=[BmTh])
        for tile in range(4):
            t0 = o0 + tile * 128
            par = tile % 2
            (a3, Bar_), (onat, Bonat) = ars[par], onats[par]
            P.dma("sp", dm(a3.rearrange("p h e -> p (h e)"), rs[0, t0:t0 + 128, :]), writes=[Bar_])
            P.op("dve", (lambda a3=a3: lambda e: e.reciprocal(out=a3[:, :, 128:129], in_=a3[:, :, 128:129]))(), reads=[Bar_], writes=[Bar_])
            P.op("dve", tt(a3[:, :, 0:128], a3[:, :, 0:128], a3[:, :, 128:129].to_broadcast([128, 8, 128]), ALU.mult),
                 reads=[Bar_], writes=[Bar_])
            ssq_ = stat[:, 2 + 3 * par:3 + 3 * par]
            xj, Bxj = (xn, Bxn) if par == 0 else (xnB, BxnB)
            P.op("act", act(xj[:, 0:1024].rearrange("p (h d) -> p h d", d=128), a3[:, :, 0:128], AF.Square, accum=ssq_),
                 reads=[Bar_], writes=[Bxj, Bsta[par]])
            P.op("act", act(ssq_, ssq_, AF.Ln, scale=1.0 / 1024, bias=EPS), reads=[Bsta[par]], writes=[Bsta[par]])
            P.op("act", act(ssq_, ssq_, AF.Exp, scale=-0.5), reads=[Bsta[par]], writes=[Bsta[par]])
            P.op("dve", stt(onat.rearrange("p (h d) -> p h d", d=128), a3[:, :, 0:128], ssq_,
                            agrow[:].rearrange("p (h d) -> p h d", d=128), ALU.mult, ALU.mult),
                 reads=[Bar_, Bsta[par], Bag], writes=[Bonat])
            tv = pbb(par).rearrange("p (c t) -> p c t", t=128)
            P.op("pe", trs([(tv[:, h, :], onat[:, h * 128:(h + 1) * 128], ident[:]) for h in range(8)]),
                 reads=[Bonat, Bid], writes=[PB[par]])
            P.op("act", acp(mTa[:, :, tile * 128:(tile + 1) * 128], tv), reads=[PB[par]], writes=[BmTa[tile]])
        for dmb in range(4):
            wt, Bw = WQ3.get()
            for tile in range(ntile):
                pbi = (2, 3, 5, 6, 7)[ctr["pb"] % 5]
                ctr["pb"] += 1
                if tile < 4:
                    tsl = slice(tile * 128, tile * 128 + 128)
                    pt = pbf(pbi)
                    specs = [(pt, mTa[:, c, tsl], wt[:, c, :], c == 0, False) for c in range(8)]
                    specs += [(pt, mTh[:, c, tsl], wt[:, 8 + c, :], False, c == 7) for c in range(8)]
                    P.op("pe", mms(specs), reads=[BmTa[tile], BmTh, Bw], writes=[PB[pbi]])
                    P.op("act", acp(mix[:, tile, dmb * 512:(dmb + 1) * 512], pt), reads=[PB[pbi]], writes=[Bmix[tile]])
                else:
                    pt = pbf(pbi)[0:NS, :]
                    P.op("pe", mms([(pt, mTs[:, c, :], wt[:, c, :], c == 0, c == 15) for c in range(16)]),
                         reads=[BmTs, Bw], writes=[PB[pbi]])
                    P.op("act", acp(smix[0:NS, dmb * 512:(dmb + 1) * 512], pt), reads=[PB[pbi]], writes=[Bsmix])
        P.handoff([BmTa] + Bar_all, [Bh2T])
        P.dma("sp", dm(grow[:], g2_d[0].partition_broadcast(128)), writes=[Bgrow])
        def C1(tile, _):
            par = tile % 2
            xdst, Bxd = x3pair[par]
            if tile < 4:
                rows_, src, Bsrc, xsrc = 128, mix[:, tile, :], Bmix[tile], xh[HALO + o0 + tile * 128: HALO + o0 + tile * 128 + 128, :]
            else:
                rows_, src, Bsrc, xsrc = NS, smix[0:NS, :], Bsmix, xs
            P.dma("sp", dm(xdst[0:rows_, :], xsrc), writes=[Bxd])
            ssq_ = stat[0:rows_, 12 + par:13 + par]
            xj, Bxj = (xn, Bxn) if par == 0 else (xnB, BxnB)
            P.op("act", act(xj[0:rows_, :], src, AF.Square, accum=ssq_), reads=[Bsrc], writes=[Bxj, Bstc[par]])
            P.op("act", act(ssq_, ssq_, AF.Ln, scale=1.0 / D, bias=EPS), reads=[Bstc[par]], writes=[Bstc[par]])
            P.op("act", act(ssq_, ssq_, AF.Exp, scale=-0.5), reads=[Bstc[par]], writes=[Bstc[par]])
            P.op("dve", stt(src, src, ssq_, grow[0:rows_, :], ALU.mult, ALU.mult), reads=[Bsrc, Bstc[par], Bgrow], writes=[Bsrc])
            P.op("pool", tt(src, src, xdst[0:rows_, :], ALU.add), reads=[Bsrc, Bxd], writes=[Bsrc])
            return (rows_, src, Bsrc)

        def C2(tile, c_):
            rows_, src, Bsrc = c_
            if tile < 4:
                P.dma("sp", dm(x1scr[o0 + tile * 128:o0 + tile * 128 + 128, :], src), reads=[Bsrc], writes=[Bx1s[tile]], sembuf=Bsrc)
                return norm_T_a(src, Bsrc, 128, g3T, Bg3, h2T, Bh2T, tile * 128)
            return norm_T_a(src, Bsrc, NS, g3T, Bg3, h2Ts, Bh2Ts, 0)

        def C3(tile, sb_):
            sb_()
        pipeline([C1, C2, C3], ntile)
        P.handoff(Bmix + [Bxt3, BxnB], [BffT])
        for t in range(NGU):
            wt, Bw = WQ3.get()
            for j in range(2):
                fi = t * 2 + j
                npair = 3 if last else 4
                pg, pu = 2 * (fi % npair), 2 * (fi % npair) + 1
                P.op("pe", mms([(pbf(pg), wt[:, c, j * 128:(j + 1) * 128], h2T[:, c, :], c == 0, c == 15) for c in range(16)]),
                     reads=[Bh2T, Bw], writes=[PB[pg]])
                P.op("pe", mms([(pbf(pu), wt[:, c, 256 + j * 128:256 + (j + 1) * 128], h2T[:, c, :], c == 0, c == 15) for c in range(16)]),
                     reads=[Bh2T, Bw], writes=[PB[pu]])
                sg_, Bsg_ = sgv[fi % 2]
                P.op("act", act(sg_, pbf(pg), AF.Silu), reads=[PB[pg]], writes=[Bsg_])
                P.op("dve", tt(ffT[:, fi, :], sg_, pbf(pu), ALU.mult), reads=[Bsg_, PB[pu]], writes=[BffT])
                if last:
                    P.op("pe", mms([(pbf(6)[:, 0:NS], wt[:, c, j * 128:(j + 1) * 128], h2Ts[:, c, :], c == 0, c == 15) for c in range(16)]
                                   + [(pbf(6)[:, 8:8 + NS], wt[:, c, 256 + j * 128:256 + (j + 1) * 128], h2Ts[:, c, :], c == 0, c == 15) for c in range(16)]),
                         reads=[Bh2Ts, Bw], writes=[PB[6]])
                    P.op("act", act(sgs, pbf(6)[:, 0:NS], AF.Silu), reads=[PB[6]], writes=[Bsgs2])
                    P.op("dve", tt(ffTs[:, fi, :], sgs, pbf(6)[:, 8:8 + NS], ALU.mult), reads=[Bsgs2, PB[6]], writes=[BffTs])
        for b_ in Bffo:
            P.handoff([Bh2T], [b_])

        f_pre = [True]
        P.dma("sp", dm(xtb_t[:], x1scr[o0:o0 + 128, :]), reads=[Bx1s[0]], writes=[Bxtb])
        for si, (dmb, kq) in enumerate(dseq):
            wt, Bw = WQ3.get()
            for tile in range(ntile):
                if tile < 4:
                    bk = tile if (last or dmb % 2 == 0) else 4 + tile
                    pt = pbf(bk)
                    tsl = slice(tile * 128, tile * 128 + 128)
                    P.op("pe", mms([(pt, ffT[:, kq * 11 + c, tsl], wt[:, c, :], kq == 0 and c == 0, kq == 3 and c == 10) for c in range(11)]),
                         reads=[BffT, Bw], writes=[PB[bk]])
                    if kq == 3:
                        P.op("act" if tile % 2 == 0 else "dve",
                             (acp if tile % 2 == 0 else cp)(ffo[:, tile, dmb * 512:(dmb + 1) * 512], pt),
                             reads=[PB[bk]], writes=[Bffo[tile]])
                else:
                    pt = pbf(4)[0:NS, :]
                    P.op("pe", mms([(pt, ffTs[:, kq * 11 + c, :], wt[:, c, :], kq == 0 and c == 0, kq == 3 and c == 10) for c in range(11)]),
                         reads=[BffTs, Bw], writes=[PB[4]])
                    if kq == 3:
                        P.op("act", acp(sffo[0:NS, dmb * 512:(dmb + 1) * 512], pt), reads=[PB[4]], writes=[Bsffo])
        P.handoff([BffT], [Bx1r, BxnF])
        P.dma("sp", dm(grow[:], g4_d[0].partition_broadcast(128)), writes=[Bgrow])
        x1pair = [(xtb_t[:], Bxtb), (x1r, Bx1r)]

        def F1(tile, _):
            par = tile % 2
            if tile < 4:
                rows_, src, Bsrc = 128, ffo[:, tile, :], Bffo[tile]
                x1v, Bx1 = x1pair[par]
                if not (tile == 0 and f_pre[0]):
                    P.dma("sp", dm(x1v, x1scr[o0 + tile * 128:o0 + tile * 128 + 128, :]), reads=[Bx1s[tile]], writes=[Bx1])
                dst = y[o0 + tile * 128:o0 + tile * 128 + 128, :]
            else:
                rows_, src, Bsrc = NS, sffo[0:NS, :], Bsffo
                x1v, Bx1 = smix, Bsmix
                dst = ys
            ssq_ = stat[0:rows_, 14 + par:15 + par]
            xj, Bxj = (xn, Bxn) if par == 0 else (xnF, BxnF)
            P.op("act", act(xj[0:rows_, :], src, AF.Square, accum=ssq_), reads=[Bsrc], writes=[Bxj, Bstf[par]])
            P.op("act", act(ssq_, ssq_, AF.Ln, scale=1.0 / D, bias=EPS), reads=[Bstf[par]], writes=[Bstf[par]])
            P.op("act", act(ssq_, ssq_, AF.Exp, scale=-0.5), reads=[Bstf[par]], writes=[Bstf[par]])
            return (rows_, src, Bsrc, x1v, Bx1, dst, ssq_, par)

        def F2(tile, c_):
            rows_, src, Bsrc, x1v, Bx1, dst, ssq_, par = c_
            P.op("dve", stt(src, src, ssq_, grow[0:rows_, :], ALU.mult, ALU.mult), reads=[Bsrc, Bstf[par], Bgrow], writes=[Bsrc])
            P.op("pool", tt(src, src, x1v[0:rows_, :], ALU.add), reads=[Bsrc, Bx1], writes=[Bsrc])
            P.dma("sp", dm(dst, src), reads=[Bsrc])
        pipeline([F1, F2], ntile)
        P.handoff([Bffo[0]], [ars[0][1]])
        P.handoff([Bffo[0], Bffo[1]], [ars[1][1]])
        P.handoff([Bffo[1], Bffo[2]], BmTa)
        P.handoff([Bffo[2]], [b for (_, b) in onats])
        for b_ in Bmix + [Bxt3, BxnB]:
            P.handoff([Bx1r, BxnF], [b_])

    P.barrier()
    P.emit(st)
    st.close()
    return nc


_NC = None


def _consts(core):
    i = np.arange(128)
    mcur = (i[:, None] <= i[None, :]).astype(np.float32)
    mprev = (i[:, None] >= i[None, :]).astype(np.float32)
    bdm = ((i[:, None] <= i[None, :]) & ((i[:, None] // 64) == (i[None, :] // 64))).astype(np.float32)
    oneh = np.zeros((128, NS, NS), np.float32)
    for j in range(NS):
        oneh[:, j, j] = 1.0
    hv = np.full((128, 1), 0.0 if core == 0 else 1.0, np.float32)
    return dict(ident=np.eye(128, dtype=np.float32), mcur=mcur, mprev=mprev, bdm=bdm, oneh=oneh, hv=hv)


def kernel(x_prompt, x_sample, cache_win_k, cache_win_v, state_hgrn, norm_pre_mix, w_in,
           hg_lb_logits, attn_out_gain, hg_norm_gain, w_out, norm_post_mix, norm_pre_ffn,
           w_gate, w_up, w_down, norm_post_ffn):
    global _NC
    f = lambda a: np.ascontiguousarray(np.asarray(a, dtype=np.float32))
    xp = f(x_prompt)[0]
    xsm = f(x_sample)[:, 0, :]
    ckv = f(cache_win_k)[0].reshape(32, 2048, 1024)
    cvv = f(cache_win_v)[0].reshape(32, 2048, 1024)
    sth = f(state_hgrn)[0]
    shared = dict(
        w_in=f(w_in)[0], w_out=f(w_out)[0], w_gate=f(w_gate)[0], w_up=f(w_up)[0], w_down=f(w_down)[0],
        g1T=f(f(norm_pre_mix)[0].reshape(16, 128).T), g3T=f(f(norm_pre_ffn)[0].reshape(16, 128).T),
        g2=f(norm_post_mix), g4=f(norm_post_ffn), ag=f(attn_out_gain), hg=f(hg_norm_gain),
        lbl=f(f(hg_lb_logits).reshape(2, 8, 128).transpose(2, 0, 1)),
    )
    in_maps = []
    for c in range(NCORE):
        xhc = np.zeros((HALO + OWN, D), np.float32)
        if c > 0:
            xhc[0:HALO] = xp[(c - 1) * OWN:c * OWN]
        xhc[HALO:] = xp[c * OWN:(c + 1) * OWN]
        m = dict(shared)
        m.update(_consts(c))
        m.update(xh=xhc, xs=f(xsm[c * NS:(c + 1) * NS]), ck=f(ckv[c * NS:(c + 1) * NS]),
                 cv=f(cvv[c * NS:(c + 1) * NS]), st_in=f(sth[c * NS:(c + 1) * NS]))
        in_maps.append(m)
    if _NC is None:
        _NC = build()
    res = run_bass_kernel_spmd(_NC, in_maps, core_ids=list(range(NCORE)))
    R = res.results
    y_prompt = np.concatenate([R[c]["y"] for c in range(NCORE)], axis=0)[None]
    y_sample = np.concatenate([R[c]["ys"] for c in range(NCORE)], axis=0)[:, None, :]
    win_k = R[NCORE - 1]["wk"].reshape(1, 1, 2048, 8, 128)
    win_v = R[NCORE - 1]["wv"].reshape(1, 1, 2048, 8, 128)
    state_p = R[NCORE - 1]["sp_out"].reshape(1, 1, 8, 128, 128)
    ks = np.concatenate([R[c]["ks_o"] for c in range(NCORE)], axis=0).reshape(1, 32, 1, 8, 128)
    vs = np.concatenate([R[c]["vs_o"] for c in range(NCORE)], axis=0).reshape(1, 32, 1, 8, 128)
    ss = np.concatenate([R[c]["ss_o"] for c in range(NCORE)], axis=0).reshape(1, 32, 8, 128, 128)
    outs = (y_prompt, y_sample, win_k, win_v, state_p, ks, vs, ss)
    return tuple(np.ascontiguousarray(o, dtype=np.float32) for o in outs)
```

```python
import contextlib
import numpy as np
import concourse.bass as bass
import concourse.mybir as mybir
from concourse.bass_utils import run_bass_kernel_spmd

F32 = mybir.dt.float32
BF16 = mybir.dt.bfloat16
AF = mybir.ActivationFunctionType
ALU = mybir.AluOpType
AX = mybir.AxisListType

D = 2048
OWN = 2048
HALO = 2048
G = 512
NCORE = 8
INW = 7168
DFF = 5632
NS = 4
EPS = 1e-6
SCALE = 128 ** -0.5
ENGS = ("pe", "act", "dve", "pool", "sp")


class Buf:
    __slots__ = ("name", "w", "r", "dkey")

    def __init__(self, name):
        self.name = name
        self.w = None
        self.r = []
        self.dkey = None


class Prog:
    def __init__(self, nc):
        self.nc = nc
        self.ops = {e: [] for e in ENGS}
        self.cnt = {}
        self.waited = {e: {} for e in ENGS}
        self.dma_keys = []
        self.nbuf = 0
        self.pending = []

    def buf(self, name=None):
        self.nbuf += 1
        return Buf(name or f"b{self.nbuf}")

    @staticmethod
    def _flat(seq):
        out = []
        for b in seq:
            if isinstance(b, (list, tuple)):
                out.extend(Prog._flat(b))
            else:
                out.append(b)
        return out

    def defer(self, thunk):
        self.pending.append(thunk)

    def flush(self):
        p, self.pending = self.pending, []
        for t in p:
            t()

    def _deps(self, eng, reads, writes, nowaw=False):
        need = {}

        def add(tok):
            if tok is None:
                return
            k, v = tok
            if k == eng and eng == "pe":
                return
            if need.get(k, 0) < v:
                need[k] = v
        for b in reads:
            add(b.w)
        for b in writes:
            if not nowaw:
                add(b.w)
            for t in b.r:
                add(t)
        waits = []
        wd = self.waited[eng]
        for k, v in need.items():
            if wd.get(k, 0) < v:
                wd[k] = v
                waits.append((k, v))
        return waits

    def _commit(self, tok, reads, writes):
        for b in reads:
            b.r.append(tok)
            if len(b.r) > 64:
                mx = {}
                for k, v in b.r:
                    if mx.get(k, 0) < v:
                        mx[k] = v
                b.r = list(mx.items())
        for b in writes:
            b.w = tok
            b.r = []

    def op(self, eng, fn, reads=(), writes=()):
        reads, writes = self._flat(reads), self._flat(writes)
        waits = self._deps(eng, reads, writes)
        self.cnt[eng] = self.cnt.get(eng, 0) + 1
        tok = (eng, self.cnt[eng])
        self.ops[eng].append((waits, fn, (eng, 1)))
        self._commit(tok, reads, writes)
        return tok

    def dma(self, queue, fn, reads=(), writes=(), sembuf=None, nowaw=False):
        reads, writes = self._flat(reads), self._flat(writes)
        sb = sembuf or (writes[0] if writes else reads[0])
        if sb.dkey is None:
            sb.dkey = f"d{len(self.dma_keys)}"
            self.dma_keys.append(sb.dkey)
        key = sb.dkey
        waits = self._deps(queue, reads, writes, nowaw=nowaw)
        self.cnt[key] = self.cnt.get(key, 0) + 16
        tok = (key, self.cnt[key])
        self.ops[queue].append((waits, fn, (key, 16)))
        self._commit(tok, reads, writes)
        return tok

    def handoff(self, old, new):
        toks = []
        old, new = self._flat(old), self._flat(new)
        for b in old:
            if b.w is not None:
                toks.append(b.w)
            toks.extend(b.r)
        for b in new:
            b.r.extend(toks)

    def barrier(self):
        self.flush()
        for e in ENGS:
            waits = []
            wd = self.waited[e]
            for k, v in self.cnt.items():
                if k == e:
                    continue
                if wd.get(k, 0) < v:
                    wd[k] = v
                    waits.append((k, v))
            if waits:
                self.ops[e].append((waits, None, None))

    def emit(self, stack):
        nc = self.nc
        sems = {}
        for k in list(ENGS) + self.dma_keys:
            if k in self.cnt:
                sems[k] = stack.enter_context(nc.semaphore(f"s_{k}"))
        block = stack.enter_context(nc.Block())

        def run(engname):
            def body(e):
                for waits, fn, inc in self.ops[engname]:
                    for k, v in waits:
                        e.wait_ge(sems[k], v)
                    if fn is not None:
                        ins = fn(e)
                        ins.then_inc(sems[inc[0]], inc[1])
            return body

        block.tensor(run("pe"))
        block.scalar(run("act"))
        block.vector(run("dve"))
        block.gpsimd(run("pool"))
        block.sync(run("sp"))


class WQ:
    def __init__(self, nslots_fn):
        self.jobs = []
        self.tiles = []
        self.issued = 0
        self.k = 0
        self.nslots = nslots_fn

    def add(self, thunk):
        self.jobs.append(thunk)

    def prime(self):
        S = self.nslots()
        while self.issued < len(self.jobs) and self.issued <= self.k + S - 1:
            self.tiles.append(self.jobs[self.issued]())
            self.issued += 1

    def get(self):
        self.prime()
        t = self.tiles[self.k]
        self.tiles[self.k] = None
        self.k += 1
        return t


def mms(specs):
    def f(e):
        ins = None
        for (o, l, r, s, t) in specs:
            ins = e.matmul(o, lhsT=l, rhs=r, start=s, stop=t)
        return ins
    return f


def trs(specs):
    def f(e):
        ins = None
        for (o, i, idt) in specs:
            ins = e.transpose(out=o, in_=i, identity=idt)
        return ins
    return f


def act(out, in_, func, scale=1.0, bias=0.0, accum=None):
    if accum is None:
        return lambda e: e.activation(out=out, in_=in_, func=func, bias=bias, scale=scale)
    return lambda e: e.activation(out=out, in_=in_, func=func, bias=bias, scale=scale, accum_out=accum)


def tt(out, a, b, op):
    return lambda e: e.tensor_tensor(out=out, in0=a, in1=b, op=op)


def ts(out, a, s1, op0, s2=None, op1=None):
    if op1 is None:
        return lambda e: e.tensor_scalar(out=out, in0=a, scalar1=s1, scalar2=None, op0=op0)
    return lambda e: e.tensor_scalar(out=out, in0=a, scalar1=s1, scalar2=s2, op0=op0, op1=op1)


def stt(out, a, s, b, op0, op1):
    return lambda e: e.scalar_tensor_tensor(out=out, in0=a, scalar=s, in1=b, op0=op0, op1=op1)


def cp(out, in_):
    return lambda e: e.tensor_copy(out=out, in_=in_)


def acp(out, in_):
    return lambda e: e.copy(out=out, in_=in_)


def dm(out, in_):
    return lambda e: e.dma_start(out=out, in_=in_)


def dmnc(out, in_):
    return lambda e: e.dma_start(out=out, in_=in_, allow_slow_non_contiguous=True)


def build():
    nc = bass.Bass("TRN2", target_bir_lowering=False)

    def din(name, shape, dt=F32):
        return nc.dram_tensor(name, list(shape), dt, kind="ExternalInput").ap()

    def dout(name, shape, dt=F32):
        return nc.dram_tensor(name, list(shape), dt, kind="ExternalOutput").ap()

    def dscr(name, shape, dt):
        return nc.dram_tensor(name, list(shape), dt, kind="Internal").ap()

    xh = din("xh", [HALO + OWN, D])
    xs = din("xs", [NS, D])
    ck = din("ck", [NS, 2048, 1024])
    cv = din("cv", [NS, 2048, 1024])
    st_in = din("st_in", [NS, 8, 128, 128])
    w_in = din("w_in", [D, INW])
    w_out = din("w_out", [D, D])
    w_gate = din("w_gate", [D, DFF])
    w_up = din("w_up", [D, DFF])
    w_down = din("w_down", [DFF, D])
    g1T_d = din("g1T", [128, 16])
    g3T_d = din("g3T", [128, 16])
    g2_d = din("g2", [1, D])
    g4_d = din("g4", [1, D])
    ag_d = din("ag", [1, 1024])
    hg_d = din("hg", [1, 128])
    lbl_d = din("lbl", [128, 2, 8])
    ident_d = din("ident", [128, 128])
    mcur_d = din("mcur", [128, 128])
    mprev_d = din("mprev", [128, 128])
    bdm_d = din("bdm", [128, 128])
    oneh_d = din("oneh", [128, NS, NS])
    hv_d = din("hv", [128, 1])

    y = dout("y", [OWN, D])
    ys = dout("ys", [NS, D])
    wk = dout("wk", [OWN, 1024])
    wv = dout("wv", [OWN, 1024])
    sp_out = dout("sp_out", [8, 128, 128])
    ks_o = dout("ks_o", [NS, 1024])
    vs_o = dout("vs_o", [NS, 1024])
    ss_o = dout("ss_o", [NS, 8, 128, 128])

    qs = dscr("qs", [OWN + 16, 1024], BF16)
    kscr = dscr("kscr", [HALO + OWN + 16, 1024], BF16)
    vscr = dscr("vscr", [HALO + OWN + 16, 1040], BF16)
    rs = dscr("rs", [3, OWN + 16, 1040], F32)
    mhg = dscr("mhg", [128, 8, OWN], BF16)
    sproj = dscr("sproj", [NS, INW], F32)
    x1scr = dscr("x1scr", [OWN, D], F32)

    st = contextlib.ExitStack()
    P = Prog(nc)

    def sb(name, shape, dt):
        t = st.enter_context(nc.sbuf_tensor("sb_" + name, list(shape), dt))
        return t, P.buf(name)

    ps_all = st.enter_context(nc.psum_tensor("ps_all", [128, 8, 512], F32))
    PB = [[P.buf(f"pb{i}")] * 4 for i in range(8)]

    def pbf(i):
        return ps_all[:, i, :]

    def pbb(i):
        return ps_all[:, i, :].bitcast(BF16)

    CQ = P.buf("constq")
    ident_f, Bidf = sb("ident_f", [128, 128], F32)
    ident, Bid = sb("ident", [128, 128], BF16)
    mcur, Bmc = sb("mcur", [128, 128], F32)
    mprev, Bmp = sb("mprev", [128, 128], F32)
    bdm, Bbdm = sb("bdm", [128, 128], F32)
    oneh, Boh = sb("oneh", [128, NS, NS], F32)
    hv, Bhv = sb("hv", [128, 1], F32)
    g1T, Bg1 = sb("g1T", [128, 16], F32)
    g3T, Bg3 = sb("g3T", [128, 16], F32)
    agrow, Bag = sb("agrow", [128, 1024], F32)
    hgrow, Bhg = sb("hgrow", [128, 128], F32)
    lbl, Blbl = sb("lbl", [128, 2, 8], F32)
    lb, Blb = sb("lb", [128, 8], F32)
    oml, Boml = sb("oml", [128, 8], F32)
    maskA, BmA = sb("maskA", [128, 4, 256], BF16)
    maskB, BmB = sb("maskB", [128, 4, 256], BF16)
    rmask, Brm = sb("rmask", [128, G], F32)

    for (t, B, src) in ((ident_f, Bidf, ident_d), (mcur, Bmc, mcur_d), (mprev, Bmp, mprev_d),
                        (bdm, Bbdm, bdm_d), (oneh, Boh, oneh_d), (hv, Bhv, hv_d), (g1T, Bg1, g1T_d),
                        (g3T, Bg3, g3T_d), (lbl, Blbl, lbl_d)):
        P.dma("sp", dm(t[:], src), writes=[B], sembuf=CQ)
    P.dma("sp", dm(agrow[:], ag_d[0].partition_broadcast(128)), writes=[Bag], sembuf=CQ)
    P.dma("sp", dm(hgrow[:], hg_d[0].partition_broadcast(128)), writes=[Bhg], sembuf=CQ)
    _tot = (CQ.dkey, P.cnt[CQ.dkey])
    for B in (Bidf, Bmc, Bmp, Bbdm, Boh, Bhv, Bg1, Bg3, Blbl, Bag, Bhg):
        B.w = _tot
    P.op("dve", cp(ident[:], ident_f[:]), reads=[Bidf], writes=[Bid])
    P.op("dve", cp(maskA[:, :, 0:128], mcur[:].unsqueeze(1).to_broadcast([128, 4, 128])), reads=[Bmc], writes=[BmA])
    P.op("dve", cp(maskA[:, :, 128:256], mprev[:].unsqueeze(1).to_broadcast([128, 4, 128])), reads=[Bmp, BmA], writes=[BmA])
    P.op("dve", cp(maskB[:, :, 0:128], mcur[:].unsqueeze(1).to_broadcast([128, 4, 128])), reads=[Bmc], writes=[BmB])
    P.op("dve", ts(maskB[:, :, 128:256], mprev[:].unsqueeze(1).to_broadcast([128, 4, 128]), hv[:, 0:1], ALU.mult),
         reads=[Bmp, Bhv, BmB], writes=[BmB])
    mbA, BmbA = sb("mbA", [128, 2, 256], BF16)
    mbB, BmbB = sb("mbB", [128, 2, 256], BF16)
    for (mb_, Bmb_, msrc, Bmsrc) in ((mbA, BmbA, maskA, BmA), (mbB, BmbB, maskB, BmB)):
        P.op("dve", ts(mb_[:], msrc[:, 0:2, :], -1.0, ALU.add, 30000.0, ALU.mult), reads=[Bmsrc], writes=[Bmb_])
    P.op("pool", lambda e: e.memset(rmask[:], 1.0), writes=[Brm])
    P.op("pool", lambda e: e.memset(rmask[:].rearrange("p (c t) -> p c t", t=64)[:, :, 0:1], 0.0), reads=[Brm], writes=[Brm])
    P.op("dve", tt(lb[:], lbl[:, 0, :], lbl[:, 1, :], ALU.subtract), reads=[Blbl], writes=[Blb])
    P.op("act", act(lb[:], lb[:], AF.Sigmoid), reads=[Blb], writes=[Blb])
    P.op("dve", ts(oml[:], lb[:], -1.0, ALU.mult, 1.0, ALU.add), reads=[Blb], writes=[Boml])

    wbuf = []
    for i in range(2):
        t, B = sb(f"wbuf{i}", [128, 16, 512], BF16)
        wbuf.append((t, B))
    wctr = [0]

    def wload(src_ap, nchunk=16):
        t, B = wbuf[wctr[0] % len(wbuf)]
        wctr[0] += 1
        P.dma("pool", dm(t[:, 0:nchunk, :], src_ap.rearrange("(c p) n -> p c n", p=128)), writes=[B])
        return t, B

    def wstream(srcs, nchunk=16):
        S = len(wbuf)
        issued = 0
        tiles = []
        for k in range(len(srcs)):
            while issued < len(srcs) and issued <= k + S - 1:
                tiles.append(wload(srcs[issued], nchunk))
                issued += 1
            yield k, tiles[k]

    R1, BR1 = sb("R1", [128, 11264], F32)
    R2, BR2 = sb("R2", [128, 8192], F32)
    R3, BR3 = sb("R3", [128, 6400], F32)
    R4, BR4 = sb("R4", [128, 7424], F32)

    def view(reg, off_b, shape, dt):
        n = int(np.prod(shape[1:]))
        esz = 4 if dt == F32 else 2
        assert off_b % 4 == 0
        nf = (n * esz + 3) // 4
        ap = reg[:, off_b // 4: off_b // 4 + nf]
        if dt != F32:
            ap = ap.bitcast(dt)
        if len(shape) == 3:
            ap = ap.rearrange("p (a b) -> p a b", b=shape[2])
        return ap

    stat, Bstat = sb("stat", [128, 16], F32)
    xt = view(R4, 0, [128, D], F32); Bxt = P.buf("xt")
    xtb_t, Bxtb = sb("xtb", [128, D], F32)
    xts = [(xt, Bxt), (xtb_t[:], Bxtb)]
    xtctr = [0]
    xn0, Bxn0 = sb("xn", [128, D], BF16)
    xn1 = view(R4, 24576, [128, D], BF16); Bxn1 = P.buf("xn1")
    xn, Bxn = xn0, Bxn0
    xnctr = [0]
    hT = view(R4, 8192, [128, 16, G], BF16); BhT = P.buf("hT")
    hTs, BhTs = sb("hTs", [128, 16, NS], BF16)
    stg32 = [sb(f"stg32_{i}", [128, 512], F32) for i in range(2)]
    stg16 = [sb(f"stg16_{i}", [128, 512], BF16) for i in range(3)]
    vstg = [sb(f"vstg_{i}", [128, 4, 130], BF16) for i in range(2)]
    sstg = [sb(f"sstg_{i}", [NS, 512], F32) for i in range(1)]
    for (t, B) in vstg:
        P.op("pool", (lambda t: lambda e: e.memset(t[:], 1.0))(t), writes=[B])
    ctr = {"s32": 0, "s16": 0, "v": 0, "ss": 0, "pb": 0}

    def nxt(lst, key):
        r = lst[ctr[key] % len(lst)]
        ctr[key] += 1
        return r

    p1flag = [False]
    Bstn = [P.buf("stn0"), P.buf("stn1")]

    def sigmoid_to(dst, Bdst, src, Bsrc):
        P.op("act", act(dst, src, AF.Exp, scale=-1.0), reads=[Bsrc], writes=[Bdst])
        P.op("pool", ts(dst, dst, 1.0, ALU.add, 1.0, ALU.mult), reads=[Bdst], writes=[Bdst])
        P.op("dve", (lambda d_: lambda e: e.reciprocal(out=d_, in_=d_))(dst), reads=[Bdst], writes=[Bdst])

    def silu_to(dst, Bdst, src, Bsrc, tmp, Btmp_):
        P.op("act", act(tmp, src, AF.Exp, scale=-1.0), reads=[Bsrc], writes=[Btmp_])
        P.op("act", acp(dst, src), reads=[Bsrc], writes=[Bdst])
        P.op("pool", ts(tmp, tmp, 1.0, ALU.add, 1.0, ALU.mult), reads=[Btmp_], writes=[Btmp_])
        P.op("dve", (lambda d_: lambda e: e.reciprocal(out=d_, in_=d_))(tmp), reads=[Btmp_], writes=[Btmp_])
        P.op("pool", tt(dst, dst, tmp, ALU.mult), reads=[Bdst, Btmp_], writes=[Bdst])

    xn_alt = [None]

    def norm_T_a(src, Bsrc, rows, gT, BgT, dst, Bdst, col0, scale_eng="pool"):
        use_alt = (xn_alt[0] is not None) and (xnctr[0] % 2 == 1)
        (xn, Bxn) = xn_alt[0] if use_alt else (xn0, Bxn0)
        sc_ = xnctr[0] % 2 if xn_alt[0] is not None else 0
        xnctr[0] += 1
        ssq = stat[0:rows, 6 + sc_:7 + sc_]
        Bst_ = Bstn[sc_]
        P.op("act", act(xn[0:rows, :], src, AF.Square, accum=ssq), reads=[Bsrc], writes=[Bxn, Bst_])
        P.op("act", act(ssq, ssq, AF.Ln, scale=1.0 / D, bias=EPS), reads=[Bst_], writes=[Bst_])
        P.op("act", act(ssq, ssq, AF.Exp, scale=-0.5), reads=[Bst_], writes=[Bst_])
        if scale_eng == "pool":
            P.op("pool", ts(xn[0:rows, :], src, ssq, ALU.mult, 1.0, ALU.mult), reads=[Bsrc, Bst_], writes=[Bxn])
        else:
            P.op("act", (lambda xn=xn, ssq=ssq: lambda e: e.activation(out=xn[0:rows, :], in_=src, func=AF.Copy, scale=ssq))(),
                 reads=[Bsrc, Bst_], writes=[Bxn])

        def stage_b():
            for half in range(2):
                pv = pbb(half).rearrange("p (c t) -> p c t", t=128)
                P.op("pe", trs([(pv[:, c, 0:rows], xn[0:rows, (half * 8 + c) * 128:(half * 8 + c + 1) * 128],
                                 ident[0:rows, 0:rows]) for c in range(8)]),
                     reads=[Bxn, Bid], writes=[PB[half]])
                P.op("dve", tt(dst[:, half * 8:half * 8 + 8, col0:col0 + rows], pv[:, :, 0:rows],
                               gT[:, half * 8:half * 8 + 8].unsqueeze(2).to_broadcast([128, 8, rows]), ALU.mult),
                     reads=[PB[half], BgT], writes=[Bdst])
        return stage_b

    def norm_T(*a, **k):
        norm_T_a(*a, **k)()

    def pipeline(stages, n):
        carry = {}
        K = len(stages)
        for step in range(n + K - 1):
            for k in range(K):
                t = step - k
                if 0 <= t < n:
                    carry[t] = stages[k](t, carry.get(t))

    vhg = view(R1, 0, [128, 4, 1024], BF16); Bvhg = P.buf("vhg")
    qsil = view(R1, 8192, [128, 4, G], F32); Bqsil = [P.buf(f"qsil{i}") for i in range(4)]
    qt = view(R1, 16384, [128, 8, G], BF16); Bqt = P.buf("qt")
    kt = view(R1, 24576, [128, 8, G], BF16); Bkt = P.buf("kt")
    kdec = view(R1, 32768, [128, 8, G], BF16); Bkdec = P.buf("kdec")
    gsil = view(R2, 0, [128, 8, G], BF16); Bgsil = P.buf("gsil")
    mst = view(R2, 8192, [128, 8, G], BF16); Bmst = P.buf("mst")
    tmpv = [view(R2, 16384 + i * 2048, [128, G], F32) for i in range(7)]
    Btmp = [P.buf(f"tmp{i}") for i in range(7)]
    Sst = view(R3, 0, [128, 8, 128], F32); BS = [P.buf(f"S{h}") for h in range(8)]
    Sbf = view(R3, 4096, [128, 8, 128], BF16); BSbf = [P.buf(f"Sbf{h}") for h in range(8)]
    elast = view(R3, 6144, [128, 8, 8], F32); Bel = P.buf("elast")
    kdT = view(R3, 6656, [128, 8, 128], BF16); BkdT = [P.buf(f"kdT{h}") for h in range(8)]
    aTm = view(R3, 8704, [128, 8, 128], BF16); BaTm = [P.buf(f"aTm{h}") for h in range(8)]
    ohg = view(R3, 10752, [128, 8, 128], F32); Bohg = P.buf("ohg")
    sq = view(R3, 14848, [128, 8, 128], F32); Bsq = P.buf("sq")
    onb = view(R3, 18944, [128, 8, 128], BF16); Bonb = P.buf("onb")
    onbs = [(onb, Bonb), (view(R2, 30720, [128, 8, 128], BF16), P.buf("onb2"))]
    pend_tail = []
    chunk_ctr = [0]
    st8 = view(R3, 20992, [128, 8], F32); Bst8 = P.buf("st8")
    sigs = [view(R3, 21504, [128, G], F32), tmpv[6], view(R1, 40960, [128, G], F32), view(R1, 43008, [128, G], F32)]
    Bsigs = [P.buf("sig0"), Btmp[6], P.buf("sig2"), P.buf("sig3")]
    qsil_s = view(R3, 23552, [128, 8, NS], F32); Bqss = P.buf("qsil_s")
    sig_s = view(R3, 23680, [128, 8, NS], F32); Bsgs = P.buf("sig_s")

    P.op("pool", lambda e: e.memset(Sst, 0.0), writes=BS)
    P.op("pool", lambda e: e.memset(Sbf, 0.0), writes=BSbf)

    own_blocks = [0, 1, 2, 3, 4, 5, 10, 11, 6, 8, 7, 9, 12, 13]
    halo_blocks = [2, 3, 4, 5, 8, 10, 9, 11]
    prep_queue = []
    xpre = []
    apre = []
    TOKM = (0, 1, 2, 3, 4, 5, 10, 11)

    def w_in_blk(b):
        return w_in[:, b * 512:(b + 1) * 512]

    def hgrn_prep(hd, sig_ap, Bsig, own):
        tf, tk, tg, tc, tE, tEi = tmpv[0:6]
        Bf, Bk, Bg_, Bc, BE, BEi = Btmp[0:6]
        P.op("dve", ts(tf, sig_ap, oml[:, hd:hd + 1], ALU.mult, lb[:, hd:hd + 1], ALU.add),
             reads=[Bsig, Boml, Blb], writes=[Bf])
        P.op("pool", ts(tk, tf, -1.0, ALU.mult, 1.0, ALU.add), reads=[Bf], writes=[Bk])
        P.op("act", act(tg, tf, AF.Ln), reads=[Bf], writes=[Bg_])
        P.op("dve", lambda e: e.tensor_tensor_scan(out=tc, data0=rmask[:], data1=tg, initial=0.0,
                                                    op0=ALU.mult, op1=ALU.add),
             reads=[Brm, Bg_], writes=[Bc])
        P.op("act", act(tE, tc, AF.Exp), reads=[Bc], writes=[BE])
        P.op("act", act(tEi, tc, AF.Exp, scale=-1.0), reads=[Bc], writes=[BEi])
        P.op("dve", cp(elast[:, hd, :], tE.rearrange("p (c t) -> p c t", t=64)[:, :, 63]),
             reads=[BE], writes=[Bel])
        P.op("pool", tt(kt[:, hd, :], tk, tEi, ALU.mult), reads=[Bk, BEi], writes=[Bkt])
        P.op("dve", tt(kdec[:, hd, :].rearrange("p (c t) -> p c t", t=64),
                       kt[:, hd, :].rearrange("p (c t) -> p c t", t=64),
                       elast[:, hd, :].unsqueeze(2).to_broadcast([128, 8, 64]), ALU.mult),
             reads=[Bkt, Bel], writes=[Bkdec])
        if own:
            P.op("pool", tt(qt[:, hd, :], qsil[:, hd % 4, :], tE, ALU.mult), reads=[Bqsil[hd % 4], BE], writes=[Bqt])

    def hgrn_prep_tile(gi, own, tile):
        tsl = slice(tile * 128, tile * 128 + 128)
        kdv = pbb(0).rearrange("p (c t) -> p c t", t=128)
        P.op("pe", trs([(kdv[:, hd, :], kdec[:, hd, tsl], ident[:]) for hd in range(8)]),
             reads=[Bkdec, Bid], writes=[PB[0]])
        P.op("act", acp(kdT[:], kdv), reads=[PB[0]], writes=BkdT)
        if own:
            for half in range(2):
                P.op("pe", mms([(pbf(1)[:, r4 * 128:(r4 + 1) * 128], kt[:, half * 4 + r4, tsl], qt[:, half * 4 + r4, tsl], True, True)
                                for r4 in range(4)]), reads=[Bkt, Bqt], writes=[PB[1]])
                P.op("dve", tt(aTm[:, half * 4:half * 4 + 4, :], pbf(1).rearrange("p (a b) -> p a b", b=128),
                               bdm[:].unsqueeze(1).to_broadcast([128, 4, 128]), ALU.mult),
                     reads=[PB[1], Bbdm], writes=BaTm[half * 4:half * 4 + 4])

    def hgrn_chunk(gi, own, tile, c2):
        hgrn_flush_tail(keep=1)
        pb_ = 64 * c2
        ch = tile * 2 + c2
        csl = slice(tile * 128 + pb_, tile * 128 + pb_ + 64)
        if own:
            for half in range(2):
                specs = []
                for r4 in range(4):
                    hd = half * 4 + r4
                    oreg = pbf(5)[0:64, r4 * 128:(r4 + 1) * 128]
                    specs.append((oreg, qt[:, hd, csl], Sbf[:, hd, :], True, False))
                    specs.append((oreg, aTm[pb_:pb_ + 64, hd, pb_:pb_ + 64],
                                  vhg[pb_:pb_ + 64, tile, hd * 128:(hd + 1) * 128], False, True))
                P.op("pe", mms(specs), reads=[Bqt, BSbf[half * 4:half * 4 + 4], BaTm[half * 4:half * 4 + 4], Bvhg], writes=[PB[5]])
                P.op("act", acp(ohg[0:64, half * 4:half * 4 + 4, :], pbf(5)[0:64, :].rearrange("p (a b) -> p a b", b=128)),
                     reads=[PB[5]], writes=[Bohg])
        for half in range(2):
            bank = 6 + half
            hs = slice(half * 4, half * 4 + 4)
            P.op("pe", mms([(pbf(bank)[:, r4 * 128:(r4 + 1) * 128], kdT[pb_:pb_ + 64, half * 4 + r4, :],
                             vhg[pb_:pb_ + 64, tile, (half * 4 + r4) * 128:(half * 4 + r4 + 1) * 128], True, True) for r4 in range(4)]),
                 reads=[BkdT[half * 4:half * 4 + 4], Bvhg], writes=[PB[bank]])
            P.op("dve", tt(Sst[:, hs, :], Sst[:, hs, :], elast[:, hs, ch:ch + 1].to_broadcast([128, 4, 128]), ALU.mult),
                 reads=[BS[half * 4:half * 4 + 4], Bel], writes=BS[half * 4:half * 4 + 4])
            P.op("dve", tt(Sst[:, hs, :], Sst[:, hs, :], pbf(bank).rearrange("p (a b) -> p a b", b=128), ALU.add),
                 reads=[BS[half * 4:half * 4 + 4], PB[bank]], writes=BS[half * 4:half * 4 + 4])
            P.op("pool", cp(Sbf[:, hs, :], Sst[:, hs, :]), reads=BS[half * 4:half * 4 + 4], writes=BSbf[half * 4:half * 4 + 4])
        if own:
            o64 = ohg[0:64]
            s64 = sq[0:64]
            P.op("act", act(s64, o64, AF.Square), reads=[Bohg], writes=[Bsq])
            P.op("dve", lambda e, s64=s64: e.tensor_reduce(out=st8[0:64, :], in_=s64, axis=AX.X, op=ALU.add),
                 reads=[Bsq], writes=[Bst8])
            P.op("act", act(st8[0:64, :], st8[0:64, :], AF.Ln, scale=1.0 / 128, bias=EPS), reads=[Bst8], writes=[Bst8])
            P.op("act", act(st8[0:64, :], st8[0:64, :], AF.Exp, scale=-0.5), reads=[Bst8], writes=[Bst8])
            P.op("dve", tt(s64, o64, st8[0:64, :].unsqueeze(2).to_broadcast([64, 8, 128]), ALU.mult),
                 reads=[Bohg, Bst8], writes=[Bsq])
            onb_, Bonb_ = onbs[chunk_ctr[0] % 2]
            chunk_ctr[0] += 1
            P.op("dve", tt(onb_[0:64], s64, hgrow[0:64, :].unsqueeze(1).to_broadcast([64, 8, 128]), ALU.mult),
                 reads=[Bsq, Bhg], writes=[Bonb_])

            def tail(onb_=onb_, Bonb_=Bonb_, csl=csl):
                tv = pbb(0).rearrange("p (c t) -> p c t", t=128)
                P.op("pe", trs([(tv[:, hd, 0:64], onb_[0:64, hd, :], ident[0:64, 0:64]) for hd in range(8)]),
                     reads=[Bonb_, Bid], writes=[PB[0]])
                P.op("dve", tt(mst[:, :, csl], tv[:, :, 0:64], gsil[:, :, csl], ALU.mult),
                     reads=[PB[0], Bgsil], writes=[Bmst])
            pend_tail.append(tail)

    def hgrn_flush_tail(keep=0):
        while len(pend_tail) > keep:
            pend_tail.pop(0)()

    def hgrn_items(gi):
        own = gi >= 4
        items = []
        for tile in range(4):
            items.append((lambda tile=tile: hgrn_prep_tile(gi, own, tile)))
            for c2 in range(2):
                items.append((lambda tile=tile, c2=c2: hgrn_chunk(gi, own, tile, c2)))
        if own:
            o0 = (gi - 4) * G
            items.append(hgrn_flush_tail)
            items.append((lambda: P.dma("sp", dm(mhg[:, :, o0:o0 + G], mst), reads=[Bmst])))
        return items

    p1flag[0] = True
    xn_alt[0] = (xn1, Bxn1)
    WQ1 = WQ(lambda: len(wbuf))
    for gi in range(8):
        for b_ in (own_blocks if gi >= 4 else halo_blocks):
            WQ1.add((lambda b_=b_: wload(w_in_blk(b_))))
    for gi in range(8):
        own = gi >= 4
        last = gi == 7
        r0 = gi * G
        o0 = (gi - 4) * G
        blocks = own_blocks if own else halo_blocks
        def A1(tile, _):
            if apre:
                return apre.pop(0)
            if xpre:
                xt_, Bxt_ = xpre.pop(0)
            else:
                xt_, Bxt_ = xts[xtctr[0] % 2]
                xtctr[0] += 1
                P.dma("act", dm(xt_, xh[r0 + tile * 128: r0 + tile * 128 + 128, :]), writes=[Bxt_])
            return norm_T_a(xt_, Bxt_, 128, g1T, Bg1, hT, BhT, tile * 128)

        def A2(tile, sb_):
            sb_()
        pipeline([A1, A2], 4)
        if last:
            P.dma("act", dm(xt[0:NS, :], xs), writes=[Bxt])
            norm_T(xt[0:NS, :], Bxt, NS, g1T, Bg1, hTs, BhTs, 0)
        citems = hgrn_items(gi - 1) if gi > 0 else []
        nsafe = 6 if own else 4
        n_items0 = len(citems)
        ss_state = [0, 0]

        def spread_items():
            ss_state[0] += 1
            if not citems:
                return
            tgt = int((ss_state[0] - 1) * n_items0 / max(1, nsafe * 4 - 5)) + (1 if ss_state[0] > 1 else 0)
            while citems and ss_state[1] < tgt:
                citems.pop(0)()
                ss_state[1] += 1
        for bi, blk in enumerate(blocks):
            if bi == len(blocks) - 2 and gi + 1 < 8:
                for t_ in range(2):
                    xt_, Bxt_ = xpre.pop(0)
                    apre.append(norm_T_a(xt_, Bxt_, 128, g1T, Bg1, hT, BhT, t_ * 128))
            if bi == 2 and gi + 1 < 8:
                for t_ in range(2):
                    xt_, Bxt_ = xts[xtctr[0] % 2]
                    xtctr[0] += 1
                    rr = (gi + 1) * G + t_ * 128
                    P.dma("act", dm(xt_, xh[rr: rr + 128, :]), writes=[Bxt_])
                    xpre.append((xt_, Bxt_))
            wt, Bw = WQ1.get()
            hgrn_flush_tail()
            if bi >= nsafe:
                while citems:
                    citems.pop(0)()
            if blk in TOKM:
                for tile in range(4 + (1 if last else 0)):
                    if prep_queue:
                        prep_queue.pop(0)()
                    spread_items()
                    pbi = (2 + (ctr["pb"] % 2)) if last else (2 + (ctr["pb"] % 3))
                    ctr["pb"] += 1
                    if tile < 4:
                        M = 128
                        lh = lambda c: hT[:, c, tile * 128:(tile + 1) * 128]
                        Bl = BhT
                    else:
                        M = NS
                        lh = lambda c: hTs[:, c, :]
                        Bl = BhTs
                    pt = pbf(pbi)[0:M, :]
                    P.op("pe", mms([(pt, lh(c), wt[:, c, :], c == 0, c == 15) for c in range(16)]),
                         reads=[Bl, Bw], writes=[PB[pbi]])
                    rows = slice(r0 + tile * 128, r0 + tile * 128 + 128)
                    orow = slice(o0 + tile * 128, o0 + tile * 128 + 128)
                    if tile == 4:
                        s_, Bs_ = nxt(sstg, "ss")
                        P.op("act", acp(s_[:], pt), reads=[PB[pbi]], writes=[Bs_])
                        P.dma("sp", dm(sproj[:, blk * 512:(blk + 1) * 512], s_[:]), reads=[Bs_])
                    elif blk in (0, 1):
                        s_, Bs_ = nxt(stg16, "s16")
                        P.op("act", acp(s_[:], pt), reads=[PB[pbi]], writes=[Bs_])
                        P.dma("sp", dm(qs[orow, blk * 512:(blk + 1) * 512], s_[:]), reads=[Bs_])
                    elif blk in (2, 3):
                        s_, Bs_ = nxt(stg16, "s16")
                        if own:
                            f_, Bf_ = nxt(stg32, "s32")
                            P.op("act", acp(f_[:], pt), reads=[PB[pbi]], writes=[Bf_])
                            P.dma("sp", dm(wk[orow, (blk - 2) * 512:(blk - 1) * 512], f_[:]), reads=[Bf_])
                            P.op("pool", cp(s_[:], f_[:]), reads=[Bf_], writes=[Bs_])
                        else:
                            P.op("act", acp(s_[:], pt), reads=[PB[pbi]], writes=[Bs_])
                        P.dma("sp", dm(kscr[rows, (blk - 2) * 512:(blk - 1) * 512], s_[:]), reads=[Bs_])
                    elif blk in (4, 5):
                        v_, Bv_ = nxt(vstg, "v")
                        if own:
                            f_, Bf_ = nxt(stg32, "s32")
                            P.op("act", acp(f_[:], pt), reads=[PB[pbi]], writes=[Bf_])
                            P.dma("sp", dm(wv[orow, (blk - 4) * 512:(blk - 3) * 512], f_[:]), reads=[Bf_])
                            P.op("pool", cp(v_[:, :, 0:128], f_[:].rearrange("p (h d) -> p h d", d=128)), reads=[Bf_], writes=[Bv_])
                        else:
                            P.op("act", acp(v_[:, :, 0:128], pt.rearrange("p (h d) -> p h d", d=128)), reads=[PB[pbi]], writes=[Bv_])
                        P.dma("sp", dm(vscr[rows, (blk - 4) * 520:(blk - 3) * 520], v_[:].rearrange("p h d -> p (h d)")), reads=[Bv_])
                    else:
                        P.op("dve", cp(vhg[:, tile, (blk - 10) * 512:(blk - 9) * 512], pt), reads=[PB[pbi]], writes=[Bvhg])
            else:
                fm_pending = []
                for j in range(4):
                    if prep_queue:
                        prep_queue.pop(0)()
                    spread_items()
                    pbi = (2 + (ctr["pb"] % 2)) if last else (2 + (ctr["pb"] % 3))
                    ctr["pb"] += 1
                    pt = pbf(pbi)
                    P.op("pe", mms([(pt, wt[:, c, j * 128:(j + 1) * 128], hT[:, c, :], c == 0, c == 15) for c in range(16)]),
                         reads=[BhT, Bw], writes=[PB[pbi]])
                    if blk in (6, 7):
                        P.op("act", act(qsil[:, j, :], pt, AF.Silu), reads=[PB[pbi]], writes=[Bqsil[j]])
                    elif blk in (8, 9):
                        hd = (blk - 8) * 4 + j
                        sg_ = sigs[j]
                        P.op("act", act(sg_, pt, AF.Sigmoid), reads=[PB[pbi]], writes=[Bsigs[j]])
                        fm_pending.append((hd, sg_, Bsigs[j]))
                    else:
                        hd = (blk - 12) * 4 + j
                        P.op("act", act(gsil[:, hd, :], pt, AF.Silu), reads=[PB[pbi]], writes=[Bgsil])
                    if last and blk in (6, 7, 8, 9):
                        hd = ((blk - 6) % 2) * 4 + j if blk in (6, 7) else (blk - 8) * 4 + j
                        if blk in (6, 7):
                            hd = (blk - 6) * 4 + j
                        pts = pbf(4)[:, 0:NS]
                        P.op("pe", mms([(pts, wt[:, c, j * 128:(j + 1) * 128], hTs[:, c, :], c == 0, c == 15) for c in range(16)]),
                             reads=[BhTs, Bw], writes=[PB[4]])
                        if blk in (6, 7):
                            P.op("act", act(qsil_s[:, hd, :], pts, AF.Silu), reads=[PB[4]], writes=[Bqss])
                        else:
                            P.op("act", act(sig_s[:, hd, :], pts, AF.Sigmoid), reads=[PB[4]], writes=[Bsgs])
                    if last and blk in (12, 13):
                        if j == 0:
                            pts = pbf(4)[0:NS, :]
                            P.op("pe", mms([(pts, hTs[:, c, :], wt[:, c, :], c == 0, c == 15) for c in range(16)]),
                                 reads=[BhTs, Bw], writes=[PB[4]])
                            s_, Bs_ = nxt(sstg, "ss")
                            P.op("act", acp(s_[:], pts), reads=[PB[4]], writes=[Bs_])
                            P.dma("sp", dm(sproj[:, blk * 512:(blk + 1) * 512], s_[:]), reads=[Bs_])
                for (hd_, sg_, Bsg_) in fm_pending:
                    prep_queue.append((lambda hd_=hd_, sg_=sg_, Bsg_=Bsg_, own=own: hgrn_prep(hd_, sg_, Bsg_, own)))
    while prep_queue:
        prep_queue.pop(0)()
    for it_ in hgrn_items(7):
        it_()

    p1flag[0] = False
    xn_alt[0] = None
    P.dma("sp", dm(sp_out.rearrange("h k v -> k h v"), Sst), reads=BS)

    P.barrier()

    def wload_gu(t):
        tl, B = wbuf[wctr[0] % len(wbuf)]
        wctr[0] += 1
        P.dma("pool", dm(tl[:, :, 0:256], w_gate[:, t * 256:(t + 1) * 256].rearrange("(c p) n -> p c n", p=128)), writes=[B])
        P.dma("pool", dm(tl[:, :, 256:512], w_up[:, t * 256:(t + 1) * 256].rearrange("(c p) n -> p c n", p=128)), writes=[B], nowaw=True)
        return tl, B

    def wd_blk(dmb, kq):
        return w_down[kq * 1408:(kq + 1) * 1408, dmb * 512:(dmb + 1) * 512]
    NGU = DFF // 256
    dseq = [(dmb, kq) for dmb in range(4) for kq in range(4)]
    WQ3 = WQ(lambda: len(wbuf))
    for og in range(4):
        for d_ in range(4):
            WQ3.add((lambda d_=d_: wload(w_out[:, d_ * 512:(d_ + 1) * 512])))
        for t_ in range(NGU):
            WQ3.add((lambda t_=t_: wload_gu(t_)))
        for sq_ in dseq:
            WQ3.add((lambda sq_=sq_: wload(wd_blk(*sq_), nchunk=11)))

    qbc = view(R1, 0, [128, 1024], F32); Bqbc = P.buf("qbc")
    kcs = [(view(R1, 4096, [128, 1024], F32), P.buf("kc0")), (view(R2, 16384, [128, 1024], F32), P.buf("kc1"))]
    vcs = [(view(R1, 8192, [128, 1024], F32), P.buf("vc0")), (view(R2, 20480, [128, 1024], F32), P.buf("vc1"))]
    prod = view(R1, 12288, [128, 8, 128], F32); Bprod = P.buf("prod")
    s8s = [(view(R1, 16384, [128, 8], F32), P.buf("s8a")), (view(R1, 16384 + 32, [128, 8], F32), P.buf("s8b"))]
    srow = view(R1, 16384 + 64, [128, 5 * 1024], F32); Bsrow = P.buf("srow")
    mrow = view(R1, 16384 + 64 + 20480, [128, 1056], F32); Bmrow = P.buf("mrow")
    Sj = view(R2, 0, [128, 8, 128], F32); BSj = P.buf("Sj")
    Sn = view(R2, 4096, [128, 8, 128], F32); BSn = P.buf("Sn")
    vb = view(R2, 8192, [128, 8, 128], F32); Bvb = P.buf("vb")
    fj = view(R2, 12288, [128, 8], F32); Bfj = P.buf("fj")
    kj = view(R2, 12288 + 64, [128, 8], F32); Bkj = P.buf("kj")
    mrgb = view(R2, 24576, [128, D], BF16); Bmrgb = P.buf("mrgb")
    mTs, BmTs = sb("mTs", [128, 16, NS], BF16)
    Bp15b = P.buf("p15b")
    A = [Bqbc, Bprod, Bsrow, Bmrow, BSj, BSn, Bvb, Bfj, Bkj, Bmrgb, Bp15b] + [b for (_, b) in kcs + vcs + s8s]

    P.dma("sp", dm(srow[0:NS, 0:3072], sproj[:, 0:3072]), writes=[Bsrow])
    P.dma("sp", dm(srow[0:NS, 3072:5120], sproj[:, 5120:7168]), writes=[Bsrow], nowaw=True)
    P.dma("sp", dm(ks_o, srow[0:NS, 1024:2048]), reads=[Bsrow])
    P.dma("sp", dm(vs_o, srow[0:NS, 2048:3072]), reads=[Bsrow])
    num_lo = pbf(0)[0:NS, :]
    num_hi = pbf(1)[0:NS, :]
    den_p = pbf(2)[0:NS, 0:8]
    pats = [(1920, 1), (1536, 4), (0, 16)]
    nacc = 0
    for j in range(NS):
        P.dma("sp", dm(qbc, sproj[j, 0:1024].partition_broadcast(128)), writes=[Bqbc])
        for (start_, dil) in pats:
            (kc, Bkc), (vc, Bvc), (s8, Bs8) = kcs[nacc % 2], vcs[nacc % 2], s8s[nacc % 2]
            krows = ck[j, start_:start_ + 128 * dil, :].rearrange("(i s) n -> i s n", s=dil)[:, 0, :]
            vrows = cv[j, start_:start_ + 128 * dil, :].rearrange("(i s) n -> i s n", s=dil)[:, 0, :]
            P.dma("sp", dm(kc, krows), writes=[Bkc])
            P.dma("act", dm(vc, vrows), writes=[Bvc])
            kc3 = kc.rearrange("p (h d) -> p h d", d=128)
            P.op("dve", tt(kc3, kc3, qbc.rearrange("p (h d) -> p h d", d=128), ALU.mult), reads=[Bkc, Bqbc], writes=[Bkc])
            P.op("dve", (lambda s8=s8, kc3=kc3: lambda e: e.tensor_reduce(out=s8, in_=kc3, axis=AX.X, op=ALU.add))(),
                 reads=[Bkc], writes=[Bs8])
            P.op("act", act(s8, s8, AF.Exp, scale=SCALE), reads=[Bs8], writes=[Bs8])
            vc3 = vc.rearrange("p (h d) -> p h d", d=128)
            P.op("pool", tt(vc3, vc3, s8.unsqueeze(2).to_broadcast([128, 8, 128]), ALU.mult), reads=[Bvc, Bs8], writes=[Bvc])
            first = nacc == 0
            lastm = nacc == NS * 3 - 1
            P.op("pe", mms([(num_lo, oneh[:, j, :], vc[:, 0:512], first, lastm),
                            (num_hi, oneh[:, j, :], vc[:, 512:1024], first, lastm),
                            (den_p, oneh[:, j, :], s8, first, lastm)]),
                 reads=[Bvc, Bs8, Boh], writes=[PB[0], PB[1], PB[2]])
            nacc += 1
    P.op("act", acp(mrow[0:NS, 0:512], num_lo), reads=[PB[0]], writes=[Bmrow])
    P.op("act", acp(mrow[0:NS, 512:1024], num_hi), reads=[PB[1], Bmrow], writes=[Bmrow])
    P.op("act", acp(mrow[0:NS, 1024:1032], den_p), reads=[PB[2], Bmrow], writes=[Bmrow])
    q4 = srow[0:NS, 0:1024].rearrange("p (h d) -> p h d", d=128)
    k4 = srow[0:NS, 1024:2048].rearrange("p (h d) -> p h d", d=128)
    v4 = srow[0:NS, 2048:3072].rearrange("p (h d) -> p h d", d=128)
    p4 = prod[0:NS]
    (s8, Bs8) = s8s[0]
    e4 = s8[0:NS, :]
    W4 = [Bprod, Bs8, Bmrow]
    P.op("dve", tt(p4, q4, k4, ALU.mult), reads=[Bsrow], writes=[Bprod])
    P.op("dve", lambda e: e.tensor_reduce(out=e4, in_=p4, axis=AX.X, op=ALU.add), reads=[Bprod], writes=[Bs8])
    P.op("act", act(e4, e4, AF.Exp, scale=SCALE), reads=[Bs8], writes=[Bs8])
    P.op("dve", ts(e4, e4, 3.0, ALU.mult), reads=[Bs8], writes=[Bs8])
    P.op("dve", tt(p4, v4, e4.unsqueeze(2).to_broadcast([NS, 8, 128]), ALU.mult), reads=[Bsrow, Bs8], writes=[Bprod])
    m4 = mrow[0:NS, 0:1024].rearrange("p (h d) -> p h d", d=128)
    d4 = mrow[0:NS, 1024:1032]
    P.op("dve", tt(m4, m4, p4, ALU.add), reads=W4, writes=[Bmrow])
    P.op("dve", tt(d4, d4, e4, ALU.add), reads=W4, writes=[Bmrow])
    P.op("dve", lambda e: e.reciprocal(out=d4, in_=d4), reads=[Bmrow], writes=[Bmrow])
    P.op("dve", tt(m4, m4, d4.unsqueeze(2).to_broadcast([NS, 8, 128]), ALU.mult), reads=[Bmrow], writes=[Bmrow])
    ssq = stat[0:NS, 1:2]
    P.op("act", act(p4.rearrange("p h d -> p (h d)"), mrow[0:NS, 0:1024], AF.Square, accum=ssq), reads=[Bmrow], writes=[Bprod, Bstat])
    P.op("act", act(ssq, ssq, AF.Ln, scale=1.0 / 1024, bias=EPS), reads=[Bstat], writes=[Bstat])
    P.op("act", act(ssq, ssq, AF.Exp, scale=-0.5), reads=[Bstat], writes=[Bstat])
    P.op("dve", stt(mrgb[0:NS, 0:1024], mrow[0:NS, 0:1024], ssq, agrow[0:NS, :], ALU.mult, ALU.mult),
         reads=[Bmrow, Bstat, Bag], writes=[Bmrgb])
    o_lo = pbf(3)[0:NS, :]
    o_hi = pbf(4)[0:NS, :]
    for j in range(NS):
        P.dma("sp", dm(Sj, st_in[j].rearrange("h k v -> k h v")), writes=[BSj])
        P.dma("act", dm(vb.rearrange("p h d -> p (h d)"), sproj[j, 5120:6144].partition_broadcast(128)), writes=[Bvb])
        P.op("dve", tt(fj, sig_s[:, :, j], oml[:], ALU.mult), reads=[Bsgs, Boml], writes=[Bfj])
        P.op("dve", tt(fj, fj, lb[:], ALU.add), reads=[Bfj, Blb], writes=[Bfj])
        P.op("dve", ts(kj, fj, -1.0, ALU.mult, 1.0, ALU.add), reads=[Bfj], writes=[Bkj])
        P.op("dve", tt(Sn, Sj, fj.unsqueeze(2).to_broadcast([128, 8, 128]), ALU.mult), reads=[BSj, Bfj], writes=[BSn])
        P.op("pool", tt(vb, vb, kj.unsqueeze(2).to_broadcast([128, 8, 128]), ALU.mult), reads=[Bvb, Bkj], writes=[Bvb])
        P.op("dve", tt(Sn, Sn, vb, ALU.add), reads=[BSn, Bvb], writes=[BSn])
        P.dma("sp", dm(ss_o[j].rearrange("h k v -> k h v"), Sn), reads=[BSn])
        P.op("pool", tt(vb, Sn, qsil_s[:, :, j].unsqueeze(2).to_broadcast([128, 8, 128]), ALU.mult), reads=[BSn, Bqss], writes=[Bvb])
        vf = vb.rearrange("p h d -> p (h d)")
        P.op("pe", mms([(o_lo, oneh[:, j, :], vf[:, 0:512], j == 0, j == NS - 1),
                        (o_hi, oneh[:, j, :], vf[:, 512:1024], j == 0, j == NS - 1)]),
             reads=[Bvb, Boh], writes=[PB[3], PB[4]])
    oh4 = mrow[0:NS, 0:1024]
    P.op("act", acp(oh4[:, 0:512], o_lo), reads=[PB[3]], writes=[Bmrow])
    P.op("act", acp(oh4[:, 512:1024], o_hi), reads=[PB[4], Bmrow], writes=[Bmrow])
    oh48 = oh4.rearrange("p (h d) -> p h d", d=128)
    P.op("act", act(p4, oh48, AF.Square), reads=[Bmrow], writes=[Bprod])
    P.op("dve", lambda e: e.tensor_reduce(out=e4, in_=p4, axis=AX.X, op=ALU.add), reads=[Bprod], writes=[Bs8])
    P.op("act", act(e4, e4, AF.Ln, scale=1.0 / 128, bias=EPS), reads=[Bs8], writes=[Bs8])
    P.op("act", act(e4, e4, AF.Exp, scale=-0.5), reads=[Bs8], writes=[Bs8])
    P.op("dve", tt(oh48, oh48, e4.unsqueeze(2).to_broadcast([NS, 8, 128]), ALU.mult), reads=[Bmrow, Bs8], writes=[Bmrow])
    P.op("dve", tt(oh48, oh48, hgrow[0:NS, :].unsqueeze(1).to_broadcast([NS, 8, 128]), ALU.mult), reads=[Bmrow, Bhg], writes=[Bmrow])
    gr = srow[0:NS, 4096:5120]
    gtmp = srow[0:NS, 3072:4096]
    P.op("act", act(gtmp, gr, AF.Exp, scale=-1.0), reads=[Bsrow], writes=[Bsrow])
    P.op("dve", ts(gtmp, gtmp, 1.0, ALU.add), reads=[Bsrow], writes=[Bsrow])
    P.op("dve", lambda e: e.reciprocal(out=gtmp, in_=gtmp), reads=[Bsrow], writes=[Bsrow])
    P.op("dve", tt(gr, gr, gtmp, ALU.mult), reads=[Bsrow], writes=[Bsrow])
    P.op("dve", tt(mrgb[0:NS, 1024:2048], oh4, gr, ALU.mult), reads=[Bmrow, Bsrow, Bmrgb], writes=[Bmrgb])
    tvs = pbb(5).rearrange("p (c t) -> p c t", t=64)
    P.op("pe", trs([(tvs[:, c, 0:NS], mrgb[0:NS, c * 128:(c + 1) * 128], ident[0:NS, 0:NS]) for c in range(16)]),
         reads=[Bmrgb, Bid], writes=[PB[5]])
    P.op("dve", cp(mTs[:], tvs[:, :, 0:NS]), reads=[PB[5]], writes=[BmTs])

    P.barrier()

    wbuf.append((view(R3, 8192, [128, 16, 512], BF16), P.buf("wbuf2")))
    WQ3.prime()
    Bp2 = P.buf("p2")
    P.handoff(A, [Bp2])
    qtm = [(view(R1, i * 2048, [128, 1024], BF16), P.buf(f"qtm{i}")) for i in range(2)]
    kcm = [(view(R1, 4096 + i * 2048, [128, 1024], BF16), P.buf(f"kcm{i}")) for i in range(2)]
    kpm = [(view(R1, 8192 + i * 2048, [128, 1024], BF16), P.buf(f"kpm{i}")) for i in range(2)]
    vcm = [(view(R2, 16512 + i * 2080, [128, 1040], BF16), P.buf(f"vcm{i}")) for i in range(4)]
    vpm = [(view(R1, 12288 + i * 2080, [128, 1040], BF16), P.buf(f"vpm{i}")) for i in range(3)]
    qT = [(view(R1, 22688 + i * 2048, [128, 8, 128], BF16), P.buf(f"qT{i}")) for i in range(2)]
    kcT = [(view(R1, 26784 + i * 2048, [128, 8, 128], BF16), P.buf(f"kcT{i}")) for i in range(3)]
    kpT = [(view(R1, 32928 + i * 2048, [128, 8, 128], BF16), P.buf(f"kpT{i}")) for i in range(2)]
    pT = [(view(R2, 8320 + i * 2048, [128, 4, 256], BF16), P.buf(f"pT{i}")) for i in range(4)]
    pTm = []
    resb = [(view(R2, i * 4160, [128, 8, 130], F32), P.buf(f"res{i}")) for i in range(2)]
    for lst in (qtm, kcm, kpm, vcm, vpm, qT, kcT, kpT, pT, pTm, resb):
        P.handoff([Bp2], [b for (_, b) in lst])
    P.handoff([Bp15b], [b for (_, b) in resb])

    Brs = P.buf("rs_acc")
    iters = [(pi, dil, r, b) for pi, dil in enumerate((1, 4, 16)) for r in range(dil) for b in range(16 // dil)]

    def S1(it, _):
        pi, dil, r, b = iters[it]
        s = it % 2
        s3 = it % 3
        s3p = (it - 1) % 3
        o_start = dil * 128 * b + r
        c_start = HALO + o_start
        p_start = HALO + dil * 128 * (b - 1) + r

        def rows(t, start):
            return t[start:start + 128 * dil, :].rearrange("(i s) n -> i s n", s=dil)[:, 0, :] if dil > 1 \
                else t[start:start + 128, :]
        (q_, Bq_), (kc_, Bkc_) = qtm[s], kcm[s]
        (vc_, Bvc_) = vcm[it % 4]
        P.dma("sp", dm(q_, rows(qs, o_start)), writes=[Bq_])
        P.dma("sp", dm(kc_, rows(kscr, c_start)), writes=[Bkc_])
        P.dma("sp", dm(vc_, rows(vscr, c_start)), writes=[Bvc_])
        (qT_, BqT_), (kcT_, BkcT_) = qT[s], kcT[s3]
        tlist = [(q_, Bq_, qT_, BqT_, 0), (kc_, Bkc_, kcT_, BkcT_, 1)]
        if b == 0:
            (kp_, Bkp_), (vp_, Bvp_) = kpm[s], vpm[it % 3]
            (kpT_, BkpT_) = kpT[s]
            P.dma("sp", dm(kp_, rows(kscr, p_start)), writes=[Bkp_])
            P.dma("sp", dm(vp_, rows(vscr, p_start)), writes=[Bvp_])
            tlist.append((kp_, Bkp_, kpT_, BkpT_, 0))
        else:
            (kpT_, BkpT_) = kcT[s3p]
            (vp_, Bvp_) = vcm[(it - 1) % 4]
        for (src, Bsrc, dstT, BdT, pbi) in tlist:
            tv = pbb(pbi).rearrange("p (c t) -> p c t", t=128)
            P.op("pe", trs([(tv[:, h, :], src[:, h * 128:(h + 1) * 128], ident[:]) for h in range(8)]),
                 reads=[Bsrc, Bid], writes=[PB[pbi]])
            P.op("dve", cp(dstT, tv), reads=[PB[pbi]], writes=[BdT])
        return (qT_, BqT_, kcT_, BkcT_, kpT_, BkpT_, vc_, Bvc_, vp_, Bvp_, o_start)

    def S2(it, c_):
        pi, dil, r, b = iters[it]
        (qT_, BqT_, kcT_, BkcT_, kpT_, BkpT_, vc_, Bvc_, vp_, Bvp_, o_start) = c_
        mb_, Bmb_ = (mbB, BmbB) if b == 0 else (mbA, BmbA)
        mbf = mb_[:].rearrange("p a b -> p (a b)")
        for half in range(2):
            (pT_, BpT_) = pT[(it % 2) * 2 + half]
            b0, b1 = (2, 3) if half == 0 else (6, 7)
            specs = []
            for bank in (b0, b1):
                specs.append((pbf(bank), ident[:], mbf, True, False))
                for e2 in range(2):
                    hh = (0 if bank == b0 else 2) + e2
                    h = half * 4 + hh
                    off = e2 * 256
                    specs.append((pbf(bank)[:, off:off + 128], kcT_[:, h, :], qT_[:, h, :], False, False))
                    specs.append((pbf(bank)[:, off + 128:off + 256], kpT_[:, h, :], qT_[:, h, :], False, e2 == 1))
            P.op("pe", mms(specs), reads=[BqT_, BkcT_, BkpT_, Bid, Bmb_], writes=[PB[b0], PB[b1]])
            P.op("act", act(pT_[:, 0:2, :].rearrange("p a b -> p (a b)"), pbf(b0), AF.Exp, scale=SCALE),
                 reads=[PB[b0]], writes=[BpT_])
            P.op("act", act(pT_[:, 2:4, :].rearrange("p a b -> p (a b)"), pbf(b1), AF.Exp, scale=SCALE),
                 reads=[PB[b1]], writes=[BpT_])
        return c_

    def S3(it, c_):
        P.flush()
        pi, dil, r, b = iters[it]
        s = it % 2
        (qT_, BqT_, kcT_, BkcT_, kpT_, BkpT_, vc_, Bvc_, vp_, Bvp_, o_start) = c_
        (res_, Bres_) = resb[s]
        for half in range(2):
            (pT_, BpT_) = pT[(it % 2) * 2 + half]
            v0, v1 = 4, 5
            specs = []
            for hh in range(4):
                h = half * 4 + hh
                bank = v0 if hh < 2 else v1
                off = (hh % 2) * 130
                oreg = pbf(bank)[:, off:off + 130]
                specs.append((oreg, pT_[:, hh, 0:128], vc_[:, h * 130:(h + 1) * 130], True, False))
                specs.append((oreg, pT_[:, hh, 128:256], vp_[:, h * 130:(h + 1) * 130], False, True))
            P.op("pe", mms(specs), reads=[BpT_, Bvc_, Bvp_], writes=[PB[v0], PB[v1]])
            P.op("dve", cp(res_[:, half * 4:half * 4 + 2, :], pbf(v0)[:, 0:260].rearrange("p (a b) -> p a b", b=130)),
                 reads=[PB[v0]], writes=[Bres_])
            P.op("act", acp(res_[:, half * 4 + 2:half * 4 + 4, :], pbf(v1)[:, 0:260].rearrange("p (a b) -> p a b", b=130)),
                 reads=[PB[v1], Bres_], writes=[Bres_])
        dst = rs[0, o_start:o_start + 128 * dil, :].rearrange("(i s) n -> i s n", s=dil)[:, 0, :] if dil > 1 \
            else rs[0, o_start:o_start + 128, :]
        if pi == 0:
            P.defer((lambda dst=dst, res_=res_, Bres_=Bres_:
                     P.dma("sp", dm(dst, res_.rearrange("p a b -> p (a b)")), reads=[Bres_, Brs])))
        else:
            P.defer((lambda dst=dst, res_=res_, Bres_=Bres_:
                     P.dma("pool", (lambda e: e.dma_start(out=dst, in_=res_.rearrange("p a b -> p (a b)"), accum_op=ALU.add)),
                           reads=[Bres_], writes=[Brs], sembuf=Brs)))
    pipeline([S1, S2, S3], len(iters))

    P.barrier()

    Bp3a = P.buf("p3a")
    P.handoff([Bp2] + [b for lst in (qtm, kcm, kpm, vcm, vpm, qT, kcT, kpT, pT, pTm) for (_, b) in lst], [Bp3a])
    mix = view(R1, 0, [128, 4, D], F32); Bmix = [P.buf(f"mix{i}") for i in range(4)]
    xt3 = view(R1, 32768, [128, D], F32); Bxt3 = P.buf("xt3")
    ffT = view(R1, 0, [128, 44, G], BF16); BffT = P.buf("ffT")
    x1r = view(R1, 0, [128, D], F32); Bx1r = P.buf("x1r")
    xnF = view(R1, 8192, [128, D], BF16); BxnF = P.buf("xnF")
    ars = [(view(R2, i * 4160, [128, 8, 130], F32), P.buf(f"ar{i}")) for i in range(2)]
    mTa = view(R2, 8320, [128, 8, G], BF16); BmTa = P.buf("mTa")
    onats = [(view(R2, 16512 + i * 2048, [128, 1024], BF16), P.buf(f"onat{i}")) for i in range(2)]
    Bar_all = [b for (_, b) in ars] + [b for (_, b) in onats]
    xnB = view(R1, 40960, [128, D], BF16); BxnB = P.buf("xnB")
    x3pair = [(xt3, None), (xtb_t[:], Bxtb)]
    Bsta = [P.buf("sta0"), P.buf("sta1")]
    Bstc = [P.buf("stc0"), P.buf("stc1")]
    Bstf = [P.buf("stf0"), P.buf("stf1")]
    h2T = view(R2, 0, [128, 16, G], BF16); Bh2T = P.buf("h2T")
    ffo = view(R2, 0, [128, 4, D], F32); Bffo = [P.buf(f"ffo{i}") for i in range(4)]
    mTh = view(R4, 0, [128, 8, G], BF16); BmTh = P.buf("mTh")
    sgv = [(view(R4, 8192 + i * 2048, [128, G], F32), P.buf(f"sg{i}")) for i in range(2)]
    h2Ts = view(R4, 12288, [128, 16, NS], BF16); Bh2Ts = P.buf("h2Ts")
    ffTs = view(R4, 12416, [128, 44, NS], BF16); BffTs = P.buf("ffTs")
    smix = view(R4, 12800, [128, D], F32); Bsmix = P.buf("smix")
    sffo = view(R4, 20992, [128, D], F32); Bsffo = P.buf("sffo")
    sgs = view(R4, 29184, [128, NS], F32); Bsgs2 = P.buf("sgs")
    grow = view(R3, 0, [128, D], F32); Bgrow = P.buf("grow")
    sxs, Bsxs = xt3, Bxt3
    Bp3c = P.buf("p3c")
    P.handoff(BS + BSbf + [Bel, Bohg, Bsq, Bonb, Bst8, Bqss, Bsgs] + BkdT + BaTm + Bsigs, [Bp3c])
    for b_ in [BmTh, Bh2Ts, BffTs, Bsmix, Bsffo, Bsxs, Bsgs2] + [b for (_, b) in sgv]:
        P.handoff([Bp3c], [b_])
    for b_ in Bmix + [Bxt3]:
        P.handoff([Bp3a], [b_])
    for b_ in [BmTa] + Bar_all:
        P.handoff([b for (_, b) in resb] + [Bp15b], [b_])
    x3pair[0] = (xt3, Bxt3)
    xn_alt[0] = (xnB, BxnB)

    def rms_rows(src, rows, width, col):
        ssq_ = stat[0:rows, col:col + 1]
        P.op("act", act(xn[0:rows, 0:width], src, AF.Square, accum=ssq_), reads=[], writes=[Bxn, Bstat])
        P.op("act", act(ssq_, ssq_, AF.Ln, scale=1.0 / width, bias=EPS), reads=[Bstat], writes=[Bstat])
        P.op("act", act(ssq_, ssq_, AF.Exp, scale=-0.5), reads=[Bstat], writes=[Bstat])
        return ssq_

    for og in range(4):
        last = og == 3
        o0 = og * G
        ntile = 4 + (1 if last else 0)
        P.dma("sp", dm(mTh, mhg[:, :, o0:o0 + G]), writes=[BmTh])
        for tile in range(4):
            t0 = o0 + tile * 128
            par = tile % 2
            (a3, Bar_), (onat, Bonat) = ars[par], onats[par]
            P.dma("sp", dm(a3.rearrange("p h e -> p (h e)"), rs[0, t0:t0 + 128, :]), writes=[Bar_])
            P.op("dve", (lambda a3=a3: lambda e: e.reciprocal(out=a3[:, :, 128:129], in_=a3[:, :, 128:129]))(), reads=[Bar_], writes=[Bar_])
            P.op("dve", tt(a3[:, :, 0:128], a3[:, :, 0:128], a3[:, :, 128:129].to_broadcast([128, 8, 128]), ALU.mult),
                 reads=[Bar_], writes=[Bar_])
            ssq_ = stat[:, 2 + 3 * par:3 + 3 * par]
            xj, Bxj = (xn, Bxn) if par == 0 else (xnB, BxnB)
            P.op("act", act(xj[:, 0:1024].rearrange("p (h d) -> p h d", d=128), a3[:, :, 0:128], AF.Square, accum=ssq_),
                 reads=[Bar_], writes=[Bxj, Bsta[par]])
            P.op("act", act(ssq_, ssq_, AF.Ln, scale=1.0 / 1024, bias=EPS), reads=[Bsta[par]], writes=[Bsta[par]])
            P.op("act", act(ssq_, ssq_, AF.Exp, scale=-0.5), reads=[Bsta[par]], writes=[Bsta[par]])
            P.op("dve", stt(onat.rearrange("p (h d) -> p h d", d=128), a3[:, :, 0:128], ssq_,
                            agrow[:].rearrange("p (h d) -> p h d", d=128), ALU.mult, ALU.mult),
                 reads=[Bar_, Bsta[par], Bag], writes=[Bonat])
            tv = pbb(par).rearrange("p (c t) -> p c t", t=128)
            P.op("pe", trs([(tv[:, h, :], onat[:, h * 128:(h + 1) * 128], ident[:]) for h in range(8)]),
                 reads=[Bonat, Bid], writes=[PB[par]])
            P.op("act", acp(mTa[:, :, tile * 128:(tile + 1) * 128], tv), reads=[PB[par]], writes=[BmTa])
        for dmb in range(4):
            wt, Bw = WQ3.get()
            for tile in range(ntile):
                pbi = (2, 3, 5, 6, 7)[ctr["pb"] % 5]
                ctr["pb"] += 1
                if tile < 4:
                    tsl = slice(tile * 128, tile * 128 + 128)
                    pt = pbf(pbi)
                    specs = [(pt, mTa[:, c, tsl], wt[:, c, :], c == 0, False) for c in range(8)]
                    specs += [(pt, mTh[:, c, tsl], wt[:, 8 + c, :], False, c == 7) for c in range(8)]
                    P.op("pe", mms(specs), reads=[BmTa, BmTh, Bw], writes=[PB[pbi]])
                    P.op("act", acp(mix[:, tile, dmb * 512:(dmb + 1) * 512], pt), reads=[PB[pbi]], writes=[Bmix[tile]])
                else:
                    pt = pbf(pbi)[0:NS, :]
                    P.op("pe", mms([(pt, mTs[:, c, :], wt[:, c, :], c == 0, c == 15) for c in range(16)]),
                         reads=[BmTs, Bw], writes=[PB[pbi]])
                    P.op("act", acp(smix[0:NS, dmb * 512:(dmb + 1) * 512], pt), reads=[PB[pbi]], writes=[Bsmix])
        P.handoff([BmTa] + Bar_all, [Bh2T])
        P.dma("sp", dm(grow[:], g2_d[0].partition_broadcast(128)), writes=[Bgrow])
        def C1(tile, _):
            par = tile % 2
            xdst, Bxd = x3pair[par]
            if tile < 4:
                rows_, src, Bsrc, xsrc = 128, mix[:, tile, :], Bmix[tile], xh[HALO + o0 + tile * 128: HALO + o0 + tile * 128 + 128, :]
            else:
                rows_, src, Bsrc, xsrc = NS, smix[0:NS, :], Bsmix, xs
            P.dma("sp", dm(xdst[0:rows_, :], xsrc), writes=[Bxd])
            ssq_ = stat[0:rows_, 12 + par:13 + par]
            xj, Bxj = (xn, Bxn) if par == 0 else (xnB, BxnB)
            P.op("act", act(xj[0:rows_, :], src, AF.Square, accum=ssq_), reads=[Bsrc], writes=[Bxj, Bstc[par]])
            P.op("act", act(ssq_, ssq_, AF.Ln, scale=1.0 / D, bias=EPS), reads=[Bstc[par]], writes=[Bstc[par]])
            P.op("act", act(ssq_, ssq_, AF.Exp, scale=-0.5), reads=[Bstc[par]], writes=[Bstc[par]])
            P.op("dve", stt(src, src, ssq_, grow[0:rows_, :], ALU.mult, ALU.mult), reads=[Bsrc, Bstc[par], Bgrow], writes=[Bsrc])
            P.op("pool", tt(src, src, xdst[0:rows_, :], ALU.add), reads=[Bsrc, Bxd], writes=[Bsrc])
            return (rows_, src, Bsrc)

        def C2(tile, c_):
            rows_, src, Bsrc = c_
            if tile < 4:
                P.dma("sp", dm(x1scr[o0 + tile * 128:o0 + tile * 128 + 128, :], src), reads=[Bsrc])
                return norm_T_a(src, Bsrc, 128, g3T, Bg3, h2T, Bh2T, tile * 128)
            return norm_T_a(src, Bsrc, NS, g3T, Bg3, h2Ts, Bh2Ts, 0)

        def C3(tile, sb_):
            sb_()
        pipeline([C1, C2, C3], ntile)
        P.handoff(Bmix + [Bxt3, BxnB], [BffT])
        for t in range(NGU):
            wt, Bw = WQ3.get()
            for j in range(2):
                fi = t * 2 + j
                npair = 3 if last else 4
                pg, pu = 2 * (fi % npair), 2 * (fi % npair) + 1
                if not last:
                    P.op("pe", mms([(pbf(pg), wt[:, c, j * 128:(j + 1) * 128], h2T[:, c, :], c == 0, c == 15) for c in range(16)]),
                         reads=[Bh2T, Bw], writes=[PB[pg]])
                    P.op("pe", mms([(pbf(pu), wt[:, c, 256 + j * 128:256 + (j + 1) * 128], h2T[:, c, :], c == 0, c == 15) for c in range(16)]),
                         reads=[Bh2T, Bw], writes=[PB[pu]])
                else:
                    specs = []
                    for c in range(16):
                        specs.append((pbf(pg), wt[:, c, j * 128:(j + 1) * 128], h2T[:, c, :], c == 0, c == 15))
                        specs.append((pbf(6)[:, 0:NS], wt[:, c, j * 128:(j + 1) * 128], h2Ts[:, c, :], c == 0, c == 15))
                    for c in range(16):
                        specs.append((pbf(pu), wt[:, c, 256 + j * 128:256 + (j + 1) * 128], h2T[:, c, :], c == 0, c == 15))
                        specs.append((pbf(6)[:, 8:8 + NS], wt[:, c, 256 + j * 128:256 + (j + 1) * 128], h2Ts[:, c, :], c == 0, c == 15))
                    P.op("pe", mms(specs), reads=[Bh2T, Bh2Ts, Bw], writes=[PB[pg], PB[pu], PB[6]])
                sg_, Bsg_ = sgv[fi % 2]
                P.op("act", act(sg_, pbf(pg), AF.Silu), reads=[PB[pg]], writes=[Bsg_])
                P.op("dve", tt(ffT[:, fi, :], sg_, pbf(pu), ALU.mult), reads=[Bsg_, PB[pu]], writes=[BffT])
                if last:
                    P.op("act", act(sgs, pbf(6)[:, 0:NS], AF.Silu), reads=[PB[6]], writes=[Bsgs2])
                    P.op("dve", tt(ffTs[:, fi, :], sgs, pbf(6)[:, 8:8 + NS], ALU.mult), reads=[Bsgs2, PB[6]], writes=[BffTs])
        for b_ in Bffo:
            P.handoff([Bh2T], [b_])

        for si, (dmb, kq) in enumerate(dseq):
            wt, Bw = WQ3.get()
            for tile in range(ntile):
                if tile < 4:
                    bk = tile if (last or dmb % 2 == 0) else 4 + tile
                    pt = pbf(bk)
                    tsl = slice(tile * 128, tile * 128 + 128)
                    P.op("pe", mms([(pt, ffT[:, kq * 11 + c, tsl], wt[:, c, :], kq == 0 and c == 0, kq == 3 and c == 10) for c in range(11)]),
                         reads=[BffT, Bw], writes=[PB[bk]])
                    if kq == 3:
                        P.op("act" if tile % 2 == 0 else "dve",
                             (acp if tile % 2 == 0 else cp)(ffo[:, tile, dmb * 512:(dmb + 1) * 512], pt),
                             reads=[PB[bk]], writes=[Bffo[tile]])
                else:
                    pt = pbf(4)[0:NS, :]
                    P.op("pe", mms([(pt, ffTs[:, kq * 11 + c, :], wt[:, c, :], kq == 0 and c == 0, kq == 3 and c == 10) for c in range(11)]),
                         reads=[BffTs, Bw], writes=[PB[4]])
                    if kq == 3:
                        P.op("act", acp(sffo[0:NS, dmb * 512:(dmb + 1) * 512], pt), reads=[PB[4]], writes=[Bsffo])
        P.handoff([BffT], [Bx1r, BxnF])
        P.dma("sp", dm(grow[:], g4_d[0].partition_broadcast(128)), writes=[Bgrow])
        x1pair = [(x1r, Bx1r), (xtb_t[:], Bxtb)]

        def F1(tile, _):
            par = tile % 2
            if tile < 4:
                rows_, src, Bsrc = 128, ffo[:, tile, :], Bffo[tile]
                x1v, Bx1 = x1pair[par]
                P.dma("sp", dm(x1v, x1scr[o0 + tile * 128:o0 + tile * 128 + 128, :]), writes=[Bx1])
                dst = y[o0 + tile * 128:o0 + tile * 128 + 128, :]
            else:
                rows_, src, Bsrc = NS, sffo[0:NS, :], Bsffo
                x1v, Bx1 = smix, Bsmix
                dst = ys
            ssq_ = stat[0:rows_, 14 + par:15 + par]
            xj, Bxj = (xn, Bxn) if par == 0 else (xnF, BxnF)
            P.op("act", act(xj[0:rows_, :], src, AF.Square, accum=ssq_), reads=[Bsrc], writes=[Bxj, Bstf[par]])
            P.op("act", act(ssq_, ssq_, AF.Ln, scale=1.0 / D, bias=EPS), reads=[Bstf[par]], writes=[Bstf[par]])
            P.op("act", act(ssq_, ssq_, AF.Exp, scale=-0.5), reads=[Bstf[par]], writes=[Bstf[par]])
            return (rows_, src, Bsrc, x1v, Bx1, dst, ssq_, par)

        def F2(tile, c_):
            rows_, src, Bsrc, x1v, Bx1, dst, ssq_, par = c_
            P.op("dve", stt(src, src, ssq_, grow[0:rows_, :], ALU.mult, ALU.mult), reads=[Bsrc, Bstf[par], Bgrow], writes=[Bsrc])
            P.op("pool", tt(src, src, x1v[0:rows_, :], ALU.add), reads=[Bsrc, Bx1], writes=[Bsrc])
            P.dma("sp", dm(dst, src), reads=[Bsrc])
        pipeline([F1, F2], ntile)
        for b_ in [BmTa] + Bar_all:
            P.handoff(Bffo, [b_])
        for b_ in Bmix + [Bxt3, BxnB]:
            P.handoff([Bx1r, BxnF], [b_])

    P.barrier()
    P.emit(st)
    st.close()
    return nc


_NC = None


def _consts(core):
    i = np.arange(128)
    mcur = (i[:, None] <= i[None, :]).astype(np.float32)
    mprev = (i[:, None] >= i[None, :]).astype(np.float32)
    bdm = ((i[:, None] <= i[None, :]) & ((i[:, None] // 64) == (i[None, :] // 64))).astype(np.float32)
    oneh = np.zeros((128, NS, NS), np.float32)
    for j in range(NS):
        oneh[:, j, j] = 1.0
    hv = np.full((128, 1), 0.0 if core == 0 else 1.0, np.float32)
    return dict(ident=np.eye(128, dtype=np.float32), mcur=mcur, mprev=mprev, bdm=bdm, oneh=oneh, hv=hv)


def kernel(x_prompt, x_sample, cache_win_k, cache_win_v, state_hgrn, norm_pre_mix, w_in,
           hg_lb_logits, attn_out_gain, hg_norm_gain, w_out, norm_post_mix, norm_pre_ffn,
           w_gate, w_up, w_down, norm_post_ffn):
    global _NC
    f = lambda a: np.ascontiguousarray(np.asarray(a, dtype=np.float32))
    xp = f(x_prompt)[0]
    xsm = f(x_sample)[:, 0, :]
    ckv = f(cache_win_k)[0].reshape(32, 2048, 1024)
    cvv = f(cache_win_v)[0].reshape(32, 2048, 1024)
    sth = f(state_hgrn)[0]
    shared = dict(
        w_in=f(w_in)[0], w_out=f(w_out)[0], w_gate=f(w_gate)[0], w_up=f(w_up)[0], w_down=f(w_down)[0],
        g1T=f(f(norm_pre_mix)[0].reshape(16, 128).T), g3T=f(f(norm_pre_ffn)[0].reshape(16, 128).T),
        g2=f(norm_post_mix), g4=f(norm_post_ffn), ag=f(attn_out_gain), hg=f(hg_norm_gain),
        lbl=f(f(hg_lb_logits).reshape(2, 8, 128).transpose(2, 0, 1)),
    )
    in_maps = []
    for c in range(NCORE):
        xhc = np.zeros((HALO + OWN, D), np.float32)
        if c > 0:
            xhc[0:HALO] = xp[(c - 1) * OWN:c * OWN]
        xhc[HALO:] = xp[c * OWN:(c + 1) * OWN]
        m = dict(shared)
        m.update(_consts(c))
        m.update(xh=xhc, xs=f(xsm[c * NS:(c + 1) * NS]), ck=f(ckv[c * NS:(c + 1) * NS]),
                 cv=f(cvv[c * NS:(c + 1) * NS]), st_in=f(sth[c * NS:(c + 1) * NS]))
        in_maps.append(m)
    if _NC is None:
        _NC = build()
    res = run_bass_kernel_spmd(_NC, in_maps, core_ids=list(range(NCORE)))
    R = res.results
    y_prompt = np.concatenate([R[c]["y"] for c in range(NCORE)], axis=0)[None]
    y_sample = np.concatenate([R[c]["ys"] for c in range(NCORE)], axis=0)[:, None, :]
    win_k = R[NCORE - 1]["wk"].reshape(1, 1, 2048, 8, 128)
    win_v = R[NCORE - 1]["wv"].reshape(1, 1, 2048, 8, 128)
    state_p = R[NCORE - 1]["sp_out"].reshape(1, 1, 8, 128, 128)
    ks = np.concatenate([R[c]["ks_o"] for c in range(NCORE)], axis=0).reshape(1, 32, 1, 8, 128)
    vs = np.concatenate([R[c]["vs_o"] for c in range(NCORE)], axis=0).reshape(1, 32, 1, 8, 128)
    ss = np.concatenate([R[c]["ss_o"] for c in range(NCORE)], axis=0).reshape(1, 32, 8, 128, 128)
    outs = (y_prompt, y_sample, win_k, win_v, state_p, ks, vs, ss)
    return tuple(np.ascontiguousarray(o, dtype=np.float32) for o in outs)
```

```python
import contextlib
import numpy as np
import concourse.bass as bass
import concourse.mybir as mybir
from concourse.bass_utils import run_bass_kernel_spmd

F32 = mybir.dt.float32
BF16 = mybir.dt.bfloat16
AF = mybir.ActivationFunctionType
ALU = mybir.AluOpType
AX = mybir.AxisListType

D = 2048
OWN = 2048
HALO = 2048
G = 512
NCORE = 8
INW = 7168
DFF = 5632
NS = 4
EPS = 1e-6
SCALE = 128 ** -0.5
ENGS = ("pe", "act", "dve", "pool", "sp")


class Buf:
    __slots__ = ("name", "w", "r", "dkey")

    def __init__(self, name):
        self.name = name
        self.w = None
        self.r = []
        self.dkey = None


class Prog:
    def __init__(self, nc):
        self.nc = nc
        self.ops = {e: [] for e in ENGS}
        self.cnt = {}
        self.waited = {e: {} for e in ENGS}
        self.dma_keys = []
        self.nbuf = 0
        self.pending = []

    def buf(self, name=None):
        self.nbuf += 1
        return Buf(name or f"b{self.nbuf}")

    @staticmethod
    def _flat(seq):
        out = []
        for b in seq:
            if isinstance(b, (list, tuple)):
                out.extend(Prog._flat(b))
            else:
                out.append(b)
        return out

    def defer(self, thunk):
        self.pending.append(thunk)

    def flush(self):
        p, self.pending = self.pending, []
        for t in p:
            t()

    def _deps(self, eng, reads, writes, nowaw=False):
        need = {}

        def add(tok):
            if tok is None:
                return
            k, v = tok
            if k == eng and eng == "pe":
                return
            if need.get(k, 0) < v:
                need[k] = v
        for b in reads:
            add(b.w)
        for b in writes:
            if not nowaw:
                add(b.w)
            for t in b.r:
                add(t)
        waits = []
        wd = self.waited[eng]
        for k, v in need.items():
            if wd.get(k, 0) < v:
                wd[k] = v
                waits.append((k, v))
        return waits

    def _commit(self, tok, reads, writes):
        for b in reads:
            b.r.append(tok)
            if len(b.r) > 64:
                mx = {}
                for k, v in b.r:
                    if mx.get(k, 0) < v:
                        mx[k] = v
                b.r = list(mx.items())
        for b in writes:
            b.w = tok
            b.r = []

    def op(self, eng, fn, reads=(), writes=()):
        reads, writes = self._flat(reads), self._flat(writes)
        waits = self._deps(eng, reads, writes)
        self.cnt[eng] = self.cnt.get(eng, 0) + 1
        tok = (eng, self.cnt[eng])
        self.ops[eng].append((waits, fn, (eng, 1)))
        self._commit(tok, reads, writes)
        return tok

    def dma(self, queue, fn, reads=(), writes=(), sembuf=None, nowaw=False):
        reads, writes = self._flat(reads), self._flat(writes)
        sb = sembuf or (writes[0] if writes else reads[0])
        if sb.dkey is None:
            sb.dkey = f"d{len(self.dma_keys)}"
            self.dma_keys.append(sb.dkey)
        key = sb.dkey
        waits = self._deps(queue, reads, writes, nowaw=nowaw)
        self.cnt[key] = self.cnt.get(key, 0) + 16
        tok = (key, self.cnt[key])
        self.ops[queue].append((waits, fn, (key, 16)))
        self._commit(tok, reads, writes)
        return tok

    def handoff(self, old, new):
        toks = []
        old, new = self._flat(old), self._flat(new)
        for b in old:
            if b.w is not None:
                toks.append(b.w)
            toks.extend(b.r)
        for b in new:
            b.r.extend(toks)

    def barrier(self):
        self.flush()
        for e in ENGS:
            waits = []
            wd = self.waited[e]
            for k, v in self.cnt.items():
                if k == e:
                    continue
                if wd.get(k, 0) < v:
                    wd[k] = v
                    waits.append((k, v))
            if waits:
                self.ops[e].append((waits, None, None))

    def emit(self, stack):
        nc = self.nc
        sems = {}
        for k in list(ENGS) + self.dma_keys:
            if k in self.cnt:
                sems[k] = stack.enter_context(nc.semaphore(f"s_{k}"))
        block = stack.enter_context(nc.Block())

        def run(engname):
            def body(e):
                for waits, fn, inc in self.ops[engname]:
                    for k, v in waits:
                        e.wait_ge(sems[k], v)
                    if fn is not None:
                        ins = fn(e)
                        ins.then_inc(sems[inc[0]], inc[1])
            return body

        block.tensor(run("pe"))
        block.scalar(run("act"))
        block.vector(run("dve"))
        block.gpsimd(run("pool"))
        block.sync(run("sp"))


class WQ:
    def __init__(self, nslots_fn):
        self.jobs = []
        self.tiles = []
        self.issued = 0
        self.k = 0
        self.nslots = nslots_fn

    def add(self, thunk):
        self.jobs.append(thunk)

    def prime(self):
        S = self.nslots()
        while self.issued < len(self.jobs) and self.issued <= self.k + S - 1:
            self.tiles.append(self.jobs[self.issued]())
            self.issued += 1

    def get(self):
        self.prime()
        t = self.tiles[self.k]
        self.tiles[self.k] = None
        self.k += 1
        return t


def mms(specs):
    def f(e):
        ins = None
        for (o, l, r, s, t) in specs:
            ins = e.matmul(o, lhsT=l, rhs=r, start=s, stop=t)
        return ins
    return f


def trs(specs):
    def f(e):
        ins = None
        for (o, i, idt) in specs:
            ins = e.transpose(out=o, in_=i, identity=idt)
        return ins
    return f


def act(out, in_, func, scale=1.0, bias=0.0, accum=None):
    if accum is None:
        return lambda e: e.activation(out=out, in_=in_, func=func, bias=bias, scale=scale)
    return lambda e: e.activation(out=out, in_=in_, func=func, bias=bias, scale=scale, accum_out=accum)


def tt(out, a, b, op):
    return lambda e: e.tensor_tensor(out=out, in0=a, in1=b, op=op)


def ts(out, a, s1, op0, s2=None, op1=None):
    if op1 is None:
        return lambda e: e.tensor_scalar(out=out, in0=a, scalar1=s1, scalar2=None, op0=op0)
    return lambda e: e.tensor_scalar(out=out, in0=a, scalar1=s1, scalar2=s2, op0=op0, op1=op1)


def stt(out, a, s, b, op0, op1):
    return lambda e: e.scalar_tensor_tensor(out=out, in0=a, scalar=s, in1=b, op0=op0, op1=op1)


def cp(out, in_):
    return lambda e: e.tensor_copy(out=out, in_=in_)


def acp(out, in_):
    return lambda e: e.copy(out=out, in_=in_)


def dm(out, in_):
    return lambda e: e.dma_start(out=out, in_=in_)


def dmnc(out, in_):
    return lambda e: e.dma_start(out=out, in_=in_, allow_slow_non_contiguous=True)


def build():
    nc = bass.Bass("TRN2", target_bir_lowering=False)

    def din(name, shape, dt=F32):
        return nc.dram_tensor(name, list(shape), dt, kind="ExternalInput").ap()

    def dout(name, shape, dt=F32):
        return nc.dram_tensor(name, list(shape), dt, kind="ExternalOutput").ap()

    def dscr(name, shape, dt):
        return nc.dram_tensor(name, list(shape), dt, kind="Internal").ap()

    xh = din("xh", [HALO + OWN, D])
    xs = din("xs", [NS, D])
    ck = din("ck", [NS, 2048, 1024])
    cv = din("cv", [NS, 2048, 1024])
    st_in = din("st_in", [NS, 8, 128, 128])
    w_in = din("w_in", [D, INW])
    w_out = din("w_out", [D, D])
    w_gate = din("w_gate", [D, DFF])
    w_up = din("w_up", [D, DFF])
    w_down = din("w_down", [DFF, D])
    g1T_d = din("g1T", [128, 16])
    g3T_d = din("g3T", [128, 16])
    g2_d = din("g2", [1, D])
    g4_d = din("g4", [1, D])
    ag_d = din("ag", [1, 1024])
    hg_d = din("hg", [1, 128])
    lbl_d = din("lbl", [128, 2, 8])
    ident_d = din("ident", [128, 128])
    mcur_d = din("mcur", [128, 128])
    mprev_d = din("mprev", [128, 128])
    bdm_d = din("bdm", [128, 128])
    oneh_d = din("oneh", [128, NS, NS])
    hv_d = din("hv", [128, 1])

    y = dout("y", [OWN, D])
    ys = dout("ys", [NS, D])
    wk = dout("wk", [OWN, 1024])
    wv = dout("wv", [OWN, 1024])
    sp_out = dout("sp_out", [8, 128, 128])
    ks_o = dout("ks_o", [NS, 1024])
    vs_o = dout("vs_o", [NS, 1024])
    ss_o = dout("ss_o", [NS, 8, 128, 128])

    qs = dscr("qs", [OWN + 16, 1024], BF16)
    kscr = dscr("kscr", [HALO + OWN + 16, 1024], BF16)
    vscr = dscr("vscr", [HALO + OWN + 16, 1040], BF16)
    rs = dscr("rs", [3, OWN + 16, 1040], F32)
    mhg = dscr("mhg", [128, 8, OWN], BF16)
    sproj = dscr("sproj", [NS, INW], F32)
    x1scr = dscr("x1scr", [OWN, D], F32)
    winbf = dscr("winbf", [14, 128, 16 * 512], BF16)

    st = contextlib.ExitStack()
    P = Prog(nc)

    def sb(name, shape, dt):
        t = st.enter_context(nc.sbuf_tensor("sb_" + name, list(shape), dt))
        return t, P.buf(name)

    ps_all = st.enter_context(nc.psum_tensor("ps_all", [128, 8, 512], F32))
    PB = [[P.buf(f"pb{i}")] * 4 for i in range(8)]

    def pbf(i):
        return ps_all[:, i, :]

    def pbb(i):
        return ps_all[:, i, :].bitcast(BF16)

    CQ = P.buf("constq")
    ident_f, Bidf = sb("ident_f", [128, 128], F32)
    ident, Bid = sb("ident", [128, 128], BF16)
    mcur, Bmc = sb("mcur", [128, 128], F32)
    mprev, Bmp = sb("mprev", [128, 128], F32)
    bdm, Bbdm = sb("bdm", [128, 128], F32)
    oneh, Boh = sb("oneh", [128, NS, NS], F32)
    hv, Bhv = sb("hv", [128, 1], F32)
    g1T, Bg1 = sb("g1T", [128, 16], F32)
    g3T, Bg3 = sb("g3T", [128, 16], F32)
    agrow, Bag = sb("agrow", [128, 1024], F32)
    hgrow, Bhg = sb("hgrow", [128, 128], F32)
    lbl, Blbl = sb("lbl", [128, 2, 8], F32)
    lb, Blb = sb("lb", [128, 8], F32)
    oml, Boml = sb("oml", [128, 8], F32)
    maskA, BmA = sb("maskA", [128, 4, 256], BF16)
    maskB, BmB = sb("maskB", [128, 4, 256], BF16)
    rmask, Brm = sb("rmask", [128, G], F32)

    for (t, B, src) in ((ident_f, Bidf, ident_d), (mcur, Bmc, mcur_d), (mprev, Bmp, mprev_d),
                        (bdm, Bbdm, bdm_d), (oneh, Boh, oneh_d), (hv, Bhv, hv_d), (g1T, Bg1, g1T_d),
                        (g3T, Bg3, g3T_d), (lbl, Blbl, lbl_d)):
        P.dma("sp", dm(t[:], src), writes=[B], sembuf=CQ)
    P.dma("sp", dm(agrow[:], ag_d[0].partition_broadcast(128)), writes=[Bag], sembuf=CQ)
    P.dma("sp", dm(hgrow[:], hg_d[0].partition_broadcast(128)), writes=[Bhg], sembuf=CQ)
    _tot = (CQ.dkey, P.cnt[CQ.dkey])
    for B in (Bidf, Bmc, Bmp, Bbdm, Boh, Bhv, Bg1, Bg3, Blbl, Bag, Bhg):
        B.w = _tot
    P.op("dve", cp(ident[:], ident_f[:]), reads=[Bidf], writes=[Bid])
    P.op("dve", cp(maskA[:, :, 0:128], mcur[:].unsqueeze(1).to_broadcast([128, 4, 128])), reads=[Bmc], writes=[BmA])
    P.op("dve", cp(maskA[:, :, 128:256], mprev[:].unsqueeze(1).to_broadcast([128, 4, 128])), reads=[Bmp, BmA], writes=[BmA])
    P.op("dve", cp(maskB[:, :, 0:128], mcur[:].unsqueeze(1).to_broadcast([128, 4, 128])), reads=[Bmc], writes=[BmB])
    P.op("dve", ts(maskB[:, :, 128:256], mprev[:].unsqueeze(1).to_broadcast([128, 4, 128]), hv[:, 0:1], ALU.mult),
         reads=[Bmp, Bhv, BmB], writes=[BmB])
    mbA, BmbA = sb("mbA", [128, 2, 256], BF16)
    mbB, BmbB = sb("mbB", [128, 2, 256], BF16)
    for (mb_, Bmb_, msrc, Bmsrc) in ((mbA, BmbA, maskA, BmA), (mbB, BmbB, maskB, BmB)):
        P.op("dve", ts(mb_[:], msrc[:, 0:2, :], -1.0, ALU.add, 30000.0, ALU.mult), reads=[Bmsrc], writes=[Bmb_])
    P.op("pool", lambda e: e.memset(rmask[:], 1.0), writes=[Brm])
    P.op("pool", lambda e: e.memset(rmask[:].rearrange("p (c t) -> p c t", t=64)[:, :, 0:1], 0.0), reads=[Brm], writes=[Brm])
    P.op("dve", tt(lb[:], lbl[:, 0, :], lbl[:, 1, :], ALU.subtract), reads=[Blbl], writes=[Blb])
    P.op("act", act(lb[:], lb[:], AF.Sigmoid), reads=[Blb], writes=[Blb])
    P.op("dve", ts(oml[:], lb[:], -1.0, ALU.mult, 1.0, ALU.add), reads=[Blb], writes=[Boml])

    wbuf = []
    for i in range(2):
        t, B = sb(f"wbuf{i}", [128, 16, 512], BF16)
        wbuf.append((t, B))
    wctr = [0]

    def wload(src_ap, nchunk=16):
        t, B = wbuf[wctr[0] % len(wbuf)]
        wctr[0] += 1
        P.dma("pool", dm(t[:, 0:nchunk, :], src_ap.rearrange("(c p) n -> p c n", p=128)), writes=[B])
        return t, B

    def wstream(srcs, nchunk=16):
        S = len(wbuf)
        issued = 0
        tiles = []
        for k in range(len(srcs)):
            while issued < len(srcs) and issued <= k + S - 1:
                tiles.append(wload(srcs[issued], nchunk))
                issued += 1
            yield k, tiles[k]

    R1, BR1 = sb("R1", [128, 11264], F32)
    R2, BR2 = sb("R2", [128, 8192], F32)
    R3, BR3 = sb("R3", [128, 6400], F32)
    R4, BR4 = sb("R4", [128, 7424], F32)

    def view(reg, off_b, shape, dt):
        n = int(np.prod(shape[1:]))
        esz = 4 if dt == F32 else 2
        assert off_b % 4 == 0
        nf = (n * esz + 3) // 4
        ap = reg[:, off_b // 4: off_b // 4 + nf]
        if dt != F32:
            ap = ap.bitcast(dt)
        if len(shape) == 3:
            ap = ap.rearrange("p (a b) -> p a b", b=shape[2])
        return ap

    stat, Bstat = sb("stat", [128, 16], F32)
    xt = view(R4, 0, [128, D], F32); Bxt = P.buf("xt")
    xtb_t, Bxtb = sb("xtb", [128, D], F32)
    xts = [(xt, Bxt), (xtb_t[:], Bxtb)]
    xtctr = [0]
    xn0, Bxn0 = sb("xn", [128, D], BF16)
    xn1 = view(R4, 24576, [128, D], BF16); Bxn1 = P.buf("xn1")
    xn, Bxn = xn0, Bxn0
    xnctr = [0]
    hT = view(R4, 8192, [128, 16, G], BF16); BhT = P.buf("hT")
    hTs, BhTs = sb("hTs", [128, 16, NS], BF16)
    stg32 = [sb(f"stg32_{i}", [128, 512], F32) for i in range(2)]
    stg16 = [sb(f"stg16_{i}", [128, 512], BF16) for i in range(3)]
    vstg = [sb(f"vstg_{i}", [128, 4, 130], BF16) for i in range(2)]
    sstg = [sb(f"sstg_{i}", [NS, 512], F32) for i in range(1)]
    for (t, B) in vstg:
        P.op("pool", (lambda t: lambda e: e.memset(t[:], 1.0))(t), writes=[B])
    ctr = {"s32": 0, "s16": 0, "v": 0, "ss": 0, "pb": 0}

    def nxt(lst, key):
        r = lst[ctr[key] % len(lst)]
        ctr[key] += 1
        return r

    p1flag = [False]
    Bstn = [P.buf("stn0"), P.buf("stn1")]

    def sigmoid_to(dst, Bdst, src, Bsrc):
        P.op("act", act(dst, src, AF.Exp, scale=-1.0), reads=[Bsrc], writes=[Bdst])
        P.op("pool", ts(dst, dst, 1.0, ALU.add, 1.0, ALU.mult), reads=[Bdst], writes=[Bdst])
        P.op("dve", (lambda d_: lambda e: e.reciprocal(out=d_, in_=d_))(dst), reads=[Bdst], writes=[Bdst])

    def silu_to(dst, Bdst, src, Bsrc, tmp, Btmp_):
        P.op("act", act(tmp, src, AF.Exp, scale=-1.0), reads=[Bsrc], writes=[Btmp_])
        P.op("act", acp(dst, src), reads=[Bsrc], writes=[Bdst])
        P.op("pool", ts(tmp, tmp, 1.0, ALU.add, 1.0, ALU.mult), reads=[Btmp_], writes=[Btmp_])
        P.op("dve", (lambda d_: lambda e: e.reciprocal(out=d_, in_=d_))(tmp), reads=[Btmp_], writes=[Btmp_])
        P.op("pool", tt(dst, dst, tmp, ALU.mult), reads=[Bdst, Btmp_], writes=[Bdst])

    xn_alt = [None]

    def norm_T_a(src, Bsrc, rows, gT, BgT, dst, Bdst, col0, scale_eng="pool"):
        use_alt = (xn_alt[0] is not None) and (xnctr[0] % 2 == 1)
        (xn, Bxn) = xn_alt[0] if use_alt else (xn0, Bxn0)
        sc_ = xnctr[0] % 2 if xn_alt[0] is not None else 0
        xnctr[0] += 1
        ssq = stat[0:rows, 6 + sc_:7 + sc_]
        Bst_ = Bstn[sc_]
        P.op("act", act(xn[0:rows, :], src, AF.Square, accum=ssq), reads=[Bsrc], writes=[Bxn, Bst_])
        P.op("act", act(ssq, ssq, AF.Ln, scale=1.0 / D, bias=EPS), reads=[Bst_], writes=[Bst_])
        P.op("act", act(ssq, ssq, AF.Exp, scale=-0.5), reads=[Bst_], writes=[Bst_])
        if scale_eng == "pool":
            P.op("pool", ts(xn[0:rows, :], src, ssq, ALU.mult, 1.0, ALU.mult), reads=[Bsrc, Bst_], writes=[Bxn])
        else:
            P.op("act", (lambda xn=xn, ssq=ssq: lambda e: e.activation(out=xn[0:rows, :], in_=src, func=AF.Copy, scale=ssq))(),
                 reads=[Bsrc, Bst_], writes=[Bxn])

        def stage_b():
            for half in range(2):
                pv = pbb(half).rearrange("p (c t) -> p c t", t=128)
                P.op("pe", trs([(pv[:, c, 0:rows], xn[0:rows, (half * 8 + c) * 128:(half * 8 + c + 1) * 128],
                                 ident[0:rows, 0:rows]) for c in range(8)]),
                     reads=[Bxn, Bid], writes=[PB[half]])
                P.op("dve", tt(dst[:, half * 8:half * 8 + 8, col0:col0 + rows], pv[:, :, 0:rows],
                               gT[:, half * 8:half * 8 + 8].unsqueeze(2).to_broadcast([128, 8, rows]), ALU.mult),
                     reads=[PB[half], BgT], writes=[Bdst])
        return stage_b

    def norm_T(*a, **k):
        norm_T_a(*a, **k)()

    def pipeline(stages, n):
        carry = {}
        K = len(stages)
        for step in range(n + K - 1):
            for k in range(K):
                t = step - k
                if 0 <= t < n:
                    carry[t] = stages[k](t, carry.get(t))

    vhg = view(R1, 0, [128, 4, 1024], BF16); Bvhg = P.buf("vhg")
    qsil = view(R1, 8192, [128, 4, G], F32); Bqsil = [P.buf(f"qsil{i}") for i in range(4)]
    qt = view(R1, 16384, [128, 8, G], BF16); Bqt = P.buf("qt")
    kt = view(R1, 24576, [128, 8, G], BF16); Bkt = P.buf("kt")
    kdec = view(R1, 32768, [128, 8, G], BF16); Bkdec = P.buf("kdec")
    gsil = view(R2, 0, [128, 8, G], BF16); Bgsil = P.buf("gsil")
    mst = view(R2, 8192, [128, 8, G], BF16); Bmst = P.buf("mst")
    tmpv = [view(R2, 16384 + i * 2048, [128, G], F32) for i in range(7)]
    Btmp = [P.buf(f"tmp{i}") for i in range(7)]
    Sst = view(R3, 0, [128, 8, 128], F32); BS = [P.buf(f"S{h}") for h in range(8)]
    Sbf = view(R3, 4096, [128, 8, 128], BF16); BSbf = [P.buf(f"Sbf{h}") for h in range(8)]
    elast = view(R3, 6144, [128, 8, 8], F32); Bel = P.buf("elast")
    kdT = view(R3, 6656, [128, 8, 128], BF16); BkdT = [P.buf(f"kdT{h}") for h in range(8)]
    aTm = view(R3, 8704, [128, 8, 128], BF16); BaTm = [P.buf(f"aTm{h}") for h in range(8)]
    ohg = view(R3, 10752, [128, 8, 128], F32); Bohg = P.buf("ohg")
    sq = view(R3, 14848, [128, 8, 128], F32); Bsq = P.buf("sq")
    onb = view(R3, 18944, [128, 8, 128], BF16); Bonb = P.buf("onb")
    onbs = [(onb, Bonb), (view(R2, 30720, [128, 8, 128], BF16), P.buf("onb2"))]
    pend_tail = []
    chunk_ctr = [0]
    st8 = view(R3, 20992, [128, 8], F32); Bst8 = P.buf("st8")
    sigs = [view(R3, 21504, [128, G], F32), tmpv[6], view(R1, 40960, [128, G], F32), view(R1, 43008, [128, G], F32)]
    Bsigs = [P.buf("sig0"), Btmp[6], P.buf("sig2"), P.buf("sig3")]
    qsil_s = view(R3, 23552, [128, 8, NS], F32); Bqss = P.buf("qsil_s")
    sig_s = view(R3, 23680, [128, 8, NS], F32); Bsgs = P.buf("sig_s")

    P.op("pool", lambda e: e.memset(Sst, 0.0), writes=BS)
    P.op("pool", lambda e: e.memset(Sbf, 0.0), writes=BSbf)

    own_blocks = [0, 1, 2, 3, 4, 5, 10, 11, 6, 8, 7, 9, 12, 13]
    halo_blocks = [2, 3, 4, 5, 8, 10, 9, 11]
    prep_queue = []
    xpre = []
    apre = []
    TOKM = (0, 1, 2, 3, 4, 5, 10, 11)

    def w_in_blk(b):
        return w_in[:, b * 512:(b + 1) * 512]

    def hgrn_prep(hd, sig_ap, Bsig, own):
        tf, tk, tg, tc, tE, tEi = tmpv[0:6]
        Bf, Bk, Bg_, Bc, BE, BEi = Btmp[0:6]
        P.op("dve", ts(tf, sig_ap, oml[:, hd:hd + 1], ALU.mult, lb[:, hd:hd + 1], ALU.add),
             reads=[Bsig, Boml, Blb], writes=[Bf])
        P.op("pool", ts(tk, tf, -1.0, ALU.mult, 1.0, ALU.add), reads=[Bf], writes=[Bk])
        P.op("act", act(tg, tf, AF.Ln), reads=[Bf], writes=[Bg_])
        P.op("dve", lambda e: e.tensor_tensor_scan(out=tc, data0=rmask[:], data1=tg, initial=0.0,
                                                    op0=ALU.mult, op1=ALU.add),
             reads=[Brm, Bg_], writes=[Bc])
        P.op("act", act(tE, tc, AF.Exp), reads=[Bc], writes=[BE])
        P.op("act", act(tEi, tc, AF.Exp, scale=-1.0), reads=[Bc], writes=[BEi])
        P.op("dve", cp(elast[:, hd, :], tE.rearrange("p (c t) -> p c t", t=64)[:, :, 63]),
             reads=[BE], writes=[Bel])
        P.op("pool", tt(kt[:, hd, :], tk, tEi, ALU.mult), reads=[Bk, BEi], writes=[Bkt])
        P.op("dve", tt(kdec[:, hd, :].rearrange("p (c t) -> p c t", t=64),
                       kt[:, hd, :].rearrange("p (c t) -> p c t", t=64),
                       elast[:, hd, :].unsqueeze(2).to_broadcast([128, 8, 64]), ALU.mult),
             reads=[Bkt, Bel], writes=[Bkdec])
        if own:
            P.op("pool", tt(qt[:, hd, :], qsil[:, hd % 4, :], tE, ALU.mult), reads=[Bqsil[hd % 4], BE], writes=[Bqt])

    def hgrn_prep_tile(gi, own, tile):
        tsl = slice(tile * 128, tile * 128 + 128)
        kdv = pbb(0).rearrange("p (c t) -> p c t", t=128)
        P.op("pe", trs([(kdv[:, hd, :], kdec[:, hd, tsl], ident[:]) for hd in range(8)]),
             reads=[Bkdec, Bid], writes=[PB[0]])
        P.op("act", acp(kdT[:], kdv), reads=[PB[0]], writes=BkdT)
        if own:
            for half in range(2):
                P.op("pe", mms([(pbf(1)[:, r4 * 128:(r4 + 1) * 128], kt[:, half * 4 + r4, tsl], qt[:, half * 4 + r4, tsl], True, True)
                                for r4 in range(4)]), reads=[Bkt, Bqt], writes=[PB[1]])
                P.op("dve", tt(aTm[:, half * 4:half * 4 + 4, :], pbf(1).rearrange("p (a b) -> p a b", b=128),
                               bdm[:].unsqueeze(1).to_broadcast([128, 4, 128]), ALU.mult),
                     reads=[PB[1], Bbdm], writes=BaTm[half * 4:half * 4 + 4])

    def hgrn_chunk(gi, own, tile, c2):
        hgrn_flush_tail(keep=1)
        pb_ = 64 * c2
        ch = tile * 2 + c2
        csl = slice(tile * 128 + pb_, tile * 128 + pb_ + 64)
        if own:
            for half in range(2):
                specs = []
                for r4 in range(4):
                    hd = half * 4 + r4
                    oreg = pbf(5)[0:64, r4 * 128:(r4 + 1) * 128]
                    specs.append((oreg, qt[:, hd, csl], Sbf[:, hd, :], True, False))
                    specs.append((oreg, aTm[pb_:pb_ + 64, hd, pb_:pb_ + 64],
                                  vhg[pb_:pb_ + 64, tile, hd * 128:(hd + 1) * 128], False, True))
                P.op("pe", mms(specs), reads=[Bqt, BSbf[half * 4:half * 4 + 4], BaTm[half * 4:half * 4 + 4], Bvhg], writes=[PB[5]])
                P.op("act", acp(ohg[0:64, half * 4:half * 4 + 4, :], pbf(5)[0:64, :].rearrange("p (a b) -> p a b", b=128)),
                     reads=[PB[5]], writes=[Bohg])
        for half in range(2):
            bank = 6 + half
            hs = slice(half * 4, half * 4 + 4)
            P.op("pe", mms([(pbf(bank)[:, r4 * 128:(r4 + 1) * 128], kdT[pb_:pb_ + 64, half * 4 + r4, :],
                             vhg[pb_:pb_ + 64, tile, (half * 4 + r4) * 128:(half * 4 + r4 + 1) * 128], True, True) for r4 in range(4)]),
                 reads=[BkdT[half * 4:half * 4 + 4], Bvhg], writes=[PB[bank]])
            P.op("dve", tt(Sst[:, hs, :], Sst[:, hs, :], elast[:, hs, ch:ch + 1].to_broadcast([128, 4, 128]), ALU.mult),
                 reads=[BS[half * 4:half * 4 + 4], Bel], writes=BS[half * 4:half * 4 + 4])
            P.op("dve", tt(Sst[:, hs, :], Sst[:, hs, :], pbf(bank).rearrange("p (a b) -> p a b", b=128), ALU.add),
                 reads=[BS[half * 4:half * 4 + 4], PB[bank]], writes=BS[half * 4:half * 4 + 4])
            P.op("pool", cp(Sbf[:, hs, :], Sst[:, hs, :]), reads=BS[half * 4:half * 4 + 4], writes=BSbf[half * 4:half * 4 + 4])
        if own:
            o64 = ohg[0:64]
            s64 = sq[0:64]
            P.op("act", act(s64, o64, AF.Square), reads=[Bohg], writes=[Bsq])
            P.op("dve", lambda e, s64=s64: e.tensor_reduce(out=st8[0:64, :], in_=s64, axis=AX.X, op=ALU.add),
                 reads=[Bsq], writes=[Bst8])
            P.op("act", act(st8[0:64, :], st8[0:64, :], AF.Ln, scale=1.0 / 128, bias=EPS), reads=[Bst8], writes=[Bst8])
            P.op("act", act(st8[0:64, :], st8[0:64, :], AF.Exp, scale=-0.5), reads=[Bst8], writes=[Bst8])
            P.op("dve", tt(s64, o64, st8[0:64, :].unsqueeze(2).to_broadcast([64, 8, 128]), ALU.mult),
                 reads=[Bohg, Bst8], writes=[Bsq])
            onb_, Bonb_ = onbs[chunk_ctr[0] % 2]
            chunk_ctr[0] += 1
            P.op("dve", tt(onb_[0:64], s64, hgrow[0:64, :].unsqueeze(1).to_broadcast([64, 8, 128]), ALU.mult),
                 reads=[Bsq, Bhg], writes=[Bonb_])

            def tail(onb_=onb_, Bonb_=Bonb_, csl=csl):
                tv = pbb(0).rearrange("p (c t) -> p c t", t=128)
                P.op("pe", trs([(tv[:, hd, 0:64], onb_[0:64, hd, :], ident[0:64, 0:64]) for hd in range(8)]),
                     reads=[Bonb_, Bid], writes=[PB[0]])
                P.op("dve", tt(mst[:, :, csl], tv[:, :, 0:64], gsil[:, :, csl], ALU.mult),
                     reads=[PB[0], Bgsil], writes=[Bmst])
            pend_tail.append(tail)

    def hgrn_flush_tail(keep=0):
        while len(pend_tail) > keep:
            pend_tail.pop(0)()

    def hgrn_items(gi):
        own = gi >= 4
        items = []
        for tile in range(4):
            items.append((lambda tile=tile: hgrn_prep_tile(gi, own, tile)))
            for c2 in range(2):
                items.append((lambda tile=tile, c2=c2: hgrn_chunk(gi, own, tile, c2)))
        if own:
            o0 = (gi - 4) * G
            items.append(hgrn_flush_tail)
            items.append((lambda: P.dma("sp", dm(mhg[:, :, o0:o0 + G], mst), reads=[Bmst])))
        return items

    p1flag[0] = True
    xn_alt[0] = (xn1, Bxn1)
    WQ1 = WQ(lambda: len(wbuf))
    Bwbf = [P.buf(f"wbf{b}") for b in range(14)]
    wb_first = {}

    def wload_bf(b_):
        t, B = wbuf[wctr[0] % len(wbuf)]
        wctr[0] += 1
        P.dma("pool", dm(t[:], winbf[b_].rearrange("p (c n) -> p c n", n=512)), reads=[Bwbf[b_]], writes=[B], sembuf=B)
        return t, B
    for gi in range(8):
        for b_ in (own_blocks if gi >= 4 else halo_blocks):
            if b_ not in wb_first:
                wb_first[b_] = gi
                WQ1.add((lambda b_=b_: wload(w_in_blk(b_))))
            else:
                WQ1.add((lambda b_=b_: wload_bf(b_)))
    for gi in range(8):
        own = gi >= 4
        last = gi == 7
        r0 = gi * G
        o0 = (gi - 4) * G
        blocks = own_blocks if own else halo_blocks
        def A1(tile, _):
            if apre:
                return apre.pop(0)
            if xpre:
                xt_, Bxt_ = xpre.pop(0)
            else:
                xt_, Bxt_ = xts[xtctr[0] % 2]
                xtctr[0] += 1
                P.dma("act", dm(xt_, xh[r0 + tile * 128: r0 + tile * 128 + 128, :]), writes=[Bxt_])
            return norm_T_a(xt_, Bxt_, 128, g1T, Bg1, hT, BhT, tile * 128)

        def A2(tile, sb_):
            sb_()
        pipeline([A1, A2], 4)
        if last:
            P.dma("act", dm(xt[0:NS, :], xs), writes=[Bxt])
            norm_T(xt[0:NS, :], Bxt, NS, g1T, Bg1, hTs, BhTs, 0)
        citems = hgrn_items(gi - 1) if gi > 0 else []
        nsafe = 6 if own else 4
        n_items0 = len(citems)
        ss_state = [0, 0]

        def spread_items():
            ss_state[0] += 1
            if not citems:
                return
            tgt = int((ss_state[0] - 1) * n_items0 / max(1, nsafe * 4 - 5)) + (1 if ss_state[0] > 1 else 0)
            while citems and ss_state[1] < tgt:
                citems.pop(0)()
                ss_state[1] += 1
        for bi, blk in enumerate(blocks):
            if bi == len(blocks) - 2 and gi + 1 < 8:
                for t_ in range(2):
                    xt_, Bxt_ = xpre.pop(0)
                    apre.append(norm_T_a(xt_, Bxt_, 128, g1T, Bg1, hT, BhT, t_ * 128))
            if bi == 2 and gi + 1 < 8:
                for t_ in range(2):
                    xt_, Bxt_ = xts[xtctr[0] % 2]
                    xtctr[0] += 1
                    rr = (gi + 1) * G + t_ * 128
                    P.dma("act", dm(xt_, xh[rr: rr + 128, :]), writes=[Bxt_])
                    xpre.append((xt_, Bxt_))
            wt, Bw = WQ1.get()
            if wb_first[blk] == gi:
                P.dma("sp", dm(winbf[blk], wt[:].rearrange("p c n -> p (c n)")), reads=[Bw], writes=[Bwbf[blk]])
            hgrn_flush_tail()
            if bi >= nsafe:
                while citems:
                    citems.pop(0)()
            if blk in TOKM:
                for tile in range(4 + (1 if last else 0)):
                    if prep_queue:
                        prep_queue.pop(0)()
                    spread_items()
                    pbi = (2 + (ctr["pb"] % 2)) if last else (2 + (ctr["pb"] % 3))
                    ctr["pb"] += 1
                    if tile < 4:
                        M = 128
                        lh = lambda c: hT[:, c, tile * 128:(tile + 1) * 128]
                        Bl = BhT
                    else:
                        M = NS
                        lh = lambda c: hTs[:, c, :]
                        Bl = BhTs
                    pt = pbf(pbi)[0:M, :]
                    P.op("pe", mms([(pt, lh(c), wt[:, c, :], c == 0, c == 15) for c in range(16)]),
                         reads=[Bl, Bw], writes=[PB[pbi]])
                    rows = slice(r0 + tile * 128, r0 + tile * 128 + 128)
                    orow = slice(o0 + tile * 128, o0 + tile * 128 + 128)
                    if tile == 4:
                        s_, Bs_ = nxt(sstg, "ss")
                        P.op("act", acp(s_[:], pt), reads=[PB[pbi]], writes=[Bs_])
                        P.dma("sp", dm(sproj[:, blk * 512:(blk + 1) * 512], s_[:]), reads=[Bs_])
                    elif blk in (0, 1):
                        s_, Bs_ = nxt(stg16, "s16")
                        P.op("act", acp(s_[:], pt), reads=[PB[pbi]], writes=[Bs_])
                        P.dma("sp", dm(qs[orow, blk * 512:(blk + 1) * 512], s_[:]), reads=[Bs_])
                    elif blk in (2, 3):
                        s_, Bs_ = nxt(stg16, "s16")
                        if own:
                            f_, Bf_ = nxt(stg32, "s32")
                            P.op("act", acp(f_[:], pt), reads=[PB[pbi]], writes=[Bf_])
                            P.dma("sp", dm(wk[orow, (blk - 2) * 512:(blk - 1) * 512], f_[:]), reads=[Bf_])
                            P.op("pool", cp(s_[:], f_[:]), reads=[Bf_], writes=[Bs_])
                        else:
                            P.op("act", acp(s_[:], pt), reads=[PB[pbi]], writes=[Bs_])
                        P.dma("sp", dm(kscr[rows, (blk - 2) * 512:(blk - 1) * 512], s_[:]), reads=[Bs_])
                    elif blk in (4, 5):
                        v_, Bv_ = nxt(vstg, "v")
                        if own:
                            f_, Bf_ = nxt(stg32, "s32")
                            P.op("act", acp(f_[:], pt), reads=[PB[pbi]], writes=[Bf_])
                            P.dma("sp", dm(wv[orow, (blk - 4) * 512:(blk - 3) * 512], f_[:]), reads=[Bf_])
                            P.op("pool", cp(v_[:, :, 0:128], f_[:].rearrange("p (h d) -> p h d", d=128)), reads=[Bf_], writes=[Bv_])
                        else:
                            P.op("act", acp(v_[:, :, 0:128], pt.rearrange("p (h d) -> p h d", d=128)), reads=[PB[pbi]], writes=[Bv_])
                        P.dma("sp", dm(vscr[rows, (blk - 4) * 520:(blk - 3) * 520], v_[:].rearrange("p h d -> p (h d)")), reads=[Bv_])
                    else:
                        P.op("dve", cp(vhg[:, tile, (blk - 10) * 512:(blk - 9) * 512], pt), reads=[PB[pbi]], writes=[Bvhg])
            else:
                fm_pending = []
                for j in range(4):
                    if prep_queue:
                        prep_queue.pop(0)()
                    spread_items()
                    pbi = (2 + (ctr["pb"] % 2)) if last else (2 + (ctr["pb"] % 3))
                    ctr["pb"] += 1
                    pt = pbf(pbi)
                    P.op("pe", mms([(pt, wt[:, c, j * 128:(j + 1) * 128], hT[:, c, :], c == 0, c == 15) for c in range(16)]),
                         reads=[BhT, Bw], writes=[PB[pbi]])
                    if blk in (6, 7):
                        P.op("act", act(qsil[:, j, :], pt, AF.Silu), reads=[PB[pbi]], writes=[Bqsil[j]])
                    elif blk in (8, 9):
                        hd = (blk - 8) * 4 + j
                        sg_ = sigs[j]
                        P.op("act", act(sg_, pt, AF.Sigmoid), reads=[PB[pbi]], writes=[Bsigs[j]])
                        fm_pending.append((hd, sg_, Bsigs[j]))
                    else:
                        hd = (blk - 12) * 4 + j
                        P.op("act", act(gsil[:, hd, :], pt, AF.Silu), reads=[PB[pbi]], writes=[Bgsil])
                    if last and blk in (6, 7, 8, 9):
                        hd = ((blk - 6) % 2) * 4 + j if blk in (6, 7) else (blk - 8) * 4 + j
                        if blk in (6, 7):
                            hd = (blk - 6) * 4 + j
                        pts = pbf(4)[:, 0:NS]
                        P.op("pe", mms([(pts, wt[:, c, j * 128:(j + 1) * 128], hTs[:, c, :], c == 0, c == 15) for c in range(16)]),
                             reads=[BhTs, Bw], writes=[PB[4]])
                        if blk in (6, 7):
                            P.op("act", act(qsil_s[:, hd, :], pts, AF.Silu), reads=[PB[4]], writes=[Bqss])
                        else:
                            P.op("act", act(sig_s[:, hd, :], pts, AF.Sigmoid), reads=[PB[4]], writes=[Bsgs])
                    if last and blk in (12, 13):
                        if j == 0:
                            pts = pbf(4)[0:NS, :]
                            P.op("pe", mms([(pts, hTs[:, c, :], wt[:, c, :], c == 0, c == 15) for c in range(16)]),
                                 reads=[BhTs, Bw], writes=[PB[4]])
                            s_, Bs_ = nxt(sstg, "ss")
                            P.op("act", acp(s_[:], pts), reads=[PB[4]], writes=[Bs_])
                            P.dma("sp", dm(sproj[:, blk * 512:(blk + 1) * 512], s_[:]), reads=[Bs_])
                for (hd_, sg_, Bsg_) in fm_pending:
                    prep_queue.append((lambda hd_=hd_, sg_=sg_, Bsg_=Bsg_, own=own: hgrn_prep(hd_, sg_, Bsg_, own)))
    while prep_queue:
        prep_queue.pop(0)()
    for it_ in hgrn_items(7):
        it_()

    p1flag[0] = False
    xn_alt[0] = None
    P.dma("sp", dm(sp_out.rearrange("h k v -> k h v"), Sst), reads=BS)

    P.barrier()

    def wload_gu(t):
        tl, B = wbuf[wctr[0] % len(wbuf)]
        wctr[0] += 1
        P.dma("pool", dm(tl[:, :, 0:256], w_gate[:, t * 256:(t + 1) * 256].rearrange("(c p) n -> p c n", p=128)), writes=[B])
        P.dma("pool", dm(tl[:, :, 256:512], w_up[:, t * 256:(t + 1) * 256].rearrange("(c p) n -> p c n", p=128)), writes=[B], nowaw=True)
        return tl, B

    def wd_blk(dmb, kq):
        return w_down[kq * 1408:(kq + 1) * 1408, dmb * 512:(dmb + 1) * 512]
    NGU = DFF // 256
    dseq = [(dmb, kq) for dmb in range(4) for kq in range(4)]
    WQ3 = WQ(lambda: len(wbuf))
    for og in range(4):
        for d_ in range(4):
            WQ3.add((lambda d_=d_: wload(w_out[:, d_ * 512:(d_ + 1) * 512])))
        for t_ in range(NGU):
            WQ3.add((lambda t_=t_: wload_gu(t_)))
        for sq_ in dseq:
            WQ3.add((lambda sq_=sq_: wload(wd_blk(*sq_), nchunk=11)))

    qbc = view(R1, 0, [128, 1024], F32); Bqbc = P.buf("qbc")
    kcs = [(view(R1, 4096, [128, 1024], F32), P.buf("kc0")), (view(R2, 16384, [128, 1024], F32), P.buf("kc1"))]
    vcs = [(view(R1, 8192, [128, 1024], F32), P.buf("vc0")), (view(R2, 20480, [128, 1024], F32), P.buf("vc1"))]
    prod = view(R1, 12288, [128, 8, 128], F32); Bprod = P.buf("prod")
    s8s = [(view(R1, 16384, [128, 8], F32), P.buf("s8a")), (view(R1, 16384 + 32, [128, 8], F32), P.buf("s8b"))]
    srow = view(R1, 16384 + 64, [128, 5 * 1024], F32); Bsrow = P.buf("srow")
    mrow = view(R1, 16384 + 64 + 20480, [128, 1056], F32); Bmrow = P.buf("mrow")
    Sj = view(R2, 0, [128, 8, 128], F32); BSj = P.buf("Sj")
    Sn = view(R2, 4096, [128, 8, 128], F32); BSn = P.buf("Sn")
    vb = view(R2, 8192, [128, 8, 128], F32); Bvb = P.buf("vb")
    fj = view(R2, 12288, [128, 8], F32); Bfj = P.buf("fj")
    kj = view(R2, 12288 + 64, [128, 8], F32); Bkj = P.buf("kj")
    mrgb = view(R2, 24576, [128, D], BF16); Bmrgb = P.buf("mrgb")
    mTs, BmTs = sb("mTs", [128, 16, NS], BF16)
    Bp15b = P.buf("p15b")
    A = [Bqbc, Bprod, Bsrow, Bmrow, BSj, BSn, Bvb, Bfj, Bkj, Bmrgb, Bp15b] + [b for (_, b) in kcs + vcs + s8s]

    P.dma("sp", dm(srow[0:NS, 0:3072], sproj[:, 0:3072]), writes=[Bsrow])
    P.dma("sp", dm(srow[0:NS, 3072:5120], sproj[:, 5120:7168]), writes=[Bsrow], nowaw=True)
    P.dma("sp", dm(ks_o, srow[0:NS, 1024:2048]), reads=[Bsrow])
    P.dma("sp", dm(vs_o, srow[0:NS, 2048:3072]), reads=[Bsrow])
    num_lo = pbf(0)[0:NS, :]
    num_hi = pbf(1)[0:NS, :]
    den_p = pbf(2)[0:NS, 0:8]
    pats = [(1920, 1), (1536, 4), (0, 16)]
    nacc = 0
    for j in range(NS):
        P.dma("sp", dm(qbc, sproj[j, 0:1024].partition_broadcast(128)), writes=[Bqbc])
        for (start_, dil) in pats:
            (kc, Bkc), (vc, Bvc), (s8, Bs8) = kcs[nacc % 2], vcs[nacc % 2], s8s[nacc % 2]
            krows = ck[j, start_:start_ + 128 * dil, :].rearrange("(i s) n -> i s n", s=dil)[:, 0, :]
            vrows = cv[j, start_:start_ + 128 * dil, :].rearrange("(i s) n -> i s n", s=dil)[:, 0, :]
            P.dma("sp", dm(kc, krows), writes=[Bkc])
            P.dma("act", dm(vc, vrows), writes=[Bvc])
            kc3 = kc.rearrange("p (h d) -> p h d", d=128)
            P.op("dve", tt(kc3, kc3, qbc.rearrange("p (h d) -> p h d", d=128), ALU.mult), reads=[Bkc, Bqbc], writes=[Bkc])
            P.op("dve", (lambda s8=s8, kc3=kc3: lambda e: e.tensor_reduce(out=s8, in_=kc3, axis=AX.X, op=ALU.add))(),
                 reads=[Bkc], writes=[Bs8])
            P.op("act", act(s8, s8, AF.Exp, scale=SCALE), reads=[Bs8], writes=[Bs8])
            vc3 = vc.rearrange("p (h d) -> p h d", d=128)
            P.op("pool", tt(vc3, vc3, s8.unsqueeze(2).to_broadcast([128, 8, 128]), ALU.mult), reads=[Bvc, Bs8], writes=[Bvc])
            first = nacc == 0
            lastm = nacc == NS * 3 - 1
            P.op("pe", mms([(num_lo, oneh[:, j, :], vc[:, 0:512], first, lastm),
                            (num_hi, oneh[:, j, :], vc[:, 512:1024], first, lastm),
                            (den_p, oneh[:, j, :], s8, first, lastm)]),
                 reads=[Bvc, Bs8, Boh], writes=[PB[0], PB[1], PB[2]])
            nacc += 1
    P.op("act", acp(mrow[0:NS, 0:512], num_lo), reads=[PB[0]], writes=[Bmrow])
    P.op("act", acp(mrow[0:NS, 512:1024], num_hi), reads=[PB[1], Bmrow], writes=[Bmrow])
    P.op("act", acp(mrow[0:NS, 1024:1032], den_p), reads=[PB[2], Bmrow], writes=[Bmrow])
    q4 = srow[0:NS, 0:1024].rearrange("p (h d) -> p h d", d=128)
    k4 = srow[0:NS, 1024:2048].rearrange("p (h d) -> p h d", d=128)
    v4 = srow[0:NS, 2048:3072].rearrange("p (h d) -> p h d", d=128)
    p4 = prod[0:NS]
    (s8, Bs8) = s8s[0]
    e4 = s8[0:NS, :]
    W4 = [Bprod, Bs8, Bmrow]
    P.op("dve", tt(p4, q4, k4, ALU.mult), reads=[Bsrow], writes=[Bprod])
    P.op("dve", lambda e: e.tensor_reduce(out=e4, in_=p4, axis=AX.X, op=ALU.add), reads=[Bprod], writes=[Bs8])
    P.op("act", act(e4, e4, AF.Exp, scale=SCALE), reads=[Bs8], writes=[Bs8])
    P.op("dve", ts(e4, e4, 3.0, ALU.mult), reads=[Bs8], writes=[Bs8])
    P.op("dve", tt(p4, v4, e4.unsqueeze(2).to_broadcast([NS, 8, 128]), ALU.mult), reads=[Bsrow, Bs8], writes=[Bprod])
    m4 = mrow[0:NS, 0:1024].rearrange("p (h d) -> p h d", d=128)
    d4 = mrow[0:NS, 1024:1032]
    P.op("dve", tt(m4, m4, p4, ALU.add), reads=W4, writes=[Bmrow])
    P.op("dve", tt(d4, d4, e4, ALU.add), reads=W4, writes=[Bmrow])
    P.op("dve", lambda e: e.reciprocal(out=d4, in_=d4), reads=[Bmrow], writes=[Bmrow])
    P.op("dve", tt(m4, m4, d4.unsqueeze(2).to_broadcast([NS, 8, 128]), ALU.mult), reads=[Bmrow], writes=[Bmrow])
    ssq = stat[0:NS, 1:2]
    P.op("act", act(p4.rearrange("p h d -> p (h d)"), mrow[0:NS, 0:1024], AF.Square, accum=ssq), reads=[Bmrow], writes=[Bprod, Bstat])
    P.op("act", act(ssq, ssq, AF.Ln, scale=1.0 / 1024, bias=EPS), reads=[Bstat], writes=[Bstat])
    P.op("act", act(ssq, ssq, AF.Exp, scale=-0.5), reads=[Bstat], writes=[Bstat])
    P.op("dve", stt(mrgb[0:NS, 0:1024], mrow[0:NS, 0:1024], ssq, agrow[0:NS, :], ALU.mult, ALU.mult),
         reads=[Bmrow, Bstat, Bag], writes=[Bmrgb])
    o_lo = pbf(3)[0:NS, :]
    o_hi = pbf(4)[0:NS, :]
    for j in range(NS):
        P.dma("sp", dm(Sj, st_in[j].rearrange("h k v -> k h v")), writes=[BSj])
        P.dma("act", dm(vb.rearrange("p h d -> p (h d)"), sproj[j, 5120:6144].partition_broadcast(128)), writes=[Bvb])
        P.op("dve", tt(fj, sig_s[:, :, j], oml[:], ALU.mult), reads=[Bsgs, Boml], writes=[Bfj])
        P.op("dve", tt(fj, fj, lb[:], ALU.add), reads=[Bfj, Blb], writes=[Bfj])
        P.op("dve", ts(kj, fj, -1.0, ALU.mult, 1.0, ALU.add), reads=[Bfj], writes=[Bkj])
        P.op("dve", tt(Sn, Sj, fj.unsqueeze(2).to_broadcast([128, 8, 128]), ALU.mult), reads=[BSj, Bfj], writes=[BSn])
        P.op("pool", tt(vb, vb, kj.unsqueeze(2).to_broadcast([128, 8, 128]), ALU.mult), reads=[Bvb, Bkj], writes=[Bvb])
        P.op("dve", tt(Sn, Sn, vb, ALU.add), reads=[BSn, Bvb], writes=[BSn])
        P.dma("sp", dm(ss_o[j].rearrange("h k v -> k h v"), Sn), reads=[BSn])
        P.op("pool", tt(vb, Sn, qsil_s[:, :, j].unsqueeze(2).to_broadcast([128, 8, 128]), ALU.mult), reads=[BSn, Bqss], writes=[Bvb])
        vf = vb.rearrange("p h d -> p (h d)")
        P.op("pe", mms([(o_lo, oneh[:, j, :], vf[:, 0:512], j == 0, j == NS - 1),
                        (o_hi, oneh[:, j, :], vf[:, 512:1024], j == 0, j == NS - 1)]),
             reads=[Bvb, Boh], writes=[PB[3], PB[4]])
    oh4 = mrow[0:NS, 0:1024]
    P.op("act", acp(oh4[:, 0:512], o_lo), reads=[PB[3]], writes=[Bmrow])
    P.op("act", acp(oh4[:, 512:1024], o_hi), reads=[PB[4], Bmrow], writes=[Bmrow])
    oh48 = oh4.rearrange("p (h d) -> p h d", d=128)
    P.op("act", act(p4, oh48, AF.Square), reads=[Bmrow], writes=[Bprod])
    P.op("dve", lambda e: e.tensor_reduce(out=e4, in_=p4, axis=AX.X, op=ALU.add), reads=[Bprod], writes=[Bs8])
    P.op("act", act(e4, e4, AF.Ln, scale=1.0 / 128, bias=EPS), reads=[Bs8], writes=[Bs8])
    P.op("act", act(e4, e4, AF.Exp, scale=-0.5), reads=[Bs8], writes=[Bs8])
    P.op("dve", tt(oh48, oh48, e4.unsqueeze(2).to_broadcast([NS, 8, 128]), ALU.mult), reads=[Bmrow, Bs8], writes=[Bmrow])
    P.op("dve", tt(oh48, oh48, hgrow[0:NS, :].unsqueeze(1).to_broadcast([NS, 8, 128]), ALU.mult), reads=[Bmrow, Bhg], writes=[Bmrow])
    gr = srow[0:NS, 4096:5120]
    gtmp = srow[0:NS, 3072:4096]
    P.op("act", act(gtmp, gr, AF.Exp, scale=-1.0), reads=[Bsrow], writes=[Bsrow])
    P.op("dve", ts(gtmp, gtmp, 1.0, ALU.add), reads=[Bsrow], writes=[Bsrow])
    P.op("dve", lambda e: e.reciprocal(out=gtmp, in_=gtmp), reads=[Bsrow], writes=[Bsrow])
    P.op("dve", tt(gr, gr, gtmp, ALU.mult), reads=[Bsrow], writes=[Bsrow])
    P.op("dve", tt(mrgb[0:NS, 1024:2048], oh4, gr, ALU.mult), reads=[Bmrow, Bsrow, Bmrgb], writes=[Bmrgb])
    tvs = pbb(5).rearrange("p (c t) -> p c t", t=64)
    P.op("pe", trs([(tvs[:, c, 0:NS], mrgb[0:NS, c * 128:(c + 1) * 128], ident[0:NS, 0:NS]) for c in range(16)]),
         reads=[Bmrgb, Bid], writes=[PB[5]])
    P.op("dve", cp(mTs[:], tvs[:, :, 0:NS]), reads=[PB[5]], writes=[BmTs])

    P.barrier()

    wbuf.append((view(R3, 8192, [128, 16, 512], BF16), P.buf("wbuf2")))
    WQ3.prime()
    Bp2 = P.buf("p2")
    P.handoff(A, [Bp2])
    qtm = [(view(R1, i * 2048, [128, 1024], BF16), P.buf(f"qtm{i}")) for i in range(2)]
    kcm = [(view(R1, 4096 + i * 2048, [128, 1024], BF16), P.buf(f"kcm{i}")) for i in range(2)]
    kpm = [(view(R1, 8192 + i * 2048, [128, 1024], BF16), P.buf(f"kpm{i}")) for i in range(2)]
    vcm = [(view(R2, 16512 + i * 2080, [128, 1040], BF16), P.buf(f"vcm{i}")) for i in range(4)]
    vpm = [(view(R1, 12288 + i * 2080, [128, 1040], BF16), P.buf(f"vpm{i}")) for i in range(3)]
    qT = [(view(R1, 22688 + i * 2048, [128, 8, 128], BF16), P.buf(f"qT{i}")) for i in range(2)]
    kcT = [(view(R1, 26784 + i * 2048, [128, 8, 128], BF16), P.buf(f"kcT{i}")) for i in range(3)]
    kpT = [(view(R1, 32928 + i * 2048, [128, 8, 128], BF16), P.buf(f"kpT{i}")) for i in range(2)]
    pT = [(view(R2, 8320 + i * 2048, [128, 4, 256], BF16), P.buf(f"pT{i}")) for i in range(4)]
    pTm = []
    resb = [(view(R2, i * 4160, [128, 8, 130], F32), P.buf(f"res{i}")) for i in range(2)]
    for lst in (qtm, kcm, kpm, vcm, vpm, qT, kcT, kpT, pT, pTm, resb):
        P.handoff([Bp2], [b for (_, b) in lst])
    P.handoff([Bp15b], [b for (_, b) in resb])

    Brs = P.buf("rs_acc")
    iters = [(pi, dil, r, b) for pi, dil in enumerate((1, 4, 16)) for r in range(dil) for b in range(16 // dil)]

    def S1(it, _):
        pi, dil, r, b = iters[it]
        s = it % 2
        s3 = it % 3
        s3p = (it - 1) % 3
        o_start = dil * 128 * b + r
        c_start = HALO + o_start
        p_start = HALO + dil * 128 * (b - 1) + r

        def rows(t, start):
            return t[start:start + 128 * dil, :].rearrange("(i s) n -> i s n", s=dil)[:, 0, :] if dil > 1 \
                else t[start:start + 128, :]
        (q_, Bq_), (kc_, Bkc_) = qtm[s], kcm[s]
        (vc_, Bvc_) = vcm[it % 4]
        P.dma("sp", dm(q_, rows(qs, o_start)), writes=[Bq_])
        P.dma("sp", dm(kc_, rows(kscr, c_start)), writes=[Bkc_])
        P.dma("sp", dm(vc_, rows(vscr, c_start)), writes=[Bvc_])
        (qT_, BqT_), (kcT_, BkcT_) = qT[s], kcT[s3]
        tlist = [(q_, Bq_, qT_, BqT_, 0), (kc_, Bkc_, kcT_, BkcT_, 1)]
        if b == 0:
            (kp_, Bkp_), (vp_, Bvp_) = kpm[s], vpm[it % 3]
            (kpT_, BkpT_) = kpT[s]
            P.dma("sp", dm(kp_, rows(kscr, p_start)), writes=[Bkp_])
            P.dma("sp", dm(vp_, rows(vscr, p_start)), writes=[Bvp_])
            tlist.append((kp_, Bkp_, kpT_, BkpT_, 0))
        else:
            (kpT_, BkpT_) = kcT[s3p]
            (vp_, Bvp_) = vcm[(it - 1) % 4]
        for (src, Bsrc, dstT, BdT, pbi) in tlist:
            tv = pbb(pbi).rearrange("p (c t) -> p c t", t=128)
            P.op("pe", trs([(tv[:, h, :], src[:, h * 128:(h + 1) * 128], ident[:]) for h in range(8)]),
                 reads=[Bsrc, Bid], writes=[PB[pbi]])
            P.op("dve", cp(dstT, tv), reads=[PB[pbi]], writes=[BdT])
        return (qT_, BqT_, kcT_, BkcT_, kpT_, BkpT_, vc_, Bvc_, vp_, Bvp_, o_start)

    def S2(it, c_):
        pi, dil, r, b = iters[it]
        (qT_, BqT_, kcT_, BkcT_, kpT_, BkpT_, vc_, Bvc_, vp_, Bvp_, o_start) = c_
        mb_, Bmb_ = (mbB, BmbB) if b == 0 else (mbA, BmbA)
        mbf = mb_[:].rearrange("p a b -> p (a b)")
        for half in range(2):
            (pT_, BpT_) = pT[(it % 2) * 2 + half]
            b0, b1 = (2, 3) if half == 0 else (6, 7)
            specs = []
            for bank in (b0, b1):
                specs.append((pbf(bank), ident[:], mbf, True, False))
                for e2 in range(2):
                    hh = (0 if bank == b0 else 2) + e2
                    h = half * 4 + hh
                    off = e2 * 256
                    specs.append((pbf(bank)[:, off:off + 128], kcT_[:, h, :], qT_[:, h, :], False, False))
                    specs.append((pbf(bank)[:, off + 128:off + 256], kpT_[:, h, :], qT_[:, h, :], False, e2 == 1))
            P.op("pe", mms(specs), reads=[BqT_, BkcT_, BkpT_, Bid, Bmb_], writes=[PB[b0], PB[b1]])
            P.op("act", act(pT_[:, 0:2, :].rearrange("p a b -> p (a b)"), pbf(b0), AF.Exp, scale=SCALE),
                 reads=[PB[b0]], writes=[BpT_])
            P.op("act", act(pT_[:, 2:4, :].rearrange("p a b -> p (a b)"), pbf(b1), AF.Exp, scale=SCALE),
                 reads=[PB[b1]], writes=[BpT_])
        return c_

    def S3(it, c_):
        P.flush()
        pi, dil, r, b = iters[it]
        s = it % 2
        (qT_, BqT_, kcT_, BkcT_, kpT_, BkpT_, vc_, Bvc_, vp_, Bvp_, o_start) = c_
        (res_, Bres_) = resb[s]
        for half in range(2):
            (pT_, BpT_) = pT[(it % 2) * 2 + half]
            v0, v1 = 4, 5
            specs = []
            for hh in range(4):
                h = half * 4 + hh
                bank = v0 if hh < 2 else v1
                off = (hh % 2) * 130
                oreg = pbf(bank)[:, off:off + 130]
                specs.append((oreg, pT_[:, hh, 0:128], vc_[:, h * 130:(h + 1) * 130], True, False))
                specs.append((oreg, pT_[:, hh, 128:256], vp_[:, h * 130:(h + 1) * 130], False, True))
            P.op("pe", mms(specs), reads=[BpT_, Bvc_, Bvp_], writes=[PB[v0], PB[v1]])
            P.op("dve", cp(res_[:, half * 4:half * 4 + 2, :], pbf(v0)[:, 0:260].rearrange("p (a b) -> p a b", b=130)),
                 reads=[PB[v0]], writes=[Bres_])
            P.op("act", acp(res_[:, half * 4 + 2:half * 4 + 4, :], pbf(v1)[:, 0:260].rearrange("p (a b) -> p a b", b=130)),
                 reads=[PB[v1], Bres_], writes=[Bres_])
        dst = rs[0, o_start:o_start + 128 * dil, :].rearrange("(i s) n -> i s n", s=dil)[:, 0, :] if dil > 1 \
            else rs[0, o_start:o_start + 128, :]
        if pi == 0:
            P.defer((lambda dst=dst, res_=res_, Bres_=Bres_:
                     P.dma("sp", dm(dst, res_.rearrange("p a b -> p (a b)")), reads=[Bres_, Brs])))
        else:
            P.defer((lambda dst=dst, res_=res_, Bres_=Bres_:
                     P.dma("pool", (lambda e: e.dma_start(out=dst, in_=res_.rearrange("p a b -> p (a b)"), accum_op=ALU.add)),
                           reads=[Bres_], writes=[Brs], sembuf=Brs)))
    pipeline([S1, S2, S3], len(iters))

    P.barrier()

    Bp3a = P.buf("p3a")
    P.handoff([Bp2] + [b for lst in (qtm, kcm, kpm, vcm, vpm, qT, kcT, kpT, pT, pTm) for (_, b) in lst], [Bp3a])
    mix = view(R1, 0, [128, 4, D], F32); Bmix = [P.buf(f"mix{i}") for i in range(4)]
    xt3 = view(R1, 32768, [128, D], F32); Bxt3 = P.buf("xt3")
    ffT = view(R1, 0, [128, 44, G], BF16); BffT = P.buf("ffT")
    x1r = view(R1, 0, [128, D], F32); Bx1r = P.buf("x1r")
    xnF = view(R1, 8192, [128, D], BF16); BxnF = P.buf("xnF")
    ars = [(view(R2, i * 4160, [128, 8, 130], F32), P.buf(f"ar{i}")) for i in range(2)]
    mTa = view(R2, 8320, [128, 8, G], BF16); BmTa = P.buf("mTa")
    onats = [(view(R2, 16512 + i * 2048, [128, 1024], BF16), P.buf(f"onat{i}")) for i in range(2)]
    Bar_all = [b for (_, b) in ars] + [b for (_, b) in onats]
    xnB = view(R1, 40960, [128, D], BF16); BxnB = P.buf("xnB")
    x3pair = [(xt3, None), (xtb_t[:], Bxtb)]
    Bsta = [P.buf("sta0"), P.buf("sta1")]
    Bstc = [P.buf("stc0"), P.buf("stc1")]
    Bstf = [P.buf("stf0"), P.buf("stf1")]
    h2T = view(R2, 0, [128, 16, G], BF16); Bh2T = P.buf("h2T")
    ffo = view(R2, 0, [128, 4, D], F32); Bffo = [P.buf(f"ffo{i}") for i in range(4)]
    mTh = view(R4, 0, [128, 8, G], BF16); BmTh = P.buf("mTh")
    sgv = [(view(R4, 8192 + i * 2048, [128, G], F32), P.buf(f"sg{i}")) for i in range(2)]
    h2Ts = view(R4, 12288, [128, 16, NS], BF16); Bh2Ts = P.buf("h2Ts")
    ffTs = view(R4, 12416, [128, 44, NS], BF16); BffTs = P.buf("ffTs")
    smix = view(R4, 12800, [128, D], F32); Bsmix = P.buf("smix")
    sffo = view(R4, 20992, [128, D], F32); Bsffo = P.buf("sffo")
    sgs = view(R4, 29184, [128, NS], F32); Bsgs2 = P.buf("sgs")
    grow = view(R3, 0, [128, D], F32); Bgrow = P.buf("grow")
    sxs, Bsxs = xt3, Bxt3
    Bp3c = P.buf("p3c")
    P.handoff(BS + BSbf + [Bel, Bohg, Bsq, Bonb, Bst8, Bqss, Bsgs] + BkdT + BaTm + Bsigs, [Bp3c])
    for b_ in [BmTh, Bh2Ts, BffTs, Bsmix, Bsffo, Bsxs, Bsgs2] + [b for (_, b) in sgv]:
        P.handoff([Bp3c], [b_])
    for b_ in Bmix + [Bxt3]:
        P.handoff([Bp3a], [b_])
    for b_ in [BmTa] + Bar_all:
        P.handoff([b for (_, b) in resb] + [Bp15b], [b_])
    x3pair[0] = (xt3, Bxt3)
    xn_alt[0] = (xnB, BxnB)

    def rms_rows(src, rows, width, col):
        ssq_ = stat[0:rows, col:col + 1]
        P.op("act", act(xn[0:rows, 0:width], src, AF.Square, accum=ssq_), reads=[], writes=[Bxn, Bstat])
        P.op("act", act(ssq_, ssq_, AF.Ln, scale=1.0 / width, bias=EPS), reads=[Bstat], writes=[Bstat])
        P.op("act", act(ssq_, ssq_, AF.Exp, scale=-0.5), reads=[Bstat], writes=[Bstat])
        return ssq_

    for og in range(4):
        last = og == 3
        o0 = og * G
        ntile = 4 + (1 if last else 0)
        P.dma("sp", dm(mTh, mhg[:, :, o0:o0 + G]), writes=[BmTh])
        for tile in range(4):
            t0 = o0 + tile * 128
            par = tile % 2
            (a3, Bar_), (onat, Bonat) = ars[par], onats[par]
            P.dma("sp", dm(a3.rearrange("p h e -> p (h e)"), rs[0, t0:t0 + 128, :]), writes=[Bar_])
            P.op("dve", (lambda a3=a3: lambda e: e.reciprocal(out=a3[:, :, 128:129], in_=a3[:, :, 128:129]))(), reads=[Bar_], writes=[Bar_])
            P.op("dve", tt(a3[:, :, 0:128], a3[:, :, 0:128], a3[:, :, 128:129].to_broadcast([128, 8, 128]), ALU.mult),
                 reads=[Bar_], writes=[Bar_])
            ssq_ = stat[:, 2 + 3 * par:3 + 3 * par]
            xj, Bxj = (xn, Bxn) if par == 0 else (xnB, BxnB)
            P.op("act", act(xj[:, 0:1024].rearrange("p (h d) -> p h d", d=128), a3[:, :, 0:128], AF.Square, accum=ssq_),
                 reads=[Bar_], writes=[Bxj, Bsta[par]])
            P.op("act", act(ssq_, ssq_, AF.Ln, scale=1.0 / 1024, bias=EPS), reads=[Bsta[par]], writes=[Bsta[par]])
            P.op("act", act(ssq_, ssq_, AF.Exp, scale=-0.5), reads=[Bsta[par]], writes=[Bsta[par]])
            P.op("dve", stt(onat.rearrange("p (h d) -> p h d", d=128), a3[:, :, 0:128], ssq_,
                            agrow[:].rearrange("p (h d) -> p h d", d=128), ALU.mult, ALU.mult),
                 reads=[Bar_, Bsta[par], Bag], writes=[Bonat])
            tv = pbb(par).rearrange("p (c t) -> p c t", t=128)
            P.op("pe", trs([(tv[:, h, :], onat[:, h * 128:(h + 1) * 128], ident[:]) for h in range(8)]),
                 reads=[Bonat, Bid], writes=[PB[par]])
            P.op("act", acp(mTa[:, :, tile * 128:(tile + 1) * 128], tv), reads=[PB[par]], writes=[BmTa])
        for dmb in range(4):
            wt, Bw = WQ3.get()
            for tile in range(ntile):
                pbi = (2, 3, 5, 6, 7)[ctr["pb"] % 5]
                ctr["pb"] += 1
                if tile < 4:
                    tsl = slice(tile * 128, tile * 128 + 128)
                    pt = pbf(pbi)
                    specs = [(pt, mTa[:, c, tsl], wt[:, c, :], c == 0, False) for c in range(8)]
                    specs += [(pt, mTh[:, c, tsl], wt[:, 8 + c, :], False, c == 7) for c in range(8)]
                    P.op("pe", mms(specs), reads=[BmTa, BmTh, Bw], writes=[PB[pbi]])
                    P.op("act", acp(mix[:, tile, dmb * 512:(dmb + 1) * 512], pt), reads=[PB[pbi]], writes=[Bmix[tile]])
                else:
                    pt = pbf(pbi)[0:NS, :]
                    P.op("pe", mms([(pt, mTs[:, c, :], wt[:, c, :], c == 0, c == 15) for c in range(16)]),
                         reads=[BmTs, Bw], writes=[PB[pbi]])
                    P.op("act", acp(smix[0:NS, dmb * 512:(dmb + 1) * 512], pt), reads=[PB[pbi]], writes=[Bsmix])
        P.handoff([BmTa] + Bar_all, [Bh2T])
        P.dma("sp", dm(grow[:], g2_d[0].partition_broadcast(128)), writes=[Bgrow])
        def C1(tile, _):
            par = tile % 2
            xdst, Bxd = x3pair[par]
            if tile < 4:
                rows_, src, Bsrc, xsrc = 128, mix[:, tile, :], Bmix[tile], xh[HALO + o0 + tile * 128: HALO + o0 + tile * 128 + 128, :]
            else:
                rows_, src, Bsrc, xsrc = NS, smix[0:NS, :], Bsmix, xs
            P.dma("sp", dm(xdst[0:rows_, :], xsrc), writes=[Bxd])
            ssq_ = stat[0:rows_, 12 + par:13 + par]
            xj, Bxj = (xn, Bxn) if par == 0 else (xnB, BxnB)
            P.op("act", act(xj[0:rows_, :], src, AF.Square, accum=ssq_), reads=[Bsrc], writes=[Bxj, Bstc[par]])
            P.op("act", act(ssq_, ssq_, AF.Ln, scale=1.0 / D, bias=EPS), reads=[Bstc[par]], writes=[Bstc[par]])
            P.op("act", act(ssq_, ssq_, AF.Exp, scale=-0.5), reads=[Bstc[par]], writes=[Bstc[par]])
            P.op("dve", stt(src, src, ssq_, grow[0:rows_, :], ALU.mult, ALU.mult), reads=[Bsrc, Bstc[par], Bgrow], writes=[Bsrc])
            P.op("pool", tt(src, src, xdst[0:rows_, :], ALU.add), reads=[Bsrc, Bxd], writes=[Bsrc])
            return (rows_, src, Bsrc)

        def C2(tile, c_):
            rows_, src, Bsrc = c_
            if tile < 4:
                P.dma("sp", dm(x1scr[o0 + tile * 128:o0 + tile * 128 + 128, :], src), reads=[Bsrc])
                return norm_T_a(src, Bsrc, 128, g3T, Bg3, h2T, Bh2T, tile * 128)
            return norm_T_a(src, Bsrc, NS, g3T, Bg3, h2Ts, Bh2Ts, 0)

        def C3(tile, sb_):
            sb_()
        pipeline([C1, C2, C3], ntile)
        P.handoff(Bmix + [Bxt3, BxnB], [BffT])
        for t in range(NGU):
            wt, Bw = WQ3.get()
            for j in range(2):
                fi = t * 2 + j
                npair = 3 if last else 4
                pg, pu = 2 * (fi % npair), 2 * (fi % npair) + 1
                P.op("pe", mms([(pbf(pg), wt[:, c, j * 128:(j + 1) * 128], h2T[:, c, :], c == 0, c == 15) for c in range(16)]),
                     reads=[Bh2T, Bw], writes=[PB[pg]])
                P.op("pe", mms([(pbf(pu), wt[:, c, 256 + j * 128:256 + (j + 1) * 128], h2T[:, c, :], c == 0, c == 15) for c in range(16)]),
                     reads=[Bh2T, Bw], writes=[PB[pu]])
                sg_, Bsg_ = sgv[fi % 2]
                P.op("act", act(sg_, pbf(pg), AF.Silu), reads=[PB[pg]], writes=[Bsg_])
                P.op("dve", tt(ffT[:, fi, :], sg_, pbf(pu), ALU.mult), reads=[Bsg_, PB[pu]], writes=[BffT])
                if last:
                    P.op("pe", mms([(pbf(6)[:, 0:NS], wt[:, c, j * 128:(j + 1) * 128], h2Ts[:, c, :], c == 0, c == 15) for c in range(16)]
                                   + [(pbf(6)[:, 8:8 + NS], wt[:, c, 256 + j * 128:256 + (j + 1) * 128], h2Ts[:, c, :], c == 0, c == 15) for c in range(16)]),
                         reads=[Bh2Ts, Bw], writes=[PB[6]])
                    P.op("act", act(sgs, pbf(6)[:, 0:NS], AF.Silu), reads=[PB[6]], writes=[Bsgs2])
                    P.op("dve", tt(ffTs[:, fi, :], sgs, pbf(6)[:, 8:8 + NS], ALU.mult), reads=[Bsgs2, PB[6]], writes=[BffTs])
        for b_ in Bffo:
            P.handoff([Bh2T], [b_])

        for si, (dmb, kq) in enumerate(dseq):
            wt, Bw = WQ3.get()
            for tile in range(ntile):
                if tile < 4:
                    bk = tile if (last or dmb % 2 == 0) else 4 + tile
                    pt = pbf(bk)
                    tsl = slice(tile * 128, tile * 128 + 128)
                    P.op("pe", mms([(pt, ffT[:, kq * 11 + c, tsl], wt[:, c, :], kq == 0 and c == 0, kq == 3 and c == 10) for c in range(11)]),
                         reads=[BffT, Bw], writes=[PB[bk]])
                    if kq == 3:
                        P.op("act" if tile % 2 == 0 else "dve",
                             (acp if tile % 2 == 0 else cp)(ffo[:, tile, dmb * 512:(dmb + 1) * 512], pt),
                             reads=[PB[bk]], writes=[Bffo[tile]])
                else:
                    pt = pbf(4)[0:NS, :]
                    P.op("pe", mms([(pt, ffTs[:, kq * 11 + c, :], wt[:, c, :], kq == 0 and c == 0, kq == 3 and c == 10) for c in range(11)]),
                         reads=[BffTs, Bw], writes=[PB[4]])
                    if kq == 3:
                        P.op("act", acp(sffo[0:NS, dmb * 512:(dmb + 1) * 512], pt), reads=[PB[4]], writes=[Bsffo])
        P.handoff([BffT], [Bx1r, BxnF])
        P.dma("sp", dm(grow[:], g4_d[0].partition_broadcast(128)), writes=[Bgrow])
        x1pair = [(x1r, Bx1r), (xtb_t[:], Bxtb)]

        def F1(tile, _):
            par = tile % 2
            if tile < 4:
                rows_, src, Bsrc = 128, ffo[:, tile, :], Bffo[tile]
                x1v, Bx1 = x1pair[par]
                P.dma("sp", dm(x1v, x1scr[o0 + tile * 128:o0 + tile * 128 + 128, :]), writes=[Bx1])
                dst = y[o0 + tile * 128:o0 + tile * 128 + 128, :]
            else:
                rows_, src, Bsrc = NS, sffo[0:NS, :], Bsffo
                x1v, Bx1 = smix, Bsmix
                dst = ys
            ssq_ = stat[0:rows_, 14 + par:15 + par]
            xj, Bxj = (xn, Bxn) if par == 0 else (xnF, BxnF)
            P.op("act", act(xj[0:rows_, :], src, AF.Square, accum=ssq_), reads=[Bsrc], writes=[Bxj, Bstf[par]])
            P.op("act", act(ssq_, ssq_, AF.Ln, scale=1.0 / D, bias=EPS), reads=[Bstf[par]], writes=[Bstf[par]])
            P.op("act", act(ssq_, ssq_, AF.Exp, scale=-0.5), reads=[Bstf[par]], writes=[Bstf[par]])
            return (rows_, src, Bsrc, x1v, Bx1, dst, ssq_, par)

        def F2(tile, c_):
            rows_, src, Bsrc, x1v, Bx1, dst, ssq_, par = c_
            P.op("dve", stt(src, src, ssq_, grow[0:rows_, :], ALU.mult, ALU.mult), reads=[Bsrc, Bstf[par], Bgrow], writes=[Bsrc])
            P.op("pool", tt(src, src, x1v[0:rows_, :], ALU.add), reads=[Bsrc, Bx1], writes=[Bsrc])
            P.dma("sp", dm(dst, src), reads=[Bsrc])
        pipeline([F1, F2], ntile)
        for b_ in [BmTa] + Bar_all:
            P.handoff(Bffo, [b_])
        for b_ in Bmix + [Bxt3, BxnB]:
            P.handoff([Bx1r, BxnF], [b_])

    P.barrier()
    P.emit(st)
    st.close()
    return nc


_NC = None


def _consts(core):
    i = np.arange(128)
    mcur = (i[:, None] <= i[None, :]).astype(np.float32)
    mprev = (i[:, None] >= i[None, :]).astype(np.float32)
    bdm = ((i[:, None] <= i[None, :]) & ((i[:, None] // 64) == (i[None, :] // 64))).astype(np.float32)
    oneh = np.zeros((128, NS, NS), np.float32)
    for j in range(NS):
        oneh[:, j, j] = 1.0
    hv = np.full((128, 1), 0.0 if core == 0 else 1.0, np.float32)
    return dict(ident=np.eye(128, dtype=np.float32), mcur=mcur, mprev=mprev, bdm=bdm, oneh=oneh, hv=hv)


def kernel(x_prompt, x_sample, cache_win_k, cache_win_v, state_hgrn, norm_pre_mix, w_in,
           hg_lb_logits, attn_out_gain, hg_norm_gain, w_out, norm_post_mix, norm_pre_ffn,
           w_gate, w_up, w_down, norm_post_ffn):
    global _NC
    f = lambda a: np.ascontiguousarray(np.asarray(a, dtype=np.float32))
    xp = f(x_prompt)[0]
    xsm = f(x_sample)[:, 0, :]
    ckv = f(cache_win_k)[0].reshape(32, 2048, 1024)
    cvv = f(cache_win_v)[0].reshape(32, 2048, 1024)
    sth = f(state_hgrn)[0]
    shared = dict(
        w_in=f(w_in)[0], w_out=f(w_out)[0], w_gate=f(w_gate)[0], w_up=f(w_up)[0], w_down=f(w_down)[0],
        g1T=f(f(norm_pre_mix)[0].reshape(16, 128).T), g3T=f(f(norm_pre_ffn)[0].reshape(16, 128).T),
        g2=f(norm_post_mix), g4=f(norm_post_ffn), ag=f(attn_out_gain), hg=f(hg_norm_gain),
        lbl=f(f(hg_lb_logits).reshape(2, 8, 128).transpose(2, 0, 1)),
    )
    in_maps = []
    for c in range(NCORE):
        xhc = np.zeros((HALO + OWN, D), np.float32)
        if c > 0:
            xhc[0:HALO] = xp[(c - 1) * OWN:c * OWN]
        xhc[HALO:] = xp[c * OWN:(c + 1) * OWN]
        m = dict(shared)
        m.update(_consts(c))
        m.update(xh=xhc, xs=f(xsm[c * NS:(c + 1) * NS]), ck=f(ckv[c * NS:(c + 1) * NS]),
                 cv=f(cvv[c * NS:(c + 1) * NS]), st_in=f(sth[c * NS:(c + 1) * NS]))
        in_maps.append(m)
    if _NC is None:
        _NC = build()
    res = run_bass_kernel_spmd(_NC, in_maps, core_ids=list(range(NCORE)))
    R = res.results
    y_prompt = np.concatenate([R[c]["y"] for c in range(NCORE)], axis=0)[None]
    y_sample = np.concatenate([R[c]["ys"] for c in range(NCORE)], axis=0)[:, None, :]
    win_k = R[NCORE - 1]["wk"].reshape(1, 1, 2048, 8, 128)
    win_v = R[NCORE - 1]["wv"].reshape(1, 1, 2048, 8, 128)
    state_p = R[NCORE - 1]["sp_out"].reshape(1, 1, 8, 128, 128)
    ks = np.concatenate([R[c]["ks_o"] for c in range(NCORE)], axis=0).reshape(1, 32, 1, 8, 128)
    vs = np.concatenate([R[c]["vs_o"] for c in range(NCORE)], axis=0).reshape(1, 32, 1, 8, 128)
    ss = np.concatenate([R[c]["ss_o"] for c in range(NCORE)], axis=0).reshape(1, 32, 8, 128, 128)
    outs = (y_prompt, y_sample, win_k, win_v, state_p, ks, vs, ss)
    return tuple(np.ascontiguousarray(o, dtype=np.float32) for o in outs)
```
